# Optimizing a Trainium2 kernel written in Bass

```python
import jax, jax.numpy as jnp
from jax import lax
import numpy as np


D_MODEL = 1024
BATCH = 4
SEQ = 8192
DEPTH = 2

GRID_W = 64
CTX_LEN = 256
N_MIXERS = 2
NORM_EPS = 1e-6
ATTN_HEADS = 16
ATTN_KV_HEADS = 4
HEAD_DIM = D_MODEL // ATTN_HEADS
ATTN_GROUP = ATTN_HEADS // ATTN_KV_HEADS
Q_WIDTH = ATTN_HEADS * HEAD_DIM
KV_WIDTH = ATTN_KV_HEADS * HEAD_DIM
QKV_WIDTH = Q_WIDTH + 2 * KV_WIDTH
AXIS_DIM = HEAD_DIM // 2
ROPE_THETA = 10000.0
Q_BLOCK = 128
HGRN_WIDTH = D_MODEL
HGRN_EXPAND = 128
HGRN_HEADS = HGRN_WIDTH // HGRN_EXPAND
HGRN_DK = HGRN_EXPAND
HGRN_DV = HGRN_WIDTH // HGRN_HEADS
HGRN_CHUNK = 64
FFN_HIDDEN = ((8 * D_MODEL // 3 + 255) // 256) * 256
N_ATTN_LAYERS = (DEPTH + 1) // 2
N_HGRN_LAYERS = DEPTH // 2

kernel_name = 'hybrid_gqa_hgrn2_adaln_prefix_dit'


def rms_norm(x, w):
    xf = x.astype(jnp.float32)
    y = xf * lax.rsqrt(jnp.mean(xf * xf, axis=-1, keepdims=True) + NORM_EPS)
    return (y * w.astype(jnp.float32)).astype(x.dtype)


def modulate(h, shift, scale):
    return h * (1.0 + scale) + shift


def axial_rope_tables(n_tokens):
    rows = n_tokens // GRID_W
    row_pos = jnp.repeat(jnp.arange(rows), GRID_W).astype(jnp.float32)
    col_pos = jnp.tile(jnp.arange(GRID_W), rows).astype(jnp.float32)
    inv_freq = ROPE_THETA ** (-(jnp.arange(AXIS_DIM // 2, dtype=jnp.float32) * 2.0 / AXIS_DIM))
    ang_r = row_pos[:, None] * inv_freq
    ang_c = col_pos[:, None] * inv_freq
    return (jnp.cos(ang_r), jnp.sin(ang_r), jnp.cos(ang_c), jnp.sin(ang_c))


def rotate_half(x, cos, sin):
    x1, x2 = jnp.split(x, 2, axis=-1)
    return jnp.concatenate([x1 * cos - x2 * sin, x2 * cos + x1 * sin], axis=-1)


def apply_axial_rope(x, rope):
    cos_r, sin_r, cos_c, sin_c = rope
    xf = x.astype(jnp.float32)
    xr, xc = jnp.split(xf, 2, axis=-1)
    out = jnp.concatenate([rotate_half(xr, cos_r, sin_r), rotate_half(xc, cos_c, sin_c)], axis=-1)
    return out.astype(x.dtype)


def gqa_attend(q, k, v):
    s = jnp.einsum('bkgqd,bksd->bkgqs', q, k, preferred_element_type=jnp.float32) * (HEAD_DIM ** -0.5)
    p = jax.nn.softmax(s, axis=-1).astype(v.dtype)
    return jnp.einsum('bkgqs,bksd->bkgqd', p, v)


def attention_mixer(h_lat, h_ctx, w_qkv, q_norm, k_norm, w_o, rope, with_ctx_out):
    b, n, _ = h_lat.shape
    m = h_ctx.shape[1]
    w_q, w_kv = w_qkv[:, :Q_WIDTH], w_qkv[:, Q_WIDTH:]

    def queries(h):
        t = h.shape[1]
        q = (h @ w_q).reshape(b, t, ATTN_KV_HEADS, ATTN_GROUP, HEAD_DIM).transpose(0, 2, 3, 1, 4)
        return rms_norm(q, q_norm)

    def keys_values(h):
        t = h.shape[1]
        k, v = jnp.split(h @ w_kv, 2, axis=-1)
        k = k.reshape(b, t, ATTN_KV_HEADS, HEAD_DIM).transpose(0, 2, 1, 3)
        v = v.reshape(b, t, ATTN_KV_HEADS, HEAD_DIM).transpose(0, 2, 1, 3)
        return rms_norm(k, k_norm), v

    q_l = apply_axial_rope(queries(h_lat), rope)
    k_l, v_l = keys_values(h_lat)
    k_l = apply_axial_rope(k_l, rope)
    k_c, v_c = keys_values(h_ctx)
    k_all = jnp.concatenate([k_c, k_l], axis=2)
    v_all = jnp.concatenate([v_c, v_l], axis=2)

    nb = n // Q_BLOCK
    q_blocks = q_l.reshape(b, ATTN_KV_HEADS, ATTN_GROUP, nb, Q_BLOCK, HEAD_DIM).transpose(3, 0, 1, 2, 4, 5)
    o_blocks = lax.map(lambda qb: gqa_attend(qb, k_all, v_all), q_blocks)
    o_l = o_blocks.transpose(1, 0, 4, 2, 3, 5).reshape(b, n, Q_WIDTH)
    y_l = o_l @ w_o
    if not with_ctx_out:
        return y_l, None
    o_c = gqa_attend(queries(h_ctx), k_c, v_c)
    y_c = o_c.transpose(0, 3, 1, 2, 4).reshape(b, m, Q_WIDTH) @ w_o
    return y_l, y_c


def hgrn2_chunk_scan(q, k, log_f, v, s0):
    b, h, n, dk = q.shape
    dv = v.shape[-1]
    nc = n // HGRN_CHUNK

    def to_chunks(a):
        return a.reshape(b, h, nc, HGRN_CHUNK, a.shape[-1]).transpose(2, 0, 1, 3, 4)

    mask = jnp.tril(jnp.ones((HGRN_CHUNK, HGRN_CHUNK), dtype=bool))[None, None, :, :, None]

    def step(state, inp):
        qc, kc, gc, vc = inp
        L = jnp.cumsum(gc, axis=2)
        o_inter = jnp.einsum('bhtk,bhkv->bhtv', qc * jnp.exp(L), state)
        diff = L[:, :, :, None, :] - L[:, :, None, :, :]
        decay = jnp.where(mask, jnp.exp(jnp.where(mask, diff, 0.0)), 0.0)
        a = jnp.einsum('bhtk,bhtsk,bhsk->bhts', qc, decay, kc)
        o_intra = jnp.einsum('bhts,bhsv->bhtv', a, vc)
        L_end = L[:, :, -1:, :]
        new_state = jnp.exp(L_end[:, :, 0, :])[..., None] * state + jnp.einsum(
            'bhsk,bhsv->bhkv', kc * jnp.exp(L_end - L), vc)
        return new_state, o_inter + o_intra

    s_final, o = lax.scan(step, s0, (to_chunks(q), to_chunks(k), to_chunks(log_f), to_chunks(v)))
    return o.transpose(1, 2, 0, 3, 4).reshape(b, h, n, dv), s_final


def hgrn2_final_state(k, log_f, v):
    L = jnp.cumsum(log_f, axis=2)
    return jnp.einsum('bhsk,bhsv->bhkv', k * jnp.exp(L[:, :, -1:, :] - L), v)


def hgrn2_mixer(h_lat, h_ctx, w_in, lb, out_norm, w_o, with_ctx_out):
    b = h_lat.shape[0]

    def heads(a):
        return a.reshape(b, a.shape[1], HGRN_HEADS, -1).transpose(0, 2, 1, 3).astype(jnp.float32)

    def gates(z):
        f = lb + (1.0 - lb) * jax.nn.sigmoid(z.astype(jnp.float32))
        return heads(jnp.log(f)), heads(1.0 - f)

    def recurrent_inputs(h):
        z_fw, z_bw, v = jnp.split(h @ w_in[:, 2 * HGRN_WIDTH:], 3, axis=-1)
        g_fw, k_fw = gates(z_fw)
        g_bw, k_bw = gates(z_bw)
        return g_fw, k_fw, g_bw, k_bw, heads(v)

    def query_gate(h):
        q, g = jnp.split(h @ w_in[:, :2 * HGRN_WIDTH], 2, axis=-1)
        return heads(q), g

    def flip(a):
        return jnp.flip(a, axis=2)

    def readout(o, g):
        t = o.shape[2]
        o = rms_norm(o.transpose(0, 2, 1, 3), out_norm.reshape(HGRN_HEADS, HGRN_DV))
        o = o * jax.nn.sigmoid(g.astype(jnp.float32)).reshape(b, t, HGRN_HEADS, HGRN_DV)
        return o.reshape(b, t, HGRN_WIDTH).astype(h_lat.dtype) @ w_o

    zeros = jnp.zeros((b, HGRN_HEADS, HGRN_DK, HGRN_DV), jnp.float32)
    gc_fw, kc_fw, gc_bw, kc_bw, vc = recurrent_inputs(h_ctx)
    y_c = None
    if with_ctx_out:
        qc, gate_c = query_gate(h_ctx)
        oc_fw, s_fw = hgrn2_chunk_scan(qc, kc_fw, gc_fw, vc, zeros)
        oc_bw, s_bw = hgrn2_chunk_scan(flip(qc), flip(kc_bw), flip(gc_bw), flip(vc), zeros)
        y_c = readout(oc_fw + flip(oc_bw), gate_c)
    else:
        s_fw = hgrn2_final_state(kc_fw, gc_fw, vc)
        s_bw = hgrn2_final_state(flip(kc_bw), flip(gc_bw), flip(vc))

    q, gate = query_gate(h_lat)
    g_fw, k_fw, g_bw, k_bw, v = recurrent_inputs(h_lat)
    o_fw, _ = hgrn2_chunk_scan(q, k_fw, g_fw, v, s_fw)
    o_bw, _ = hgrn2_chunk_scan(flip(q), flip(k_bw), flip(g_bw), flip(v), s_bw)
    return readout(o_fw + flip(o_bw), gate), y_c


def swiglu(h, w_in, w_out):
    a, u = jnp.split(h @ w_in, 2, axis=-1)
    return (jax.nn.silu(a) * u) @ w_out


def setup_inputs(seed: int = 0) -> dict:
    key = jax.random.key(seed)
    ks = jax.random.split(key, 19)
    D = D_MODEL

    def nrm(k, shape, scale):
        return jax.random.normal(k, shape, jnp.float32) * scale

    return {
        'x': nrm(ks[0], (BATCH, SEQ, D), 1.0),
        'c': nrm(ks[1], (BATCH, D), 1.0),
        'ctx': nrm(ks[2], (BATCH, CTX_LEN, D), 1.0),
        'c_ctx': nrm(ks[3], (D,), 1.0),
        'ada_w': nrm(ks[4], (DEPTH, D, 6 * D), 0.5 * D ** -0.5),
        'ada_b': nrm(ks[5], (DEPTH, 6 * D), 0.02),
        'norm_mix_w': 1.0 + nrm(ks[6], (DEPTH, D), 0.05),
        'norm_ffn_w': 1.0 + nrm(ks[7], (DEPTH, D), 0.05),
        'attn_w_qkv': nrm(ks[8], (N_ATTN_LAYERS, D, QKV_WIDTH), D ** -0.5),
        'attn_q_norm': 1.0 + nrm(ks[9], (N_ATTN_LAYERS, HEAD_DIM), 0.05),
        'attn_k_norm': 1.0 + nrm(ks[10], (N_ATTN_LAYERS, HEAD_DIM), 0.05),
        'attn_w_o': nrm(ks[11], (N_ATTN_LAYERS, Q_WIDTH, D), Q_WIDTH ** -0.5),
        'hgrn_w_in': nrm(ks[12], (N_HGRN_LAYERS, D, 5 * HGRN_WIDTH), D ** -0.5),
        'hgrn_lb_logits': nrm(ks[13], (DEPTH, HGRN_WIDTH), 0.5),
        'hgrn_out_norm': 1.0 + nrm(ks[14], (N_HGRN_LAYERS, HGRN_WIDTH), 0.05),
        'hgrn_w_o': nrm(ks[15], (N_HGRN_LAYERS, HGRN_WIDTH, D), HGRN_WIDTH ** -0.5),
        'ffn_w_in': nrm(ks[16], (DEPTH, D, 2 * FFN_HIDDEN), D ** -0.5),
        'ffn_w_out': nrm(ks[17], (DEPTH, FFN_HIDDEN, D), FFN_HIDDEN ** -0.5),
        'final_norm_w': 1.0 + nrm(ks[18], (D,), 0.05),
    }


def reference(x, c, ctx, c_ctx, ada_w, ada_b, norm_mix_w, norm_ffn_w, attn_w_qkv, attn_q_norm,
              attn_k_norm, attn_w_o, hgrn_w_in, hgrn_lb_logits, hgrn_out_norm, hgrn_w_o,
              ffn_w_in, ffn_w_out, final_norm_w):
    n = x.shape[1]
    rope = axial_rope_tables(n)
    p = jax.nn.softmax(hgrn_lb_logits.astype(jnp.float32), axis=0)
    lower_bounds = jnp.cumsum(p, axis=0) - p[0:1]

    h, hc = x, ctx
    for i in range(DEPTH):
        last = i == DEPTH - 1
        mod_l = (jax.nn.silu(c) @ ada_w[i] + ada_b[i])[:, None, :]
        mod_c = (jax.nn.silu(c_ctx) @ ada_w[i] + ada_b[i])[None, None, :]
        sh1_l, sc1_l, gt1_l, sh2_l, sc2_l, gt2_l = jnp.split(mod_l, 6, axis=-1)
        sh1_c, sc1_c, gt1_c, sh2_c, sc2_c, gt2_c = jnp.split(mod_c, 6, axis=-1)

        hn_l = modulate(rms_norm(h, norm_mix_w[i]), sh1_l, sc1_l)
        hn_c = modulate(rms_norm(hc, norm_mix_w[i]), sh1_c, sc1_c)
        if i % N_MIXERS == 0:
            j = i // N_MIXERS
            y_l, y_c = attention_mixer(hn_l, hn_c, attn_w_qkv[j], attn_q_norm[j], attn_k_norm[j],
                                       attn_w_o[j], rope, not last)
        else:
            j = i // N_MIXERS
            y_l, y_c = hgrn2_mixer(hn_l, hn_c, hgrn_w_in[j], lower_bounds[i], hgrn_out_norm[j],
                                   hgrn_w_o[j], not last)

        h = h + gt1_l * y_l
        h = h + gt2_l * swiglu(modulate(rms_norm(h, norm_ffn_w[i]), sh2_l, sc2_l), ffn_w_in[i], ffn_w_out[i])
        if not last:
            hc = hc + gt1_c * y_c
            hc = hc + gt2_c * swiglu(modulate(rms_norm(hc, norm_ffn_w[i]), sh2_c, sc2_c),
                                     ffn_w_in[i], ffn_w_out[i])
    return rms_norm(h, final_norm_w)
```

```python
import os
import numpy as np
import concourse.bass as bass
import concourse.mybir as mybir
from concourse.bass_utils import run_bass_kernel_spmd

F32 = mybir.dt.float32
BF16 = mybir.dt.bfloat16
AF = mybir.ActivationFunctionType
ALU = mybir.AluOpType

D = 1024
SEQ = 8192
HALF = 4096
CTX = 256
NB = 4
FF = 2816
EPS = 1e-6
HEAD_ORDER = [0, 4, 1, 5, 2, 6, 3, 7, 8, 12, 9, 13, 10, 14, 11, 15]
ENGS = ["pe", "act", "dve", "pool", "sp"]
DMAQ = ("sp", "pool")
NDS = 8


class V:
    __slots__ = ("ap", "key", "extra")

    def __init__(self, ap, key, extra=()):
        self.ap = ap
        self.key = key
        self.extra = extra

    def bc(self, shape):
        return V(self.ap.to_broadcast(list(shape)), self.key)

    def re(self, pat, **kw):
        return V(self.ap.rearrange(pat, **kw), self.key)


class Tile:
    def __init__(self, h, tid, sub=None):
        self.h = h
        self.tid = tid
        self.sub = sub

    def __getitem__(self, idx):
        return V(self.h[idx], (self.tid, self.sub))

    def s(self, sub):
        return Tile(self.h, self.tid, sub)

    def view(self, ap, sub=None):
        return Tile(ap, self.tid, sub)


class Op:
    __slots__ = ("fn", "deps", "sig", "dma", "dsem", "dval", "cnt", "cc")

    def __init__(self, fn, deps, dma, cc=False):
        self.fn = fn
        self.deps = deps
        self.sig = False
        self.dma = dma
        self.dsem = None
        self.dval = 0
        self.cnt = 0
        self.cc = cc


class Prog:
    def __init__(self, nc):
        self.nc = nc
        self.ops = {e: [] for e in ENGS}
        self.last_w = {}
        self.readers = {}
        self.subs = {}
        self.ntid = 0
        self.ndma = {q: [] for q in DMAQ}
        self.dma_since_barrier = []
        self.last_real = {}
        self.excl = set()

    def newtid(self):
        self.ntid += 1
        return self.ntid

    def _conf(self, key):
        tid, sub = key
        ss = self.subs.setdefault(tid, set())
        ss.add(sub)
        if sub is None:
            return [(tid, s) for s in ss]
        return [(tid, sub), (tid, None)] if None in ss else [(tid, sub)]

    def add(self, eng, fn, reads=(), writes=(), dma=False, cc=False):
        idx = len(self.ops[eng])
        deps = set()
        xr = [k for k in reads if k[0] in self.excl]
        if xr:
            reads = [k for k in reads if k[0] not in self.excl]
            writes = list(writes) + xr
        for k in reads:
            for ck in self._conf(k):
                w = self.last_w.get(ck)
                if w is not None:
                    deps.add(w)
        for k in writes:
            for ck in self._conf(k):
                w = self.last_w.get(ck)
                if w is not None:
                    deps.add(w)
                for r in self.readers.get(ck, ()):
                    deps.add(r)
        if cc:
            self.dma_since_barrier.append((eng, idx))
        elif dma:
            lst = self.ndma[eng]
            if len(lst) >= NDS:
                deps.add((eng, lst[len(lst) - NDS]))
            lst.append(idx)
            self.dma_since_barrier.append((eng, idx))
        deps.discard((eng, idx))
        if eng == "pe":
            deps = {d for d in deps if d[0] != "pe"}
        op = Op(fn, deps, dma or cc, cc)
        self.ops[eng].append(op)
        self.last_real[eng] = idx
        me = (eng, idx)
        for k in writes:
            if k[1] is None:
                for ck in self._conf(k):
                    self.last_w[ck] = me
                    self.readers[ck] = []
            else:
                self.last_w[k] = me
                self.readers[k] = []
        for k in reads:
            rl = self.readers.setdefault(k, [])
            if not dma:
                rl[:] = [r for r in rl if r[0] != eng or self.ops[r[0]][r[1]].dma]
            rl.append(me)
        return me

    def barrier(self):
        lasts = [(e, i) for e, i in self.last_real.items()]
        dmas = list(self.dma_since_barrier)
        self.dma_since_barrier = []
        for e in ENGS:
            deps = set(lasts) | set(dmas)
            op = Op(None, deps, False)
            self.ops[e].append(op)

    def mm(self, out, lhsT, rhs, start=True, stop=True):
        self.add("pe", lambda e: e.matmul(out.ap, lhsT=lhsT.ap, rhs=rhs.ap, start=start, stop=stop),
                 reads=[lhsT.key, rhs.key], writes=[out.key])

    def tr(self, out, in_, ident):
        self.add("pe", lambda e: e.transpose(out.ap, in_.ap, ident.ap), reads=[in_.key, ident.key], writes=[out.key])

    def act(self, out, in_, func, scale=1.0, bias=0.0):
        reads = [in_.key] + list(in_.extra)
        sc = scale
        bi = bias
        if isinstance(scale, V):
            reads.append(scale.key)
            sc = scale.ap
        if isinstance(bias, V):
            reads.append(bias.key)
            bi = bias.ap
        self.add("act", lambda e: e.activation(out=out.ap, in_=in_.ap, func=func, bias=bi, scale=sc),
                 reads=reads, writes=[out.key])

    def copy(self, eng, out, in_):
        if eng == "act":
            self.add("act", lambda e: e.copy(out=out.ap, in_=in_.ap), reads=[in_.key], writes=[out.key])
        else:
            self.add(eng, lambda e: e.tensor_copy(out=out.ap, in_=in_.ap), reads=[in_.key], writes=[out.key])

    def tt(self, eng, out, in0, in1, op):
        self.add(eng, lambda e: e.tensor_tensor(out=out.ap, in0=in0.ap, in1=in1.ap, op=op),
                 reads=[in0.key, in1.key], writes=[out.key])

    def ts(self, eng, out, in0, s1, op0, s2=None, op1=None):
        reads = [in0.key]
        a1 = s1
        a2 = s2
        if isinstance(s1, V):
            reads.append(s1.key)
            a1 = s1.ap
        if isinstance(s2, V):
            reads.append(s2.key)
            a2 = s2.ap
        if op1 is None:
            self.add(eng, lambda e: e.tensor_scalar(out=out.ap, in0=in0.ap, scalar1=a1, scalar2=None, op0=op0),
                     reads=reads, writes=[out.key])
        else:
            self.add(eng, lambda e: e.tensor_scalar(out=out.ap, in0=in0.ap, scalar1=a1, scalar2=a2, op0=op0, op1=op1),
                     reads=reads, writes=[out.key])

    def stt(self, out, in0, scalar, in1, op0, op1):
        reads = [in0.key, in1.key]
        sc = scalar
        if isinstance(scalar, V):
            reads.append(scalar.key)
            sc = scalar.ap
        self.add("dve", lambda e: e.scalar_tensor_tensor(out=out.ap, in0=in0.ap, scalar=sc, in1=in1.ap, op0=op0, op1=op1),
                 reads=reads, writes=[out.key])

    def scan(self, out, d0, d1, initial, op0, op1):
        self.add("dve", lambda e: e.tensor_tensor_scan(out=out.ap, data0=d0.ap, data1=d1.ap, initial=initial, op0=op0, op1=op1),
                 reads=[d0.key, d1.key], writes=[out.key])

    def recip(self, out, in_):
        self.add("dve", lambda e: e.reciprocal(out=out.ap, in_=in_.ap), reads=[in_.key], writes=[out.key])

    def shuf(self, out, in_, mask):
        self.add("dve", lambda e: e.stream_shuffle(out=out.ap, in_=in_.ap, mask=mask), reads=[in_.key], writes=[out.key])

    def memset(self, eng, out, val):
        self.add(eng, lambda e: e.memset(out.ap, val), writes=[out.key])

    def dma(self, q, out, in_):
        if q == "pool":
            self.add(q, lambda e: e.dma_start(out=out.ap, in_=in_.ap, max_dma_last_dim=4096), reads=[in_.key], writes=[out.key], dma=True)
        else:
            self.add(q, lambda e: e.dma_start(out=out.ap, in_=in_.ap), reads=[in_.key], writes=[out.key], dma=True)

    def emit(self, extra_ctx=None):
        nc = self.nc
        ops = self.ops
        for e in ENGS:
            for op in ops[e]:
                for (e2, i2) in op.deps:
                    ops[e2][i2].sig = True
        for e in ENGS:
            c = 0
            for op in ops[e]:
                if op.sig and not op.dma:
                    c += 1
                op.cnt = c
        from contextlib import ExitStack
        with ExitStack() as es:
            sems = {e: es.enter_context(nc.semaphore("s_" + e)) for e in ENGS}
            dsems = {q: [es.enter_context(nc.semaphore("d_%s%d" % (q, j))) for j in range(NDS)] for q in DMAQ}
            for q in DMAQ:
                for n, idx in enumerate(self.ndma[q]):
                    op = ops[q][idx]
                    op.dsem = dsems[q][n % NDS]
                    op.dval = 16 * (n // NDS + 1)
            ncc = 0
            for e in ENGS:
                for op in ops[e]:
                    if op.cc:
                        op.dsem = es.enter_context(nc.semaphore("ccs%d" % ncc))
                        ncc += 1
                        op.dval = 1
            block = es.enter_context(nc.Block())
            self.nwaits = 0

            def run(ename, eng):
                known = {}
                for idx, op in enumerate(ops[ename]):
                    need = {}
                    for (e2, i2) in op.deps:
                        o2 = ops[e2][i2]
                        if o2.dma:
                            sem, val = o2.dsem, o2.dval
                        else:
                            sem, val = sems[e2], o2.cnt
                        k = id(sem)
                        if known.get(k, 0) >= val:
                            continue
                        if k not in need or need[k][1] < val:
                            need[k] = (sem, val)
                    for k, (sem, val) in need.items():
                        eng.wait_ge(sem, val)
                        known[k] = val
                        self.nwaits += 1
                    if op.fn is None:
                        assert not op.sig
                        continue
                    ins = op.fn(eng)
                    if op.cc:
                        ins.then_inc(op.dsem)
                    elif op.dma:
                        ins.then_inc(op.dsem, 16)
                    elif op.sig:
                        ins.then_inc(sems[ename], 1)

            @block.tensor
            def _(t):
                run("pe", t)

            @block.scalar
            def _(a):
                run("act", a)

            @block.vector
            def _(v):
                run("dve", v)

            @block.gpsimd
            def _(g):
                run("pool", g)

            @block.sync
            def _(s):
                run("sp", s)


class Arena:
    def __init__(self, nc, P, nbytes):
        self.P = P
        self.nbytes = nbytes
        self.h = nc.alloc_sbuf_tensor("arena", [128, nbytes // 4], F32)
        self.off = 0

    def alloc(self, shape, dtype, parts=128):
        n = 1
        for x in shape:
            n *= x
        esz = 4 if dtype == F32 else 2
        nb = (n * esz + 63) // 64 * 64
        assert self.off + nb <= self.nbytes, ("SBUF arena overflow", self.off, nb, self.nbytes)
        w0 = self.off // 4
        self.off += nb
        ap = self.h[0:parts, w0:w0 + nb // 4]
        if dtype != F32:
            ap = ap.bitcast(dtype)
        ap = ap[:, 0:n]
        if len(shape) == 2:
            ap = ap.rearrange("p (a b) -> p a b", a=shape[0])
        elif len(shape) == 3:
            ap = ap.rearrange("p (a b c) -> p a b c", a=shape[0], b=shape[1])
        elif len(shape) == 4:
            ap = ap.rearrange("p (a b c d) -> p a b c d", a=shape[0], b=shape[1], c=shape[2])
        return Tile(ap, self.P.newtid())

    def mark(self):
        return self.off

    def release(self, m):
        self.P.barrier()
        self.off = m


def build_program(PHASES=("A", "B", "F0", "H", "F1"), dbg_fn=None):
    nc = bass.Bass("TRN2", target_bir_lowering=False)
    P = Prog(nc)

    def din(name, shape, dt=F32):
        return Tile(nc.dram_tensor(name, list(shape), dt, kind="ExternalInput").ap(), P.newtid())

    def dscr(name, shape, dt=F32):
        if os.environ.get("K_DBG") == "1":
            return Tile(nc.dram_tensor(name, list(shape), dt, kind="ExternalOutput").ap(), P.newtid())
        return Tile(nc.dram_tensor(name, list(shape), dt), P.newtid())

    xo = din("xo", [HALF, D])
    cx = din("cx", [CTX, D])
    cvec = din("cvec", [128, 8, 2])
    rope_o = din("rope_o", [2, 128, HALF])
    pimat_d = din("pimat", [128, 128])
    ada_w = din("ada_w", [2, D, 6 * D])
    ada_b = din("ada_b", [128, 2, 48])
    nmw_d = din("nmw", [128, 2, 8])
    nfw_d = din("nfw", [128, 2, 8])
    fnw_d = din("fnw", [128, 8])
    wqkv_d = din("wqkv", [D, 1536])
    qkn_d = din("qkn", [128, 2])
    wo_d = din("wo", [64, 16, D])
    hwin_d = din("hwin", [D, 5 * D])
    lbl_d = din("lbl", [128, 2, 8])
    onw_d = din("onw", [128, 8])
    hwo_d = din("hwo", [D, D])
    fwin_d = din("fwin", [2, D, 2 * FF])
    fwout_d = din("fwout", [2, FF, D])
    yout = Tile(nc.dram_tensor("y", [HALF, D], F32, kind="ExternalOutput").ap(), P.newtid())

    NTOK = HALF + CTX
    H0 = dscr("H0", [128, 8, NTOK])
    QS = dscr("QS", [128, 8, NTOK], BF16)
    H1 = dscr("H1", [128, 8, NTOK])
    H2 = dscr("H2", [128, 8, NTOK])
    OF = dscr("OF", [128, 8, HALF])
    H3 = dscr("H3", [128, 8, HALF])
    if os.environ.get("K_DBG") == "1":
        H4 = dscr("H4", [128, 8, HALF])
        H5 = dscr("H5", [128, 8, HALF])
    NOWN = HALF // 128
    fwin_b = dscr("fwin_b", [2, D, 2 * FF], BF16)
    fwout_r = dscr("fwout_r", [2, 8, 128, 22 * 128], BF16)
    hwin_b = dscr("hwin_b", [D, 5 * D], BF16)
    hwo_b = dscr("hwo_b", [D, D], BF16)
    KX_src = nc.dram_tensor("KXs", [256, HALF], BF16)
    KX_dst = nc.dram_tensor("KXd", [512, HALF], BF16)
    NVH = NOWN // 2
    VX_src = [nc.dram_tensor("VXs%d" % i, [128, NVH * 260], BF16) for i in range(2)]
    VX_dst = [nc.dram_tensor("VXd%d" % i, [256, NVH * 260], BF16) for i in range(2)]
    KXs, KXd = Tile(KX_src, P.newtid()), Tile(KX_dst, P.newtid())
    VXs = [Tile(t_, P.newtid()) for t_ in VX_src]
    VXd = [Tile(t_, P.newtid()) for t_ in VX_dst]
    SX_src = nc.dram_tensor("SXs", [128, 1024], F32)
    SX_dst = nc.dram_tensor("SXd", [256, 1024], F32)
    SXs = Tile(SX_src, P.newtid())
    SXd = Tile(SX_dst, P.newtid())
    parw_d = din("parw", [128, 2])

    A = Arena(nc, P, 207 * 1024)
    pp = [nc.alloc_psum_tensor("pp%d" % i, [128, 1024], F32) for i in range(4)]
    psum = [Tile(pp[i // 2][:, (i % 2) * 512:(i % 2 + 1) * 512], P.newtid()) for i in range(8)]

    def ppair(k, W):
        return V(pp[k][:, :].rearrange("p (a b) -> p a b", a=2)[:, :, 0:W], (psum[2 * k].tid, None), extra=((psum[2 * k + 1].tid, None),))
    for t_ in psum:
        P.excl.add(t_.tid)

    ident = A.alloc([128], F32)
    identb = A.alloc([128], BF16)
    ones_b = A.alloc([128], BF16)
    blk_b = A.alloc([128], BF16)
    ones_f = A.alloc([128], F32)
    pimat = A.alloc([128], F32)
    zero_f = A.alloc([128], F32)
    P.memset("pool", zero_f[:, :], 0.0)
    P.memset("pool", ones_f[:, :], 1.0)
    P.add("pool", lambda e: e.affine_select(out=ident[:, :].ap, in_=zero_f[:, :].ap, pattern=[[-1, 128]],
                                            compare_op=ALU.not_equal, fill=1.0, base=0, channel_multiplier=1),
          reads=[zero_f[:, :].key], writes=[ident[:, :].key])
    P.copy("pool", identb[:, :], ident[:, :])
    P.copy("pool", ones_b[:, :], ones_f[:, :])
    P.memset("pool", blk_b[:, :], 0.0)
    P.memset("pool", blk_b[0:64, 0:64], 1.0)
    P.memset("pool", blk_b[64:128, 64:128], 1.0)
    P.dma("sp", pimat[:, :], pimat_d[:, :])

    smallv = A.alloc([256], F32)
    sv_off = [0]

    def small(n):
        o = sv_off[0]
        sv_off[0] += n
        assert sv_off[0] <= 256
        return o

    o_cv = small(16)
    o_nmw = small(16)
    o_nfw = small(16)
    o_fnw = small(8)
    o_qkn = small(2)
    o_lbl = small(16)
    o_onw = small(8)
    o_lb = small(8)
    o_l1 = small(8)
    o_nl = small(8)
    o_csil = small(16)
    o_parw = small(2)
    sv = smallv

    def svv(o, n):
        return sv[:, o:o + n]

    P.dma("sp", svv(o_cv, 16), cvec[:, :, :].re("p a b -> p (a b)"))
    P.dma("sp", svv(o_nmw, 16), nmw_d[:, :, :].re("p a b -> p (a b)"))
    P.dma("sp", svv(o_nfw, 16), nfw_d[:, :, :].re("p a b -> p (a b)"))
    P.dma("sp", svv(o_fnw, 8), fnw_d[:, :])
    P.dma("sp", svv(o_qkn, 2), qkn_d[:, :])
    P.dma("sp", svv(o_lbl, 16), lbl_d[:, :, :].re("p a b -> p (a b)"))
    P.dma("sp", svv(o_onw, 8), onw_d[:, :])
    P.dma("sp", svv(o_parw, 2), parw_d[:, :])
    parw = sv.view(sv.h[:, o_parw:o_parw + 2])
    adab = A.alloc([96], F32)
    P.dma("sp", adab[:, :], ada_b[:, :, :].re("p a b -> p (a b)"))
    lbe = A.alloc([24], F32)
    P.act(lbe[:, 0:16], svv(o_lbl, 16), AF.Exp)
    P.tt("dve", lbe[:, 16:24], lbe[:, 0:8], lbe[:, 8:16], ALU.add)
    P.recip(lbe[:, 16:24], lbe[:, 16:24])
    P.tt("dve", svv(o_lb, 8), lbe[:, 8:16], lbe[:, 16:24], ALU.mult)
    P.ts("dve", svv(o_l1, 8), svv(o_lb, 8), -1.0, ALU.mult, 1.0, ALU.add)
    P.ts("dve", svv(o_nl, 8), svv(o_l1, 8), -1.0, ALU.mult)
    P.act(svv(o_csil, 16), svv(o_cv, 16), AF.Silu)

    modv = A.alloc([2, 48, 2], F32)
    modA = A.alloc([2, 2, 8, 2], F32)
    m0 = A.mark()
    wblk = [A.alloc([8, 512], F32) for _ in range(2)]
    modrow = A.alloc([6 * D], F32, parts=2)
    csil = svv(o_csil, 16)
    n_ada = 0
    for l in range(2):
        for blk in range(12):
            wb = wblk[n_ada % 2]
            n_ada += 1
            P.dma("sp", wb[:, :, :], ada_w[l, :, blk * 512:(blk + 1) * 512].re("(kc p) n -> p kc n", p=128))
            prow = psum[2 + blk % 2]
            for kc in range(8):
                P.mm(prow[0:2, :], V(sv.h[:, o_csil + kc * 2:o_csil + kc * 2 + 2], csil.key), wb[:, kc, :],
                     start=(kc == 0), stop=(kc == 7))
            P.copy("act" if blk % 2 else "dve", modrow[0:2, blk * 512:(blk + 1) * 512], prow[0:2, :])
        for jj in range(48):
            P.tr(psum[l][:, jj * 2:jj * 2 + 2], modrow[0:2, jj * 128:(jj + 1) * 128], ident[0:2, 0:2])
        P.tt("dve", modv[:, l, :, :], psum[l][:, 0:96].re("p (a b) -> p a b", b=2),
             adab[:, l * 48:(l + 1) * 48].re("p (a b) -> p a b", b=1).bc([128, 48, 2]), ALU.add)
        for nrm, (jsc, ow) in enumerate(((8, o_nmw), (32, o_nfw))):
            wv = sv[:, ow + l * 8:ow + l * 8 + 8].re("p (a b) -> p a b", b=1).bc([128, 8, 2])
            P.stt(modA[:, l, nrm, :, :], modv[:, l, jsc:jsc + 8, :], 1.0, wv, ALU.add, ALU.mult)
    A.release(m0)

    def mod(l, kind, which):
        return lambda kc: modv[:, l, kind * 8 + kc, which:which + 1]

    def modAv(l, nrm, which):
        return lambda kc: modA[:, l, nrm, kc, which:which + 1]

    rr = [0]

    def evac_eng():
        rr[0] += 1
        return "act" if rr[0] % 2 else "dve"

    def load_xT(src_rows, W, xin, xT, pbanks):
        nj = W // 128
        P.dma("sp", xin[:, 0:nj, :], src_rows.re("(j p) d -> p j d", p=128))
        for kc in range(8):
            pb = pbanks[kc % len(pbanks)]
            for j in range(nj):
                P.tr(pb[:, j * 128:(j + 1) * 128], xin[:, j, kc * 128:(kc + 1) * 128], ident[:, :])
            P.copy(evac_eng(), xT[:, kc, 0:W], pb[:, 0:W])

    def norm_mod(xT, W, Af, Bf, hn, sq, tmpf, pbank, nfeat=1024.0):
        for kc in range(8):
            P.act(sq[:, kc, 0:W], xT[:, kc, 0:W], AF.Square)
        for kc in range(8):
            P.mm(pbank[:, 0:W], ones_b[:, :], sq[:, kc, 0:W], start=(kc == 0), stop=(kc == 7))
        P.act(tmpf[0][:, 0:W], pbank[:, 0:W], AF.Ln, scale=1.0 / nfeat, bias=epsv[:, 0:1])
        P.act(tmpf[0][:, 0:W], tmpf[0][:, 0:W], AF.Exp, scale=-0.5)
        for kc in range(8):
            t = tmpf[1 + kc % 2]
            P.tt("dve", t[:, 0:W], xT[:, kc, 0:W], tmpf[0][:, 0:W], ALU.mult)
            P.ts("pool", hn[:, kc, 0:W], t[:, 0:W], Af(kc), ALU.mult, (0.0 if Bf is None else Bf(kc)), ALU.add)

    epsv = A.alloc([1], F32)
    P.memset("pool", epsv[:, :], EPS)

    def layer0():
        mL0 = A.mark()
        NKC = (2 * HALF + CTX) // 128
        KT = A.alloc([2, NKC * 128], BF16)
        VA = A.alloc([NKC, 4, 65], BF16)
        P.memset("pool", VA[:, :, :, 64:65], 1.0)
        mA = A.mark()
        wqkv = A.alloc([8, 1536], BF16)
        for kc in range(8):
            P.dma("pool", wqkv[:, kc, :], wqkv_d[kc * 128:(kc + 1) * 128, :])
        xin = [A.alloc([4, 1024], F32) for _ in range(2)]
        xT2 = [A.alloc([8, 512], F32) for _ in range(2)]
        hn = A.alloc([8, 512], BF16)
        rtab = [A.alloc([2, 512], F32) for _ in range(2)]
        qT = A.alloc([8, 512], BF16)
        sq = qT
        sqh2 = [A.alloc([512], BF16) for _ in range(2)]
        kf2 = [A.alloc([512], F32) for _ in range(2)]
        rs2 = [A.alloc([512], F32) for _ in range(2)]
        t12 = [A.alloc([512], F32) for _ in range(2)]
        t22 = [A.alloc([512], F32) for _ in range(2)]
        tmpf = [A.alloc([512], F32), t12[0], t22[0]]
        qkc = [0]

        def qknorm_rope(ps, W, wv, rt, outv):
            SUB = 9
            par = qkc[0] % 2
            qkc[0] += 1
            sqh, kf, rs, t1, t2 = sqh2[par], kf2[par], rs2[par], t12[par], t22[par]
            pssq = psum[5] if par == 0 else psum[0]
            pkp = psum[6] if par == 0 else psum[1]
            P.act(sqh[:, 0:W], ps[:, 0:W], AF.Square)
            if SUB < 2:
                return
            P.ts("dve", kf[:, 0:W], ps[:, 0:W], wv, ALU.mult)
            P.mm(pssq[:, 0:W], blk_b[:, :], sqh[:, 0:W])
            P.act(rs[:, 0:W], pssq[:, 0:W], AF.Ln, scale=1.0 / 64.0, bias=epsv[:, 0:1])
            P.act(rs[:, 0:W], rs[:, 0:W], AF.Exp, scale=-0.5)
            if SUB < 3:
                return
            if rt is not None:
                P.mm(pkp[:, 0:W], pimat[:, :], kf[:, 0:W])
                P.tt("dve", t1[:, 0:W], kf[:, 0:W], rt[:, 0, 0:W], ALU.mult)
                P.tt("dve", t2[:, 0:W], pkp[:, 0:W], rt[:, 1, 0:W], ALU.mult)
                P.tt("pool", t1[:, 0:W], t1[:, 0:W], t2[:, 0:W], ALU.add)
                P.tt("dve", outv, t1[:, 0:W], rs[:, 0:W], ALU.mult)
            else:
                P.tt("dve", outv, kf[:, 0:W], rs[:, 0:W], ALU.mult)

        tiles = [("own", i) for i in range(HALF // 512)] + [("ctx", 0)]
        for tn, (kind, ti) in enumerate(tiles):
            W = 512 if kind != "ctx" else CTX
            which = 1 if kind == "ctx" else 0
            if kind == "own":
                src = xo[ti * 512:(ti + 1) * 512, :]
                kbase = ti * 512
                hcol = ti * 512
            elif kind == "par":
                src = xp[ti * 512:(ti + 1) * 512, :]
                kbase = HALF + ti * 512
                hcol = None
            else:
                src = cx[:, :]
                kbase = 2 * HALF
                hcol = HALF
            xi = xin[tn % 2]
            xT = xT2[tn % 2]
            load_xT(src, W, xi, xT, [psum[0], psum[1]])
            rt = None
            if kind != "ctx":
                rt = rtab[tn % 2]
                rsrc = rope_o
                P.dma("sp", rt[:, :, :], rsrc[:, :, ti * 512:(ti + 1) * 512].re("a p n -> p a n"))
            if hcol is not None:
                P.dma("sp", H0[:, :, hcol:hcol + W], xT[:, :, 0:W])
            LVL = int(os.environ.get("K_LVL", "9"))
            if LVL < 2:
                continue
            norm_mod(xT, W, modAv(0, 0, which), mod(0, 0, which), hn, sq, tmpf, psum[2])
            if LVL < 3:
                continue
            for c in range(2):
                pb = psum[3 + c % 2]
                for kc in range(8):
                    P.mm(pb[:, 0:W], wqkv[:, kc, 1024 + c * 128:1024 + (c + 1) * 128], hn[:, kc, 0:W],
                         start=(kc == 0), stop=(kc == 7))
                qknorm_rope(pb, W, svv(o_qkn + 1, 1), rt, KT[:, c, kbase:kbase + W])
            for j in range(W // 128 if LVL >= 4 else 0):
                pb = psum[7]
                for kc in range(8):
                    P.mm(pb[:, 0:256], hn[:, kc, j * 128:(j + 1) * 128], wqkv[:, kc, 1280:1536],
                         start=(kc == 0), stop=(kc == 7))
                P.copy(evac_eng(), VA[:, kbase // 128 + j, :, 0:64], pb[:, 0:256].re("p (h d) -> p h d", d=64))
            if hcol is not None and LVL >= 5:
                for c in range(8):
                    pb = psum[3 + c % 2]
                    for kc in range(8):
                        P.mm(pb[:, 0:W], wqkv[:, kc, c * 128:(c + 1) * 128], hn[:, kc, 0:W],
                             start=(kc == 0), stop=(kc == 7))
                    qknorm_rope(pb, W, svv(o_qkn, 1), rt, qT[:, c, 0:W])
                P.dma("sp", QS[:, :, hcol:hcol + W], qT[:, :, 0:W])
        for c in range(2):
            P.dma("sp", KXs[c * 128:(c + 1) * 128, :], KT[:, c, 0:HALF])
        for i in range(2):
            P.dma("sp", VXs[i][:, :], VA[:, i * NVH:(i + 1) * NVH, :, :].re("p j h d -> p (j h d)"))
        grp = [[2 * i, 2 * i + 1] for i in range(NB)]
        P.add("pool", lambda e: e.collective_compute("AllGather", ALU.bypass, ins=[KX_src.ap().opt()], outs=[KX_dst.ap().opt()],
                                                     replica_groups=grp),
              reads=[KXs[:, :].key], writes=[KXd[:, :].key], cc=True)
        for i in range(2):
            P.add("pool", lambda e, i=i: e.collective_compute("AllGather", ALU.bypass, ins=[VX_src[i].ap().opt()], outs=[VX_dst[i].ap().opt()],
                                                              replica_groups=grp),
                  reads=[VXs[i][:, :].key], writes=[VXd[i][:, :].key], cc=True)
        for r in range(2):
            for c in range(2):
                P.dma("sp", KT[:, c, r * HALF:(r + 1) * HALF], KXd[(2 * r + c) * 128:(2 * r + c + 1) * 128, :])
            for i in range(2):
                P.dma("sp", VA[:, r * NOWN + i * NVH:r * NOWN + (i + 1) * NVH, :, :].re("p j h d -> p (j h d)"), VXd[i][r * 128:(r + 1) * 128, :])
        A.release(mA)

        if "B" not in PHASES:
            A.release(mL0)
            return
        wo = A.alloc([16, 1024], BF16, parts=64)
        P.dma("pool", wo[:, :, :], wo_d[:, :, :])
        for l in range(2):
            for kc in range(8):
                P.dma("pool", fwin_b[l, kc * 128:(kc + 1) * 128, :], fwin_d[l, kc * 128:(kc + 1) * 128, :])
            for oc in range(8):
                P.add("pool", lambda e, l=l, oc=oc: e.dma_start(
                          out=fwout_r.h[l, oc, :, :].rearrange("p (a b) -> p a b", b=128),
                          in_=fwout_d.h[l, :, oc * 128:(oc + 1) * 128].rearrange("(a p) b -> p a b", p=128)),
                      reads=[fwout_d[l, :, :].key], writes=[(fwout_r.tid, (l, oc))], dma=True)
            if l == 0:
                for kc in range(8):
                    P.dma("pool", hwin_b[kc * 128:(kc + 1) * 128, :], hwin_d[kc * 128:(kc + 1) * 128, :])
                    P.dma("pool", hwo_b[kc * 128:(kc + 1) * 128, :], hwo_d[kc * 128:(kc + 1) * 128, :])
        qTb = [A.alloc([8, 512], BF16) for _ in range(2)]
        xTb = [A.alloc([8, 512], F32) for _ in range(2)]
        PT = [A.alloc([2, 512], BF16) for _ in range(3)]
        osb = A.alloc([2, 512], F32)
        rinv = A.alloc([2, 512], F32)
        rb = A.alloc([2, 512], F32)
        P.memset("pool", rinv[:, :, :], 1.0)
        oT = A.alloc([16, 512], BF16)
        hmid = A.alloc([8, 512], F32)
        SCALE = 64 ** -0.5
        qtiles = [("own", i) for i in range(HALF // 512)] + [("ctx", 0)]
        for tn, (kind, ti) in enumerate(qtiles):
            W = 512 if kind == "own" else CTX
            which = 0 if kind == "own" else 1
            hcol = ti * 512 if kind == "own" else HALF
            kcs = list(range(NKC)) if kind == "own" else [NKC - 2, NKC - 1]
            qb = qTb[tn % 2]
            xb = xTb[tn % 2]
            P.dma("sp", qb[:, :, 0:W], QS[:, :, hcol:hcol + W])
            P.dma("sp", xb[:, :, 0:W], H0[:, :, hcol:hcol + W])
            step = 0
            for c in range(8):
                kvc = c // 4
                Sb = [(psum[0], psum[1]), (psum[2], psum[3]), (psum[4], psum[5])]
                Ob = (psum[6], psum[7])

                def QK(i, st):
                    kc = kcs[i]
                    sa, sbb = Sb[st % 3]
                    P.mm(sa[:, 0:W], KT[0:64, kvc, kc * 128:(kc + 1) * 128], qb[0:64, c, 0:W])
                    P.mm(sbb[:, 0:W], KT[64:128, kvc, kc * 128:(kc + 1) * 128], qb[64:128, c, 0:W])

                def EXP(i, st):
                    pt = PT[st % 3]
                    P.act(pt[:, :, 0:W], ppair(st % 3, W), AF.Exp, scale=SCALE)

                def PV(i, st):
                    kc = kcs[i]
                    pt = PT[st % 3]
                    for ab in range(2):
                        P.mm(Ob[ab][0:65, 0:W], VA[:, kc, 2 * kvc + ab, 0:65], pt[:, ab, 0:W],
                             start=(i == 0), stop=(i == len(kcs) - 1))

                n = len(kcs)
                for j in range(min(2, n)):
                    QK(j, step + j)
                for i in range(n):
                    if i + 2 < n:
                        QK(i + 2, step + i + 2)
                    EXP(i, step + i)
                    PV(i, step + i)
                step += n
                for ab in range(2):
                    P.copy("dve", osb[0:65, ab, 0:W], Ob[ab][0:65, 0:W])
                P.recip(rinv[64:65, :, 0:W], osb[64:65, :, 0:W])
                P.shuf(rb[0:32, :, 0:W], rinv[64:96, :, 0:W], [0] * 32)
                P.shuf(rb[32:64, :, 0:W], rinv[64:96, :, 0:W], [0] * 32)
                for ab in range(2):
                    P.tt("dve", oT[0:64, 2 * c + ab, 0:W], osb[0:64, ab, 0:W], rb[0:64, ab, 0:W], ALU.mult)
            for oc in range(8):
                pb = psum[oc % 6]
                for j in range(16):
                    P.mm(pb[:, 0:W], wo[0:64, j, oc * 128:(oc + 1) * 128], oT[0:64, j, 0:W], start=(j == 0), stop=(j == 15))
                P.stt(hmid[:, oc, 0:W], pb[:, 0:W], mod(0, 2, which)(oc), xb[:, oc, 0:W], ALU.mult, ALU.add)
            P.dma("sp", H1[:, :, hcol:hcol + W], hmid[:, :, 0:W])
        A.release(mL0)


    if "A" in PHASES:
        layer0()

    def ffn_phase(l, src, dst, tilespecs, final=False):
        m = A.mark()
        win = A.alloc([8, 2 * FF], BF16)
        for kc in range(8):
            P.dma("sp", win[:, kc, :], fwin_b[l, kc * 128:(kc + 1) * 128, :])
        wob = [A.alloc([22, 128], BF16) for _ in range(3)]
        h2 = [A.alloc([8, 512], F32) for _ in range(2)]
        hn2 = [A.alloc([8, 512], BF16) for _ in range(2)]
        sq = A.alloc([8, 512], BF16)
        tmpf = [A.alloc([512], F32) for _ in range(3)]
        sa = [A.alloc([512], F32) for _ in range(2)]
        sT = A.alloc([22, 512], BF16)
        if final:
            fo = sT.view(sT.h[:, 0:16, :].rearrange("p a b -> p (a b)").bitcast(F32).rearrange("p (a b) -> p a b", a=8))
        nt = len(tilespecs)
        nwo = [0]

        def load(t):
            col, W, which = tilespecs[t]
            P.dma("sp", h2[t % 2][:, :, 0:W], src[:, :, col:col + W])

        def norm(t):
            col, W, which = tilespecs[t]
            norm_mod(h2[t % 2], W, modAv(l, 1, which), mod(l, 3, which), hn2[t % 2], sq, tmpf, psum[0])

        def stage_in(t):
            col, W, which = tilespecs[t]
            hn = hn2[t % 2]
            for hc in range(22):
                pa = psum[1 + 2 * (hc % 2)]
                pu = psum[2 + 2 * (hc % 2)]
                for kc in range(8):
                    P.mm(pa[:, 0:W], win[:, kc, hc * 128:(hc + 1) * 128], hn[:, kc, 0:W], start=(kc == 0), stop=(kc == 7))
                for kc in range(8):
                    P.mm(pu[:, 0:W], win[:, kc, FF + hc * 128:FF + (hc + 1) * 128], hn[:, kc, 0:W], start=(kc == 0), stop=(kc == 7))
                s_ = sa[hc % 2]
                P.act(s_[:, 0:W], pa[:, 0:W], AF.Silu)
                P.tt("dve", sT[:, hc, 0:W], s_[:, 0:W], pu[:, 0:W], ALU.mult)

        def stage_out(t):
            col, W, which = tilespecs[t]
            h = h2[t % 2]
            for oc in range(8):
                wb = wob[nwo[0] % 3]
                nwo[0] += 1
                P.dma("sp", wb[:, :, :].re("p a b -> p (a b)"), fwout_r.s((l, oc))[l, oc, :, :])
                pb = psum[5 + oc % 2]
                for hc in range(22):
                    P.mm(pb[:, 0:W], wb[:, hc, :], sT[:, hc, 0:W], start=(hc == 0), stop=(hc == 21))
                P.stt(h[:, oc, 0:W], pb[:, 0:W], mod(l, 5, which)(oc), h[:, oc, 0:W], ALU.mult, ALU.add)
            if not final:
                P.dma("sp", dst[:, :, col:col + W], h[:, :, 0:W])
            else:
                norm_mod(h, W, lambda kc: svv(o_fnw + kc, 1), lambda kc: zero_f[:, 0:1], fo, sq, tmpf, psum[0])
                for j in range(W // 128):
                    for half in range(2):
                        pb = psum[1 + (2 * j + half) % 4]
                        for q in range(4):
                            kc = half * 4 + q
                            P.tr(pb[:, q * 128:(q + 1) * 128], fo[:, kc, j * 128:(j + 1) * 128], ident[:, :])
                        yb = sa[(2 * j + half) % 2]
                        P.copy(evac_eng(), yb[:, :], pb[:, :])
                        P.dma("sp", dst[col + j * 128:col + (j + 1) * 128, half * 512:(half + 1) * 512], yb[:, :])

        load(0)
        if nt > 1:
            load(1)
        norm(0)
        for t in range(nt):
            stage_in(t)
            if t + 1 < nt and not final:
                norm(t + 1)
            stage_out(t)
            if t + 1 < nt and final:
                norm(t + 1)
            if t + 2 < nt:
                load(t + 2)
        A.release(m)

    lat_tiles = [(i * 512, 512, 0) for i in range(HALF // 512)]
    if "F0" in PHASES:
        ffn_phase(0, H1, H2, lat_tiles + [(HALF, CTX, 1)])

    def hgrn_layer():
        mH = A.mark()
        hw = A.alloc([8, 4 * D], BF16)
        hwo = A.alloc([8, D], BF16)
        S32 = A.alloc([8, 128], F32)
        Sbf2 = [A.alloc([8, 128], BF16) for _ in range(2)]
        sc = [0]
        maskF = A.alloc([64], F32, parts=64)
        maskB = A.alloc([64], F32, parts=64)
        rmask = A.alloc([512], BF16)
        P.memset("pool", maskF[:, :], 1.0)
        P.memset("pool", maskB[:, :], 1.0)
        P.add("pool", lambda e: e.affine_select(out=maskF[:, :].ap, in_=maskF[:, :].ap, pattern=[[1, 64]],
                                                compare_op=ALU.is_ge, fill=0.0, base=0, channel_multiplier=-1),
              reads=[maskF[:, :].key], writes=[maskF[:, :].key])
        P.add("pool", lambda e: e.affine_select(out=maskB[:, :].ap, in_=maskB[:, :].ap, pattern=[[-1, 64]],
                                                compare_op=ALU.is_ge, fill=0.0, base=0, channel_multiplier=1),
              reads=[maskB[:, :].key], writes=[maskB[:, :].key])
        P.memset("pool", rmask[:, :], 1.0)
        P.memset("pool", V(rmask.h[:, :].rearrange("p (c t) -> p c t", t=64)[:, :, 0:1], rmask[:, :].key), 0.0)
        h = A.alloc([8, 512], F32)
        sq = A.alloc([8, 512], BF16)
        hn = A.alloc([8, 512], BF16)
        sg2 = [A.alloc([512], F32) for _ in range(2)]
        gT2 = [A.alloc([512], F32) for _ in range(2)]
        kT2 = [A.alloc([512], F32) for _ in range(2)]
        LT2 = [A.alloc([512], F32) for _ in range(2)]
        eL2 = [A.alloc([512], F32) for _ in range(2)]
        e22 = [A.alloc([512], F32) for _ in range(2)]
        tmpf = [gT2[0], gT2[1], kT2[0]]
        gs = sg2[0]
        eTotA = A.alloc([8, 8], F32)
        qtA = A.alloc([8, 512], BF16)
        kvA = A.alloc([16, 512], BF16)
        khA = kvA.view(kvA.h[:, 0:8, :], "kh")
        vTA = kvA.view(kvA.h[:, 8:16, :], "vT")
        Rst = kvA.view(kvA.h[:, :, :].rearrange("p a b -> p (a b)").bitcast(F32).rearrange("p (a b) -> p a b", a=8))
        ktok2 = [A.alloc([8, 128], BF16, parts=64) for _ in range(2)]
        vtok2 = [A.alloc([8, 128], BF16, parts=64) for _ in range(2)]
        Am2 = [A.alloc([8, 64], BF16, parts=64) for _ in range(2)]
        oTt = A.alloc([8, 512], F32)
        hww = hw.s("w")
        hwow = hwo.s("w")

        class HB:
            def __init__(self, fn):
                self.fn = fn

            def __getitem__(self, idx):
                _, hd, sl = idx
                return self.fn(hd, sl)

        def carve(t, a0, sub):
            key = (t.tid, sub)
            return HB(lambda hd, sl: V(t.h[:, a0 + hd // 2, (hd % 2) * 512 + sl.start:(hd % 2) * 512 + sl.stop], key))

        qtB = carve(hw, 0, "qtB") if False else HB(lambda hd, sl: V(hw.h[:, hd // 2, 3 * D + (hd % 2) * 512 + sl.start:3 * D + (hd % 2) * 512 + sl.stop], (hw.tid, "qtB")))
        khB = HB(lambda hd, sl: V(hw.h[:, 4 + hd // 2, 3 * D + (hd % 2) * 512 + sl.start:3 * D + (hd % 2) * 512 + sl.stop], (hw.tid, "khB")))
        vTB = HB(lambda hd, sl: V(hwo.h[:, hd // 2, (hd % 2) * 512 + sl.start:(hd % 2) * 512 + sl.stop], (hwo.tid, "vTB")))
        eTotB = hwo.view(hwo.h[:, 4, 0:128].bitcast(F32).rearrange("p (a b) -> p a b", a=8), "eB")
        bufs = [(qtA, khA, vTA, eTotA), (qtB, khB, vTB, eTotB)]
        pbf = [psum[i].view(psum[i].h[:, 0:512].bitcast(BF16)) for i in range(8)]

        def load_hw(blocks):
            for bi, sb in enumerate(blocks):
                for kc in range(8):
                    P.dma("sp", (hww if bi < 3 else hw)[:, kc, bi * D:(bi + 1) * D], hwin_b[kc * 128:(kc + 1) * 128, sb * D:(sb + 1) * D])

        class TileJob:
            def __init__(self, src_col, W, which, rev, emit_out, xsrc, bset, readout=False, of_col=None):
                self.src_col, self.W, self.which, self.rev, self.emit_out = src_col, W, which, rev, emit_out
                self.xsrc, self.readout, self.of_col = xsrc, readout, of_col
                self.qt, self.kh, self.vT, self.eTot = bufs[bset]
                self.nch = W // 64
                self.order = list(range(self.nch))[::-1] if rev else list(range(self.nch))
                self.mask = maskB if rev else maskF
                self.alone = False

            def load_h(self):
                W = self.W
                P.dma("sp", h[:, :, 0:W], self.xsrc[:, :, self.src_col:self.src_col + W])

            def prologue(self, load=True):
                W = self.W
                if load:
                    self.load_h()
                norm_mod(h, W, modAv(1, 0, self.which), mod(1, 0, self.which), hn, sq, tmpf, psum[7])

            def head(self, hd, standalone=False):
                W, nch, rev = self.W, self.nch, self.rev
                qt, kh, vT, eTot = self.qt, self.kh, self.vT, self.eTot
                par = hd % 2
                sg, gT, kT, LT, eL, e2 = sg2[par], gT2[par], kT2[par], LT2[par], eL2[par], e22[par]
                pz = psum[2] if (standalone and par) else psum[6]
                for kc in range(8):
                    P.mm(pz[:, 0:W], hww[:, kc, D + hd * 128:D + (hd + 1) * 128], hn[:, kc, 0:W], start=(kc == 0), stop=(kc == 7))
                P.act(sg[:, 0:W], pz[:, 0:W], AF.Sigmoid)
                P.act(gT[:, 0:W], sg[:, 0:W], AF.Ln, scale=svv(o_l1 + hd, 1), bias=svv(o_lb + hd, 1))
                P.ts("dve", kT[:, 0:W], sg[:, 0:W], svv(o_nl + hd, 1), ALU.mult, svv(o_l1 + hd, 1), ALU.add)
                P.scan(LT[:, 0:W], rmask[:, 0:W], gT[:, 0:W], 0.0, ALU.mult, ALU.add)
                LTc = V(LT.h[:, 0:W].rearrange("p (c t) -> p c t", t=64), LT[:, :].key)
                P.act(eTot[:, hd, 0:nch], V(LTc.ap[:, :, 63], LTc.key), AF.Exp)
                Lsrc = LT
                if rev:
                    P.tt("dve", gT[:, 0:W], gT[:, 0:W], LT[:, 0:W], ALU.subtract)
                    tot = V(LTc.ap[:, :, 63:64].to_broadcast([128, nch, 64]), LTc.key)
                    P.tt("dve", sg[:, 0:W].re("p (c t) -> p c t", t=64), gT[:, 0:W].re("p (c t) -> p c t", t=64), tot, ALU.add)
                    Lsrc = sg
                P.act(eL[:, 0:W], Lsrc[:, 0:W], AF.Exp)
                P.act(e2[:, 0:W], Lsrc[:, 0:W], AF.Exp, scale=-1.0)
                P.tt("dve", kh[:, hd, slice(0, W)], kT[:, 0:W], e2[:, 0:W], ALU.mult)
                pq = psum[3] if (standalone and par) else psum[7]
                for kc in range(8):
                    P.mm(pq[:, 0:W], hww[:, kc, hd * 128:(hd + 1) * 128], hn[:, kc, 0:W], start=(kc == 0), stop=(kc == 7))
                P.tt("dve", qt[:, hd, slice(0, W)], pq[:, 0:W], eL[:, 0:W], ALU.mult)
                pv = psum[2] if (standalone and par) else psum[6]
                for kc in range(8):
                    P.mm(pv[:, 0:W], hww[:, kc, 2 * D + hd * 128:2 * D + (hd + 1) * 128], hn[:, kc, 0:W], start=(kc == 0), stop=(kc == 7))
                P.copy("act", vT[:, hd, slice(0, W)], pv[:, 0:W])

            def pre(self, k):
                ci = self.order[k]
                cs = slice(ci * 64, (ci + 1) * 64)
                qt, kh, vT = self.qt, self.kh, self.vT
                ktok, vtok, Am = ktok2[k % 2], vtok2[k % 2], Am2[k % 2]
                if self.emit_out:
                    for hd in range(8):
                        P.mm(psum[4][0:64, hd * 64:(hd + 1) * 64], kh[:, hd, cs], qt[:, hd, cs])
                for hd in range(8):
                    P.tr(pbf[2][0:64, hd * 128:(hd + 1) * 128], kh[:, hd, cs], identb[:, :])
                for hd in range(8):
                    P.tr(pbf[3][0:64, hd * 128:(hd + 1) * 128], vT[:, hd, cs], identb[:, :])
                P.copy("act", ktok[0:64, :, :], pbf[2][0:64, :].re("p (h d) -> p h d", d=128))
                P.copy("dve", vtok[0:64, :, :], pbf[3][0:64, :].re("p (h d) -> p h d", d=128))
                if self.emit_out:
                    P.tt("dve", Am[0:64, :, :], psum[4][0:64, 0:512].re("p (h t) -> p h t", t=64),
                         V(self.mask.h[0:64, :].unsqueeze(1).to_broadcast([64, 8, 64]), self.mask[:, :].key), ALU.mult)

            def preU(self, k):
                ktok, vtok = ktok2[k % 2], vtok2[k % 2]
                up = 3 * (k % 2) if self.alone else 0
                for hd in range(8):
                    pb = psum[2 * up + hd // 4]
                    P.mm(pb[:, (hd % 4) * 128:(hd % 4 + 1) * 128], ktok[0:64, hd, :], vtok[0:64, hd, :])

            def post(self, k):
                ci = self.order[k]
                cs = slice(ci * 64, (ci + 1) * 64)
                qt = self.qt
                ktok, vtok, Am = ktok2[k % 2], vtok2[k % 2], Am2[k % 2]
                Scur = Sbf2[sc[0] % 2]
                Snew = Sbf2[(sc[0] + 1) % 2]
                sc[0] += 1
                up = 3 * (k % 2) if self.alone else 0
                U = V(pp[up][:, :].rearrange("p (h d) -> p h d", d=128), (psum[2 * up].tid, None), extra=((psum[2 * up + 1].tid, None),))
                P.add("dve", lambda e, U=U: e.tensor_tensor(out=S32[:, :, :].ap, in0=U.ap, in1=S32[:, :, :].ap, op=ALU.add),
                      reads=[U.key, U.extra[0], S32[:, :, :].key], writes=[S32[:, :, :].key])
                et = V(self.eTot.h[:, :, ci:ci + 1].to_broadcast([128, 8, 128]), self.eTot[:, :, :].key)
                P.tt("dve", S32[:, :, :], S32[:, :, :], et, ALU.mult)
                P.copy("act", Snew[:, :, :], S32[:, :, :])
                if self.emit_out:
                    for hd in range(8):
                        P.mm(psum[5][:, hd * 64:(hd + 1) * 64], Scur[:, hd, :], qt[:, hd, cs], start=True, stop=False)
                        P.mm(psum[5][:, hd * 64:(hd + 1) * 64], vtok[0:64, hd, :], Am[0:64, hd, :], start=False, stop=True)
                    po = psum[5][:, 0:512].re("p (h t) -> p h t", t=64)
                    if self.of_col is None:
                        P.copy("act", oTt[:, :, cs], po)
                    else:
                        P.tt("dve", oTt[:, :, cs], po, oTt[:, :, cs], ALU.add)

            def epilogue(self):
                if not self.readout:
                    return
                qt = self.qt
                og = qt
                P.dma("sp", Rst[:, :, :], self.xsrc[:, :, self.src_col:self.src_col + 512])
                for hd in range(8):
                    par = hd % 2
                    r0 = gT2[par]
                    r1 = kT2[par]
                    gsp = sg2[par]
                    pss = psum[1] if par else psum[5]
                    pg = psum[7] if par else psum[6]
                    P.act(sq[:, hd, :], oTt[:, hd, :], AF.Square)
                    P.mm(pss[:, :], ones_b[:, :], sq[:, hd, :])
                    for kc in range(8):
                        P.mm(pg[:, :], hw[:, kc, 3 * D + hd * 128:3 * D + (hd + 1) * 128], hn[:, kc, :], start=(kc == 0), stop=(kc == 7))
                    P.act(r0[:, :], pss[:, :], AF.Ln, scale=1.0 / 128.0, bias=epsv[:, 0:1])
                    P.act(r0[:, :], r0[:, :], AF.Exp, scale=-0.5)
                    P.act(gsp[:, :], pg[:, :], AF.Sigmoid)
                    P.stt(r1[:, :], oTt[:, hd, :], svv(o_onw + hd, 1), r0[:, :], ALU.mult, ALU.mult)
                    P.tt("dve", og[:, hd, slice(0, 512)], r1[:, :], gsp[:, :], ALU.mult)
                for oc in range(8):
                    pb = psum[2 + oc % 2]
                    for kc in range(8):
                        P.mm(pb[:, :], hwow[:, kc, oc * 128:(oc + 1) * 128], og[:, kc, slice(0, 512)], start=(kc == 0), stop=(kc == 7))
                    P.stt(Rst[:, oc, :], pb[:, :], mod(1, 2, 0)(oc), Rst[:, oc, :], ALU.mult, ALU.add)
                P.dma("sp", H3[:, :, self.src_col:self.src_col + 512], Rst[:, :, :])

        def run_chunks(job, nxt):
            job.alone = nxt is None
            n = job.nch
            job.pre(0)
            job.preU(0)
            for k in range(n):
                if k + 1 < n:
                    job.pre(k + 1)
                job.post(k)
                if nxt is not None and k < 8:
                    if k == 0:
                        nxt.prologue()
                    nxt.head(k)
                if k + 1 < n:
                    job.preU(k + 1)
            if nxt is not None:
                for hd in range(n, 8):
                    if n == 0:
                        nxt.prologue()
                    nxt.head(hd)

        load_hw([0, 2, 4])
        P.memset("dve", S32[:, :, :], 0.0)
        P.memset("dve", Sbf2[0][:, :, :], 0.0)
        jobs = [TileJob(HALF, CTX, 1, False, False, H2, 0)]
        for ti in range(HALF // 512):
            jobs.append(TileJob(ti * 512, 512, 0, False, True, H2, (ti + 1) % 2))
        jobs[0].prologue()
        for hd in range(8):
            jobs[0].head(hd, standalone=True)
        for ji, job in enumerate(jobs):
            nxt = jobs[ji + 1] if ji + 1 < len(jobs) else None
            run_chunks(job, nxt)
            if job.emit_out:
                P.dma("sp", OF[:, :, job.src_col:job.src_col + 512], oTt[:, :, :])
        P.dma("sp", SXs[:, :], S32[:, :, :].re("p h d -> p (h d)"))
        P.add("pool", lambda e: e.collective_compute("AllGather", ALU.bypass, ins=[SX_src.ap().opt()], outs=[SX_dst.ap().opt()],
                                                     replica_groups=[[2 * i, 2 * i + 1] for i in range(NB)]),
              reads=[SXs[:, :].key], writes=[SXd[:, :].key], cc=True)
        load_hw([0, 3, 4, 1])
        for kc in range(8):
            P.dma("sp", hwo[:, kc, :], hwo_b[kc * 128:(kc + 1) * 128, :])
        g0 = h[:, 0:2, :].re("p a b -> p (a b)")
        g1 = oTt[:, 0:2, :].re("p a b -> p (a b)")
        P.dma("sp", g0, SXd[0:128, :])
        P.dma("sp", g1, SXd[128:256, :])
        P.ts("dve", g0, g0, parw[:, 0:1], ALU.mult)
        P.stt(S32[:, :, :].re("p h d -> p (h d)"), g1, parw[:, 1:2], g0, ALU.mult, ALU.add)
        P.copy("act", Sbf2[sc[0] % 2][:, :, :], S32[:, :, :])
        tis = list(reversed(range(HALF // 512)))
        jobs2 = [TileJob(ti * 512, 512, 0, True, True, H2, 0, readout=True, of_col=ti * 512) for ti in tis]
        jobs2[0].load_h()
        for ji, job in enumerate(jobs2):
            job.prologue(load=False)
            P.dma("sp", oTt[:, :, :], OF[:, :, job.src_col:job.src_col + 512])
            if ji + 1 < len(jobs2):
                jobs2[ji + 1].load_h()
            for hd in range(8):
                job.head(hd, standalone=True)
            run_chunks(job, None)
            job.epilogue()
        A.release(mH)

    if "H" in PHASES:
        hgrn_layer()

    if "F1" in PHASES:
        ffn_phase(1, H3, yout, lat_tiles, final=True)

    if dbg_fn is not None:
        dbg_fn(locals())

    P.barrier()
    P.emit()
    return nc, P


_CACHE = {}


def _rope_tables(pos):
    inv = (np.float32(10000.0) ** (-(np.arange(16, dtype=np.float32) * np.float32(2.0) / np.float32(32.0)))).astype(np.float32)
    row = (pos // 64).astype(np.float32)
    col = (pos % 64).astype(np.float32)
    ar = row[:, None] * inv[None, :]
    ac = col[:, None] * inv[None, :]
    cr, sr, cc, sc = np.cos(ar), np.sin(ar), np.cos(ac), np.sin(ac)
    C = np.concatenate([cr, cr, cc, cc], axis=1).T.astype(np.float32)
    S = np.concatenate([-sr, sr, -sc, sc], axis=1).T.astype(np.float32)
    C = np.concatenate([C, C], axis=0)
    S = np.concatenate([S, S], axis=0)
    return np.ascontiguousarray(np.stack([C, S], axis=0))


def _fm(v):
    return np.ascontiguousarray(np.asarray(v).reshape(8, 128).T)


def kernel(x, c, ctx, c_ctx, ada_w, ada_b, norm_mix_w, norm_ffn_w, attn_w_qkv, attn_q_norm,
           attn_k_norm, attn_w_o, hgrn_w_in, hgrn_lb_logits, hgrn_out_norm, hgrn_w_o,
           ffn_w_in, ffn_w_out, final_norm_w):
    in_maps, gather = make_inputs(x, c, ctx, c_ctx, ada_w, ada_b, norm_mix_w, norm_ffn_w, attn_w_qkv, attn_q_norm,
                                  attn_k_norm, attn_w_o, hgrn_w_in, hgrn_lb_logits, hgrn_out_norm, hgrn_w_o,
                                  ffn_w_in, ffn_w_out, final_norm_w)
    if "nc" not in _CACHE:
        _CACHE["nc"] = build_program()[0]
    nc = _CACHE["nc"]
    res = run_bass_kernel_spmd(nc, in_maps, core_ids=list(range(2 * NB)))
    return gather([r["y"] for r in res.results])


def make_inputs(x, c, ctx, c_ctx, ada_w, ada_b, norm_mix_w, norm_ffn_w, attn_w_qkv, attn_q_norm,
                attn_k_norm, attn_w_o, hgrn_w_in, hgrn_lb_logits, hgrn_out_norm, hgrn_w_o,
                ffn_w_in, ffn_w_out, final_norm_w):
    f = lambda a: np.ascontiguousarray(np.asarray(a, dtype=np.float32))
    x, c, ctx, c_ctx = f(x), f(c), f(ctx), f(c_ctx)
    idx0 = np.arange(HALF)
    idx1 = SEQ - 1 - np.arange(HALF)
    rope = [_rope_tables(idx0), _rope_tables(idx1)]
    pim = np.zeros((128, 128), np.float32)
    for m in range(128):
        d = m % 64
        pi = d + 16 if (d % 32) < 16 else d - 16
        pim[(m // 64) * 64 + pi, m] = 1.0
    wqkv = f(attn_w_qkv)[0]
    qcols = np.concatenate([np.arange(h * 64, (h + 1) * 64) for h in HEAD_ORDER])
    wqkv_dev = np.ascontiguousarray(np.concatenate([wqkv[:, qcols], wqkv[:, 1024:]], axis=1))
    wo = f(attn_w_o)[0]
    wo_dev = np.ascontiguousarray(np.stack([wo[h * 64:(h + 1) * 64, :] for h in HEAD_ORDER], axis=1))
    qkn = np.ascontiguousarray(np.stack([np.tile(f(attn_q_norm)[0], 2), np.tile(f(attn_k_norm)[0], 2)], axis=1))
    hwin = f(hgrn_w_in)[0]
    hwin_sw = np.ascontiguousarray(np.concatenate([hwin[:, :2 * D], hwin[:, 3 * D:4 * D], hwin[:, 2 * D:3 * D], hwin[:, 4 * D:]], axis=1))
    adab = np.ascontiguousarray(np.stack([f(ada_b)[l].reshape(48, 128).T for l in range(2)], axis=1))
    nmw = np.ascontiguousarray(np.stack([_fm(f(norm_mix_w)[l]) for l in range(2)], axis=1))
    nfw = np.ascontiguousarray(np.stack([_fm(f(norm_ffn_w)[l]) for l in range(2)], axis=1))
    lbl = np.ascontiguousarray(np.stack([_fm(f(hgrn_lb_logits)[l]) for l in range(2)], axis=1))
    common = {
        "pimat": pim, "ada_w": f(ada_w), "ada_b": adab, "nmw": nmw, "nfw": nfw, "fnw": _fm(f(final_norm_w)),
        "wqkv": wqkv_dev, "qkn": qkn, "wo": wo_dev, "lbl": lbl, "onw": _fm(f(hgrn_out_norm)[0]),
        "hwo": f(hgrn_w_o)[0], "fwin": f(ffn_w_in), "fwout": f(ffn_w_out),
    }
    in_maps = []
    for core in range(2 * NB):
        b, s = core // 2, core % 2
        own = idx0 if s == 0 else idx1
        par = idx1 if s == 0 else idx0
        m = dict(common)
        m["xo"] = np.ascontiguousarray(x[b][own])
        m["cx"] = np.ascontiguousarray(ctx[b] if s == 0 else ctx[b][::-1])
        m["cvec"] = np.ascontiguousarray(np.stack([_fm(c[b]), _fm(c_ctx)], axis=2))
        m["rope_o"] = rope[s]
        m["hwin"] = hwin if s == 0 else hwin_sw
        m["parw"] = np.ascontiguousarray(np.tile(np.array([[0.0, 1.0]] if s == 0 else [[1.0, 0.0]], np.float32), (128, 1)))
        in_maps.append(m)

    def gather(ys):
        out = np.empty((NB, SEQ, D), np.float32)
        for core in range(2 * NB):
            b, s = core // 2, core % 2
            own = idx0 if s == 0 else idx1
            out[b][own] = ys[core]
        return out

    return in_maps, gather
```

```python
import os
import numpy as np
import concourse.bass as bass
import concourse.mybir as mybir
from concourse.bass_utils import run_bass_kernel_spmd

F32 = mybir.dt.float32
BF16 = mybir.dt.bfloat16
AF = mybir.ActivationFunctionType
ALU = mybir.AluOpType

D = 1024
SEQ = 8192
HALF = 4096
CTX = 256
NB = 4
FF = 2816
EPS = 1e-6
HEAD_ORDER = [0, 4, 1, 5, 2, 6, 3, 7, 8, 12, 9, 13, 10, 14, 11, 15]
ENGS = ["pe", "act", "dve", "pool", "sp"]
DMAQ = ("sp", "pool")
NDS = 8


class V:
    __slots__ = ("ap", "key", "extra")

    def __init__(self, ap, key, extra=()):
        self.ap = ap
        self.key = key
        self.extra = extra

    def bc(self, shape):
        return V(self.ap.to_broadcast(list(shape)), self.key)

    def re(self, pat, **kw):
        return V(self.ap.rearrange(pat, **kw), self.key)


class Tile:
    def __init__(self, h, tid, sub=None):
        self.h = h
        self.tid = tid
        self.sub = sub

    def __getitem__(self, idx):
        return V(self.h[idx], (self.tid, self.sub))

    def s(self, sub):
        return Tile(self.h, self.tid, sub)

    def view(self, ap, sub=None):
        return Tile(ap, self.tid, sub)


class Op:
    __slots__ = ("fn", "deps", "sig", "dma", "dsem", "dval", "cnt", "cc")

    def __init__(self, fn, deps, dma, cc=False):
        self.fn = fn
        self.deps = deps
        self.sig = False
        self.dma = dma
        self.dsem = None
        self.dval = 0
        self.cnt = 0
        self.cc = cc


class Prog:
    def __init__(self, nc):
        self.nc = nc
        self.ops = {e: [] for e in ENGS}
        self.last_w = {}
        self.readers = {}
        self.subs = {}
        self.ntid = 0
        self.ndma = {q: [] for q in DMAQ}
        self.dma_since_barrier = []
        self.last_real = {}
        self.excl = set()

    def newtid(self):
        self.ntid += 1
        return self.ntid

    def _conf(self, key):
        tid, sub = key
        ss = self.subs.setdefault(tid, set())
        ss.add(sub)
        if sub is None:
            return [(tid, s) for s in ss]
        return [(tid, sub), (tid, None)] if None in ss else [(tid, sub)]

    def add(self, eng, fn, reads=(), writes=(), dma=False, cc=False):
        idx = len(self.ops[eng])
        deps = set()
        xr = [k for k in reads if k[0] in self.excl]
        if xr:
            reads = [k for k in reads if k[0] not in self.excl]
            writes = list(writes) + xr
        for k in reads:
            for ck in self._conf(k):
                w = self.last_w.get(ck)
                if w is not None:
                    deps.add(w)
        for k in writes:
            for ck in self._conf(k):
                w = self.last_w.get(ck)
                if w is not None:
                    deps.add(w)
                for r in self.readers.get(ck, ()):
                    deps.add(r)
        if cc:
            self.dma_since_barrier.append((eng, idx))
        elif dma:
            lst = self.ndma[eng]
            if len(lst) >= NDS:
                deps.add((eng, lst[len(lst) - NDS]))
            lst.append(idx)
            self.dma_since_barrier.append((eng, idx))
        deps.discard((eng, idx))
        if eng == "pe":
            deps = {d for d in deps if d[0] != "pe"}
        op = Op(fn, deps, dma or cc, cc)
        self.ops[eng].append(op)
        self.last_real[eng] = idx
        me = (eng, idx)
        for k in writes:
            if k[1] is None:
                for ck in self._conf(k):
                    self.last_w[ck] = me
                    self.readers[ck] = []
            else:
                self.last_w[k] = me
                self.readers[k] = []
        for k in reads:
            rl = self.readers.setdefault(k, [])
            if not dma:
                rl[:] = [r for r in rl if r[0] != eng or self.ops[r[0]][r[1]].dma]
            rl.append(me)
        return me

    def barrier(self):
        lasts = [(e, i) for e, i in self.last_real.items()]
        dmas = list(self.dma_since_barrier)
        self.dma_since_barrier = []
        for e in ENGS:
            deps = set(lasts) | set(dmas)
            op = Op(None, deps, False)
            self.ops[e].append(op)

    def mm(self, out, lhsT, rhs, start=True, stop=True):
        self.add("pe", lambda e: e.matmul(out.ap, lhsT=lhsT.ap, rhs=rhs.ap, start=start, stop=stop),
                 reads=[lhsT.key, rhs.key], writes=[out.key])

    def tr(self, out, in_, ident):
        self.add("pe", lambda e: e.transpose(out.ap, in_.ap, ident.ap), reads=[in_.key, ident.key], writes=[out.key])

    def act(self, out, in_, func, scale=1.0, bias=0.0):
        reads = [in_.key] + list(in_.extra)
        sc = scale
        bi = bias
        if isinstance(scale, V):
            reads.append(scale.key)
            sc = scale.ap
        if isinstance(bias, V):
            reads.append(bias.key)
            bi = bias.ap
        self.add("act", lambda e: e.activation(out=out.ap, in_=in_.ap, func=func, bias=bi, scale=sc),
                 reads=reads, writes=[out.key])

    def copy(self, eng, out, in_):
        if eng == "act":
            self.add("act", lambda e: e.copy(out=out.ap, in_=in_.ap), reads=[in_.key], writes=[out.key])
        else:
            self.add(eng, lambda e: e.tensor_copy(out=out.ap, in_=in_.ap), reads=[in_.key], writes=[out.key])

    def tt(self, eng, out, in0, in1, op):
        self.add(eng, lambda e: e.tensor_tensor(out=out.ap, in0=in0.ap, in1=in1.ap, op=op),
                 reads=[in0.key, in1.key], writes=[out.key])

    def ts(self, eng, out, in0, s1, op0, s2=None, op1=None):
        reads = [in0.key]
        a1 = s1
        a2 = s2
        if isinstance(s1, V):
            reads.append(s1.key)
            a1 = s1.ap
        if isinstance(s2, V):
            reads.append(s2.key)
            a2 = s2.ap
        if op1 is None:
            self.add(eng, lambda e: e.tensor_scalar(out=out.ap, in0=in0.ap, scalar1=a1, scalar2=None, op0=op0),
                     reads=reads, writes=[out.key])
        else:
            self.add(eng, lambda e: e.tensor_scalar(out=out.ap, in0=in0.ap, scalar1=a1, scalar2=a2, op0=op0, op1=op1),
                     reads=reads, writes=[out.key])

    def stt(self, out, in0, scalar, in1, op0, op1):
        reads = [in0.key, in1.key]
        sc = scalar
        if isinstance(scalar, V):
            reads.append(scalar.key)
            sc = scalar.ap
        self.add("dve", lambda e: e.scalar_tensor_tensor(out=out.ap, in0=in0.ap, scalar=sc, in1=in1.ap, op0=op0, op1=op1),
                 reads=reads, writes=[out.key])

    def scan(self, out, d0, d1, initial, op0, op1):
        self.add("dve", lambda e: e.tensor_tensor_scan(out=out.ap, data0=d0.ap, data1=d1.ap, initial=initial, op0=op0, op1=op1),
                 reads=[d0.key, d1.key], writes=[out.key])

    def recip(self, out, in_):
        self.add("dve", lambda e: e.reciprocal(out=out.ap, in_=in_.ap), reads=[in_.key], writes=[out.key])

    def shuf(self, out, in_, mask):
        self.add("dve", lambda e: e.stream_shuffle(out=out.ap, in_=in_.ap, mask=mask), reads=[in_.key], writes=[out.key])

    def memset(self, eng, out, val):
        self.add(eng, lambda e: e.memset(out.ap, val), writes=[out.key])

    def dma(self, q, out, in_):
        if q == "pool":
            self.add(q, lambda e: e.dma_start(out=out.ap, in_=in_.ap, max_dma_last_dim=4096), reads=[in_.key], writes=[out.key], dma=True)
        else:
            self.add(q, lambda e: e.dma_start(out=out.ap, in_=in_.ap), reads=[in_.key], writes=[out.key], dma=True)

    def emit(self, extra_ctx=None):
        nc = self.nc
        ops = self.ops
        for e in ENGS:
            for op in ops[e]:
                for (e2, i2) in op.deps:
                    ops[e2][i2].sig = True
        for e in ENGS:
            c = 0
            for op in ops[e]:
                if op.sig and not op.dma:
                    c += 1
                op.cnt = c
        from contextlib import ExitStack
        with ExitStack() as es:
            sems = {e: es.enter_context(nc.semaphore("s_" + e)) for e in ENGS}
            dsems = {q: [es.enter_context(nc.semaphore("d_%s%d" % (q, j))) for j in range(NDS)] for q in DMAQ}
            for q in DMAQ:
                for n, idx in enumerate(self.ndma[q]):
                    op = ops[q][idx]
                    op.dsem = dsems[q][n % NDS]
                    op.dval = 16 * (n // NDS + 1)
            ncc = 0
            for e in ENGS:
                for op in ops[e]:
                    if op.cc:
                        op.dsem = es.enter_context(nc.semaphore("ccs%d" % ncc))
                        ncc += 1
                        op.dval = 1
            block = es.enter_context(nc.Block())
            self.nwaits = 0

            def run(ename, eng):
                known = {}
                for idx, op in enumerate(ops[ename]):
                    need = {}
                    for (e2, i2) in op.deps:
                        o2 = ops[e2][i2]
                        if o2.dma:
                            sem, val = o2.dsem, o2.dval
                        else:
                            sem, val = sems[e2], o2.cnt
                        k = id(sem)
                        if known.get(k, 0) >= val:
                            continue
                        if k not in need or need[k][1] < val:
                            need[k] = (sem, val)
                    for k, (sem, val) in need.items():
                        eng.wait_ge(sem, val)
                        known[k] = val
                        self.nwaits += 1
                    if op.fn is None:
                        assert not op.sig
                        continue
                    ins = op.fn(eng)
                    if op.cc:
                        ins.then_inc(op.dsem)
                    elif op.dma:
                        ins.then_inc(op.dsem, 16)
                    elif op.sig:
                        ins.then_inc(sems[ename], 1)

            @block.tensor
            def _(t):
                run("pe", t)

            @block.scalar
            def _(a):
                run("act", a)

            @block.vector
            def _(v):
                run("dve", v)

            @block.gpsimd
            def _(g):
                run("pool", g)

            @block.sync
            def _(s):
                run("sp", s)


class Arena:
    def __init__(self, nc, P, nbytes):
        self.P = P
        self.nbytes = nbytes
        self.h = nc.alloc_sbuf_tensor("arena", [128, nbytes // 4], F32)
        self.off = 0

    def alloc(self, shape, dtype, parts=128):
        n = 1
        for x in shape:
            n *= x
        esz = 4 if dtype == F32 else 2
        nb = (n * esz + 63) // 64 * 64
        assert self.off + nb <= self.nbytes, ("SBUF arena overflow", self.off, nb, self.nbytes)
        w0 = self.off // 4
        self.off += nb
        ap = self.h[0:parts, w0:w0 + nb // 4]
        if dtype != F32:
            ap = ap.bitcast(dtype)
        ap = ap[:, 0:n]
        if len(shape) == 2:
            ap = ap.rearrange("p (a b) -> p a b", a=shape[0])
        elif len(shape) == 3:
            ap = ap.rearrange("p (a b c) -> p a b c", a=shape[0], b=shape[1])
        elif len(shape) == 4:
            ap = ap.rearrange("p (a b c d) -> p a b c d", a=shape[0], b=shape[1], c=shape[2])
        return Tile(ap, self.P.newtid())

    def mark(self):
        return self.off

    def release(self, m):
        self.P.barrier()
        self.off = m


def build_program(PHASES=("A", "B", "F0", "H", "F1"), dbg_fn=None):
    nc = bass.Bass("TRN2", target_bir_lowering=False)
    P = Prog(nc)

    def din(name, shape, dt=F32):
        return Tile(nc.dram_tensor(name, list(shape), dt, kind="ExternalInput").ap(), P.newtid())

    def dscr(name, shape, dt=F32):
        if os.environ.get("K_DBG") == "1":
            return Tile(nc.dram_tensor(name, list(shape), dt, kind="ExternalOutput").ap(), P.newtid())
        return Tile(nc.dram_tensor(name, list(shape), dt), P.newtid())

    xo = din("xo", [HALF, D])
    cx = din("cx", [CTX, D])
    cvec = din("cvec", [128, 8, 2])
    rope_o = din("rope_o", [2, 128, HALF])
    pimat_d = din("pimat", [128, 128])
    ada_w = din("ada_w", [2, D, 6 * D])
    ada_b = din("ada_b", [128, 2, 48])
    nmw_d = din("nmw", [128, 2, 8])
    nfw_d = din("nfw", [128, 2, 8])
    fnw_d = din("fnw", [128, 8])
    wqkv_d = din("wqkv", [D, 1536])
    qkn_d = din("qkn", [128, 2])
    wo_d = din("wo", [64, 16, D])
    hwin_d = din("hwin", [D, 5 * D])
    lbl_d = din("lbl", [128, 2, 8])
    onw_d = din("onw", [128, 8])
    hwo_d = din("hwo", [D, D])
    fwin_d = din("fwin", [2, D, 2 * FF])
    fwout_d = din("fwout", [2, FF, D])
    yout = Tile(nc.dram_tensor("y", [HALF, D], F32, kind="ExternalOutput").ap(), P.newtid())

    NTOK = HALF + CTX
    H0 = dscr("H0", [128, 8, NTOK])
    QS = dscr("QS", [128, 8, NTOK], BF16)
    H1 = dscr("H1", [128, 8, NTOK])
    H2 = dscr("H2", [128, 8, NTOK])
    OF = dscr("OF", [128, 8, HALF])
    H3 = dscr("H3", [128, 8, HALF])
    if os.environ.get("K_DBG") == "1":
        H4 = dscr("H4", [128, 8, HALF])
        H5 = dscr("H5", [128, 8, HALF])
    NOWN = HALF // 128
    fwin_b = dscr("fwin_b", [2, D, 2 * FF], BF16)
    fwout_r = dscr("fwout_r", [2, 8, 128, 22 * 128], BF16)
    hwin_b = dscr("hwin_b", [D, 5 * D], BF16)
    hwo_b = dscr("hwo_b", [D, D], BF16)
    KX_src = nc.dram_tensor("KXs", [256, HALF], BF16)
    KX_dst = nc.dram_tensor("KXd", [512, HALF], BF16)
    NVH = NOWN // 2
    VX_src = [nc.dram_tensor("VXs%d" % i, [128, NVH * 260], BF16) for i in range(2)]
    VX_dst = [nc.dram_tensor("VXd%d" % i, [256, NVH * 260], BF16) for i in range(2)]
    KXs, KXd = Tile(KX_src, P.newtid()), Tile(KX_dst, P.newtid())
    VXs = [Tile(t_, P.newtid()) for t_ in VX_src]
    VXd = [Tile(t_, P.newtid()) for t_ in VX_dst]
    SX_src = nc.dram_tensor("SXs", [128, 1024], F32)
    SX_dst = nc.dram_tensor("SXd", [256, 1024], F32)
    SXs = Tile(SX_src, P.newtid())
    SXd = Tile(SX_dst, P.newtid())
    parw_d = din("parw", [128, 2])

    A = Arena(nc, P, 207 * 1024)
    pp = [nc.alloc_psum_tensor("pp%d" % i, [128, 1024], F32) for i in range(4)]
    psum = [Tile(pp[i // 2][:, (i % 2) * 512:(i % 2 + 1) * 512], P.newtid()) for i in range(8)]

    def ppair(k, W):
        return V(pp[k][:, :].rearrange("p (a b) -> p a b", a=2)[:, :, 0:W], (psum[2 * k].tid, None), extra=((psum[2 * k + 1].tid, None),))
    for t_ in psum:
        P.excl.add(t_.tid)

    ident = A.alloc([128], F32)
    identb = A.alloc([128], BF16)
    ones_b = A.alloc([128], BF16)
    blk_b = A.alloc([128], BF16)
    ones_f = A.alloc([128], F32)
    pimat = A.alloc([128], F32)
    zero_f = A.alloc([128], F32)
    P.memset("pool", zero_f[:, :], 0.0)
    P.memset("pool", ones_f[:, :], 1.0)
    P.add("pool", lambda e: e.affine_select(out=ident[:, :].ap, in_=zero_f[:, :].ap, pattern=[[-1, 128]],
                                            compare_op=ALU.not_equal, fill=1.0, base=0, channel_multiplier=1),
          reads=[zero_f[:, :].key], writes=[ident[:, :].key])
    P.copy("pool", identb[:, :], ident[:, :])
    P.copy("pool", ones_b[:, :], ones_f[:, :])
    P.memset("pool", blk_b[:, :], 0.0)
    P.memset("pool", blk_b[0:64, 0:64], 1.0)
    P.memset("pool", blk_b[64:128, 64:128], 1.0)
    P.dma("sp", pimat[:, :], pimat_d[:, :])

    smallv = A.alloc([256], F32)
    sv_off = [0]

    def small(n):
        o = sv_off[0]
        sv_off[0] += n
        assert sv_off[0] <= 256
        return o

    o_cv = small(16)
    o_nmw = small(16)
    o_nfw = small(16)
    o_fnw = small(8)
    o_qkn = small(2)
    o_lbl = small(16)
    o_onw = small(8)
    o_lb = small(8)
    o_l1 = small(8)
    o_nl = small(8)
    o_csil = small(16)
    o_parw = small(2)
    sv = smallv

    def svv(o, n):
        return sv[:, o:o + n]

    P.dma("sp", svv(o_cv, 16), cvec[:, :, :].re("p a b -> p (a b)"))
    P.dma("sp", svv(o_nmw, 16), nmw_d[:, :, :].re("p a b -> p (a b)"))
    P.dma("sp", svv(o_nfw, 16), nfw_d[:, :, :].re("p a b -> p (a b)"))
    P.dma("sp", svv(o_fnw, 8), fnw_d[:, :])
    P.dma("sp", svv(o_qkn, 2), qkn_d[:, :])
    P.dma("sp", svv(o_lbl, 16), lbl_d[:, :, :].re("p a b -> p (a b)"))
    P.dma("sp", svv(o_onw, 8), onw_d[:, :])
    P.dma("sp", svv(o_parw, 2), parw_d[:, :])
    parw = sv.view(sv.h[:, o_parw:o_parw + 2])
    adab = A.alloc([96], F32)
    P.dma("sp", adab[:, :], ada_b[:, :, :].re("p a b -> p (a b)"))
    lbe = A.alloc([24], F32)
    P.act(lbe[:, 0:16], svv(o_lbl, 16), AF.Exp)
    P.tt("dve", lbe[:, 16:24], lbe[:, 0:8], lbe[:, 8:16], ALU.add)
    P.recip(lbe[:, 16:24], lbe[:, 16:24])
    P.tt("dve", svv(o_lb, 8), lbe[:, 8:16], lbe[:, 16:24], ALU.mult)
    P.ts("dve", svv(o_l1, 8), svv(o_lb, 8), -1.0, ALU.mult, 1.0, ALU.add)
    P.ts("dve", svv(o_nl, 8), svv(o_l1, 8), -1.0, ALU.mult)
    P.act(svv(o_csil, 16), svv(o_cv, 16), AF.Silu)

    modv = A.alloc([2, 48, 2], F32)
    modA = A.alloc([2, 2, 8, 2], F32)
    m0 = A.mark()
    wblk = [A.alloc([8, 512], F32) for _ in range(2)]
    modrow = A.alloc([6 * D], F32, parts=2)
    csil = svv(o_csil, 16)
    n_ada = 0
    for l in range(2):
        for blk in range(12):
            wb = wblk[n_ada % 2]
            n_ada += 1
            P.dma("sp", wb[:, :, :], ada_w[l, :, blk * 512:(blk + 1) * 512].re("(kc p) n -> p kc n", p=128))
            prow = psum[2 + blk % 2]
            for kc in range(8):
                P.mm(prow[0:2, :], V(sv.h[:, o_csil + kc * 2:o_csil + kc * 2 + 2], csil.key), wb[:, kc, :],
                     start=(kc == 0), stop=(kc == 7))
            P.copy("act" if blk % 2 else "dve", modrow[0:2, blk * 512:(blk + 1) * 512], prow[0:2, :])
        for jj in range(48):
            P.tr(psum[l][:, jj * 2:jj * 2 + 2], modrow[0:2, jj * 128:(jj + 1) * 128], ident[0:2, 0:2])
        P.tt("dve", modv[:, l, :, :], psum[l][:, 0:96].re("p (a b) -> p a b", b=2),
             adab[:, l * 48:(l + 1) * 48].re("p (a b) -> p a b", b=1).bc([128, 48, 2]), ALU.add)
        for nrm, (jsc, ow) in enumerate(((8, o_nmw), (32, o_nfw))):
            wv = sv[:, ow + l * 8:ow + l * 8 + 8].re("p (a b) -> p a b", b=1).bc([128, 8, 2])
            P.stt(modA[:, l, nrm, :, :], modv[:, l, jsc:jsc + 8, :], 1.0, wv, ALU.add, ALU.mult)
    A.release(m0)

    def mod(l, kind, which):
        return lambda kc: modv[:, l, kind * 8 + kc, which:which + 1]

    def modAv(l, nrm, which):
        return lambda kc: modA[:, l, nrm, kc, which:which + 1]

    rr = [0]

    def evac_eng():
        rr[0] += 1
        return "act" if rr[0] % 2 else "dve"

    def load_xT(src_rows, W, xin, xT, pbanks):
        nj = W // 128
        P.dma("sp", xin[:, 0:nj, :], src_rows.re("(j p) d -> p j d", p=128))
        for kc in range(8):
            pb = pbanks[kc % len(pbanks)]
            for j in range(nj):
                P.tr(pb[:, j * 128:(j + 1) * 128], xin[:, j, kc * 128:(kc + 1) * 128], ident[:, :])
            P.copy(evac_eng(), xT[:, kc, 0:W], pb[:, 0:W])

    def norm_mod(xT, W, Af, Bf, hn, sq, tmpf, pbank, nfeat=1024.0):
        for kc in range(8):
            P.act(sq[:, kc, 0:W], xT[:, kc, 0:W], AF.Square)
        for kc in range(8):
            P.mm(pbank[:, 0:W], ones_b[:, :], sq[:, kc, 0:W], start=(kc == 0), stop=(kc == 7))
        P.act(tmpf[0][:, 0:W], pbank[:, 0:W], AF.Ln, scale=1.0 / nfeat, bias=epsv[:, 0:1])
        P.act(tmpf[0][:, 0:W], tmpf[0][:, 0:W], AF.Exp, scale=-0.5)
        for kc in range(8):
            t = tmpf[1 + kc % 2]
            P.tt("dve", t[:, 0:W], xT[:, kc, 0:W], tmpf[0][:, 0:W], ALU.mult)
            P.ts("pool", hn[:, kc, 0:W], t[:, 0:W], Af(kc), ALU.mult, (0.0 if Bf is None else Bf(kc)), ALU.add)

    epsv = A.alloc([1], F32)
    P.memset("pool", epsv[:, :], EPS)

    def layer0():
        mL0 = A.mark()
        NKC = (2 * HALF + CTX) // 128
        KT = A.alloc([2, NKC * 128], BF16)
        VA = A.alloc([NKC, 4, 65], BF16)
        P.memset("pool", VA[:, :, :, 64:65], 1.0)
        mA = A.mark()
        wqkv = A.alloc([8, 1536], BF16)
        for kc in range(8):
            P.dma("pool", wqkv[:, kc, :], wqkv_d[kc * 128:(kc + 1) * 128, :])
        xin = [A.alloc([4, 1024], F32) for _ in range(2)]
        xT2 = [A.alloc([8, 512], F32) for _ in range(2)]
        hn = A.alloc([8, 512], BF16)
        rtab = [A.alloc([2, 512], F32) for _ in range(2)]
        qT = A.alloc([8, 512], BF16)
        sq = qT
        sqh2 = [A.alloc([512], BF16) for _ in range(2)]
        kf2 = [A.alloc([512], F32) for _ in range(2)]
        rs2 = [A.alloc([512], F32) for _ in range(2)]
        t12 = [A.alloc([512], F32) for _ in range(2)]
        t22 = [A.alloc([512], F32) for _ in range(2)]
        tmpf = [A.alloc([512], F32), t12[0], t22[0]]
        qkc = [0]

        def qknorm_rope(ps, W, wv, rt, outv):
            SUB = 9
            par = qkc[0] % 2
            qkc[0] += 1
            sqh, kf, rs, t1, t2 = sqh2[par], kf2[par], rs2[par], t12[par], t22[par]
            pssq = psum[5] if par == 0 else psum[0]
            pkp = psum[6] if par == 0 else psum[1]
            P.act(sqh[:, 0:W], ps[:, 0:W], AF.Square)
            if SUB < 2:
                return
            P.ts("dve", kf[:, 0:W], ps[:, 0:W], wv, ALU.mult)
            P.mm(pssq[:, 0:W], blk_b[:, :], sqh[:, 0:W])
            P.act(rs[:, 0:W], pssq[:, 0:W], AF.Ln, scale=1.0 / 64.0, bias=epsv[:, 0:1])
            P.act(rs[:, 0:W], rs[:, 0:W], AF.Exp, scale=-0.5)
            if SUB < 3:
                return
            if rt is not None:
                P.mm(pkp[:, 0:W], pimat[:, :], kf[:, 0:W])
                P.tt("dve", t1[:, 0:W], kf[:, 0:W], rt[:, 0, 0:W], ALU.mult)
                P.tt("dve", t2[:, 0:W], pkp[:, 0:W], rt[:, 1, 0:W], ALU.mult)
                P.tt("pool", t1[:, 0:W], t1[:, 0:W], t2[:, 0:W], ALU.add)
                P.tt("dve", outv, t1[:, 0:W], rs[:, 0:W], ALU.mult)
            else:
                P.tt("dve", outv, kf[:, 0:W], rs[:, 0:W], ALU.mult)

        tiles = [("own", i) for i in range(HALF // 512)] + [("ctx", 0)]
        for tn, (kind, ti) in enumerate(tiles):
            W = 512 if kind != "ctx" else CTX
            which = 1 if kind == "ctx" else 0
            if kind == "own":
                src = xo[ti * 512:(ti + 1) * 512, :]
                kbase = ti * 512
                hcol = ti * 512
            elif kind == "par":
                src = xp[ti * 512:(ti + 1) * 512, :]
                kbase = HALF + ti * 512
                hcol = None
            else:
                src = cx[:, :]
                kbase = 2 * HALF
                hcol = HALF
            xi = xin[tn % 2]
            xT = xT2[tn % 2]
            load_xT(src, W, xi, xT, [psum[0], psum[1]])
            rt = None
            if kind != "ctx":
                rt = rtab[tn % 2]
                rsrc = rope_o
                P.dma("sp", rt[:, :, :], rsrc[:, :, ti * 512:(ti + 1) * 512].re("a p n -> p a n"))
            if hcol is not None:
                P.dma("sp", H0[:, :, hcol:hcol + W], xT[:, :, 0:W])
            LVL = int(os.environ.get("K_LVL", "9"))
            if LVL < 2:
                continue
            norm_mod(xT, W, modAv(0, 0, which), mod(0, 0, which), hn, sq, tmpf, psum[2])
            if LVL < 3:
                continue
            for c in range(2):
                pb = psum[3 + c % 2]
                for kc in range(8):
                    P.mm(pb[:, 0:W], wqkv[:, kc, 1024 + c * 128:1024 + (c + 1) * 128], hn[:, kc, 0:W],
                         start=(kc == 0), stop=(kc == 7))
                qknorm_rope(pb, W, svv(o_qkn + 1, 1), rt, KT[:, c, kbase:kbase + W])
            for j in range(W // 128 if LVL >= 4 else 0):
                pb = psum[7]
                for kc in range(8):
                    P.mm(pb[:, 0:256], hn[:, kc, j * 128:(j + 1) * 128], wqkv[:, kc, 1280:1536],
                         start=(kc == 0), stop=(kc == 7))
                P.copy(evac_eng(), VA[:, kbase // 128 + j, :, 0:64], pb[:, 0:256].re("p (h d) -> p h d", d=64))
            if hcol is not None and LVL >= 5:
                for c in range(8):
                    pb = psum[3 + c % 2]
                    for kc in range(8):
                        P.mm(pb[:, 0:W], wqkv[:, kc, c * 128:(c + 1) * 128], hn[:, kc, 0:W],
                             start=(kc == 0), stop=(kc == 7))
                    qknorm_rope(pb, W, svv(o_qkn, 1), rt, qT[:, c, 0:W])
                P.dma("sp", QS[:, :, hcol:hcol + W], qT[:, :, 0:W])
        for c in range(2):
            P.dma("sp", KXs[c * 128:(c + 1) * 128, :], KT[:, c, 0:HALF])
        for i in range(2):
            P.dma("sp", VXs[i][:, :], VA[:, i * NVH:(i + 1) * NVH, :, :].re("p j h d -> p (j h d)"))
        grp = [[2 * i, 2 * i + 1] for i in range(NB)]
        P.add("pool", lambda e: e.collective_compute("AllGather", ALU.bypass, ins=[KX_src.ap().opt()], outs=[KX_dst.ap().opt()],
                                                     replica_groups=grp),
              reads=[KXs[:, :].key], writes=[KXd[:, :].key], cc=True)
        for i in range(2):
            P.add("pool", lambda e, i=i: e.collective_compute("AllGather", ALU.bypass, ins=[VX_src[i].ap().opt()], outs=[VX_dst[i].ap().opt()],
                                                              replica_groups=grp),
                  reads=[VXs[i][:, :].key], writes=[VXd[i][:, :].key], cc=True)
        for r in range(2):
            for c in range(2):
                P.dma("sp", KT[:, c, r * HALF:(r + 1) * HALF], KXd[(2 * r + c) * 128:(2 * r + c + 1) * 128, :])
            for i in range(2):
                P.dma("sp", VA[:, r * NOWN + i * NVH:r * NOWN + (i + 1) * NVH, :, :].re("p j h d -> p (j h d)"), VXd[i][r * 128:(r + 1) * 128, :])
        A.release(mA)

        if "B" not in PHASES:
            A.release(mL0)
            return
        wo = A.alloc([16, 1024], BF16, parts=64)
        P.dma("pool", wo[:, :, :], wo_d[:, :, :])
        for l in range(2):
            for kc in range(8):
                P.dma("pool", fwin_b[l, kc * 128:(kc + 1) * 128, :], fwin_d[l, kc * 128:(kc + 1) * 128, :])
            for oc in range(8):
                P.add("pool", lambda e, l=l, oc=oc: e.dma_start(
                          out=fwout_r.h[l, oc, :, :].rearrange("p (a b) -> p a b", b=128),
                          in_=fwout_d.h[l, :, oc * 128:(oc + 1) * 128].rearrange("(a p) b -> p a b", p=128)),
                      reads=[fwout_d[l, :, :].key], writes=[(fwout_r.tid, (l, oc))], dma=True)
            if l == 0:
                for kc in range(8):
                    P.dma("pool", hwin_b[kc * 128:(kc + 1) * 128, :], hwin_d[kc * 128:(kc + 1) * 128, :])
                    P.dma("pool", hwo_b[kc * 128:(kc + 1) * 128, :], hwo_d[kc * 128:(kc + 1) * 128, :])
        qTb = [A.alloc([8, 512], BF16) for _ in range(2)]
        xTb = [A.alloc([8, 512], F32) for _ in range(2)]
        PT = [A.alloc([2, 512], BF16) for _ in range(3)]
        osb = A.alloc([2, 512], F32)
        rinv = A.alloc([2, 512], F32)
        rb = A.alloc([2, 512], F32)
        P.memset("pool", rinv[:, :, :], 1.0)
        oT = A.alloc([16, 512], BF16)
        hmid = A.alloc([8, 512], F32)
        SCALE = 64 ** -0.5
        qtiles = [("own", i) for i in range(HALF // 512)] + [("ctx", 0)]
        for tn, (kind, ti) in enumerate(qtiles):
            W = 512 if kind == "own" else CTX
            which = 0 if kind == "own" else 1
            hcol = ti * 512 if kind == "own" else HALF
            kcs = list(range(NKC)) if kind == "own" else [NKC - 2, NKC - 1]
            qb = qTb[tn % 2]
            xb = xTb[tn % 2]
            P.dma("sp", qb[:, :, 0:W], QS[:, :, hcol:hcol + W])
            P.dma("sp", xb[:, :, 0:W], H0[:, :, hcol:hcol + W])
            step = 0
            for c in range(8):
                kvc = c // 4
                Sb = [(psum[0], psum[1]), (psum[2], psum[3]), (psum[4], psum[5])]
                Ob = (psum[6], psum[7])

                def QK(i, st):
                    kc = kcs[i]
                    sa, sbb = Sb[st % 3]
                    P.mm(sa[:, 0:W], KT[0:64, kvc, kc * 128:(kc + 1) * 128], qb[0:64, c, 0:W])
                    P.mm(sbb[:, 0:W], KT[64:128, kvc, kc * 128:(kc + 1) * 128], qb[64:128, c, 0:W])

                def EXP(i, st):
                    pt = PT[st % 3]
                    P.act(pt[:, :, 0:W], ppair(st % 3, W), AF.Exp, scale=SCALE)

                def PV(i, st):
                    kc = kcs[i]
                    pt = PT[st % 3]
                    for ab in range(2):
                        P.mm(Ob[ab][0:65, 0:W], VA[:, kc, 2 * kvc + ab, 0:65], pt[:, ab, 0:W],
                             start=(i == 0), stop=(i == len(kcs) - 1))

                n = len(kcs)
                for j in range(min(2, n)):
                    QK(j, step + j)
                for i in range(n):
                    if i + 2 < n:
                        QK(i + 2, step + i + 2)
                    EXP(i, step + i)
                    PV(i, step + i)
                step += n
                for ab in range(2):
                    P.copy("dve", osb[0:65, ab, 0:W], Ob[ab][0:65, 0:W])
                P.recip(rinv[64:65, :, 0:W], osb[64:65, :, 0:W])
                P.shuf(rb[0:32, :, 0:W], rinv[64:96, :, 0:W], [0] * 32)
                P.shuf(rb[32:64, :, 0:W], rinv[64:96, :, 0:W], [0] * 32)
                for ab in range(2):
                    P.tt("dve", oT[0:64, 2 * c + ab, 0:W], osb[0:64, ab, 0:W], rb[0:64, ab, 0:W], ALU.mult)
            for oc in range(8):
                pb = psum[oc % 6]
                for j in range(16):
                    P.mm(pb[:, 0:W], wo[0:64, j, oc * 128:(oc + 1) * 128], oT[0:64, j, 0:W], start=(j == 0), stop=(j == 15))
                P.stt(hmid[:, oc, 0:W], pb[:, 0:W], mod(0, 2, which)(oc), xb[:, oc, 0:W], ALU.mult, ALU.add)
            P.dma("sp", H1[:, :, hcol:hcol + W], hmid[:, :, 0:W])
        A.release(mL0)


    if "A" in PHASES:
        layer0()

    def ffn_phase(l, src, dst, tilespecs, final=False):
        m = A.mark()
        win = A.alloc([8, 2 * FF], BF16)
        for kc in range(8):
            P.dma("sp", win[:, kc, :], fwin_b[l, kc * 128:(kc + 1) * 128, :])
        wob = [A.alloc([22, 128], BF16) for _ in range(3)]
        h2 = [A.alloc([8, 512], F32) for _ in range(2)]
        hn2 = [A.alloc([8, 512], BF16) for _ in range(2)]
        sq = A.alloc([8, 512], BF16)
        tmpf = [A.alloc([512], F32) for _ in range(3)]
        sa = [A.alloc([512], F32) for _ in range(2)]
        sT = A.alloc([22, 512], BF16)
        if final:
            fo = sT.view(sT.h[:, 0:16, :].rearrange("p a b -> p (a b)").bitcast(F32).rearrange("p (a b) -> p a b", a=8))
        nt = len(tilespecs)
        nwo = [0]

        def load(t):
            col, W, which = tilespecs[t]
            P.dma("sp", h2[t % 2][:, :, 0:W], src[:, :, col:col + W])

        def norm(t):
            col, W, which = tilespecs[t]
            norm_mod(h2[t % 2], W, modAv(l, 1, which), mod(l, 3, which), hn2[t % 2], sq, tmpf, psum[0])

        def stage_in(t):
            col, W, which = tilespecs[t]
            hn = hn2[t % 2]
            for hc in range(22):
                pa = psum[1 + 2 * (hc % 2)]
                pu = psum[2 + 2 * (hc % 2)]
                for kc in range(8):
                    P.mm(pa[:, 0:W], win[:, kc, hc * 128:(hc + 1) * 128], hn[:, kc, 0:W], start=(kc == 0), stop=(kc == 7))
                for kc in range(8):
                    P.mm(pu[:, 0:W], win[:, kc, FF + hc * 128:FF + (hc + 1) * 128], hn[:, kc, 0:W], start=(kc == 0), stop=(kc == 7))
                s_ = sa[hc % 2]
                P.act(s_[:, 0:W], pa[:, 0:W], AF.Silu)
                P.tt("dve", sT[:, hc, 0:W], s_[:, 0:W], pu[:, 0:W], ALU.mult)

        def stage_out(t):
            col, W, which = tilespecs[t]
            h = h2[t % 2]
            for oc in range(8):
                wb = wob[nwo[0] % 3]
                nwo[0] += 1
                P.dma("sp", wb[:, :, :].re("p a b -> p (a b)"), fwout_r.s((l, oc))[l, oc, :, :])
                pb = psum[5 + oc % 2]
                for hc in range(22):
                    P.mm(pb[:, 0:W], wb[:, hc, :], sT[:, hc, 0:W], start=(hc == 0), stop=(hc == 21))
                P.stt(h[:, oc, 0:W], pb[:, 0:W], mod(l, 5, which)(oc), h[:, oc, 0:W], ALU.mult, ALU.add)
            if not final:
                P.dma("sp", dst[:, :, col:col + W], h[:, :, 0:W])
            else:
                norm_mod(h, W, lambda kc: svv(o_fnw + kc, 1), lambda kc: zero_f[:, 0:1], fo, sq, tmpf, psum[0])
                for j in range(W // 128):
                    for half in range(2):
                        pb = psum[1 + (2 * j + half) % 4]
                        for q in range(4):
                            kc = half * 4 + q
                            P.tr(pb[:, q * 128:(q + 1) * 128], fo[:, kc, j * 128:(j + 1) * 128], ident[:, :])
                        yb = sa[(2 * j + half) % 2]
                        P.copy(evac_eng(), yb[:, :], pb[:, :])
                        P.dma("sp", dst[col + j * 128:col + (j + 1) * 128, half * 512:(half + 1) * 512], yb[:, :])

        load(0)
        if nt > 1:
            load(1)
        norm(0)
        for t in range(nt):
            stage_in(t)
            if t + 1 < nt and not final:
                norm(t + 1)
            stage_out(t)
            if t + 1 < nt and final:
                norm(t + 1)
            if t + 2 < nt:
                load(t + 2)
        A.release(m)

    lat_tiles = [(i * 512, 512, 0) for i in range(HALF // 512)]
    if "F0" in PHASES:
        ffn_phase(0, H1, H2, lat_tiles + [(HALF, CTX, 1)])

    def hgrn_layer():
        mH = A.mark()
        hw = A.alloc([8, 4 * D], BF16)
        hwo = A.alloc([8, D], BF16)
        S32 = A.alloc([8, 128], F32)
        Sbf2 = [A.alloc([8, 128], BF16) for _ in range(2)]
        sc = [0]
        maskF = A.alloc([64], F32, parts=64)
        maskB = A.alloc([64], F32, parts=64)
        rmask = A.alloc([512], BF16)
        P.memset("pool", maskF[:, :], 1.0)
        P.memset("pool", maskB[:, :], 1.0)
        P.add("pool", lambda e: e.affine_select(out=maskF[:, :].ap, in_=maskF[:, :].ap, pattern=[[1, 64]],
                                                compare_op=ALU.is_ge, fill=0.0, base=0, channel_multiplier=-1),
              reads=[maskF[:, :].key], writes=[maskF[:, :].key])
        P.add("pool", lambda e: e.affine_select(out=maskB[:, :].ap, in_=maskB[:, :].ap, pattern=[[-1, 64]],
                                                compare_op=ALU.is_ge, fill=0.0, base=0, channel_multiplier=1),
              reads=[maskB[:, :].key], writes=[maskB[:, :].key])
        P.memset("pool", rmask[:, :], 1.0)
        P.memset("pool", V(rmask.h[:, :].rearrange("p (c t) -> p c t", t=64)[:, :, 0:1], rmask[:, :].key), 0.0)
        h = A.alloc([8, 512], F32)
        sq = A.alloc([8, 512], BF16)
        hn = A.alloc([8, 512], BF16)
        sg2 = [A.alloc([512], F32) for _ in range(2)]
        gT2 = [A.alloc([512], F32) for _ in range(2)]
        kT2 = [A.alloc([512], F32) for _ in range(2)]
        LT2 = [A.alloc([512], F32) for _ in range(2)]
        eL2 = [A.alloc([512], F32) for _ in range(2)]
        e22 = [A.alloc([512], F32) for _ in range(2)]
        tmpf = [gT2[0], gT2[1], kT2[0]]
        gs = sg2[0]
        eTotA = A.alloc([8, 8], F32)
        qtA = A.alloc([8, 512], BF16)
        kvA = A.alloc([16, 512], BF16)
        khA = kvA.view(kvA.h[:, 0:8, :], "kh")
        vTA = kvA.view(kvA.h[:, 8:16, :], "vT")
        Rst = kvA.view(kvA.h[:, :, :].rearrange("p a b -> p (a b)").bitcast(F32).rearrange("p (a b) -> p a b", a=8))
        ktok2 = [A.alloc([8, 128], BF16, parts=64) for _ in range(2)]
        vtok2 = [A.alloc([8, 128], BF16, parts=64) for _ in range(2)]
        Am2 = [A.alloc([8, 64], BF16, parts=64) for _ in range(2)]
        oTt = A.alloc([8, 512], F32)
        hww = hw.s("w")
        hwow = hwo.s("w")

        class HB:
            def __init__(self, fn):
                self.fn = fn

            def __getitem__(self, idx):
                _, hd, sl = idx
                return self.fn(hd, sl)

        def carve(t, a0, sub):
            key = (t.tid, sub)
            return HB(lambda hd, sl: V(t.h[:, a0 + hd // 2, (hd % 2) * 512 + sl.start:(hd % 2) * 512 + sl.stop], key))

        qtB = carve(hw, 0, "qtB") if False else HB(lambda hd, sl: V(hw.h[:, hd // 2, 3 * D + (hd % 2) * 512 + sl.start:3 * D + (hd % 2) * 512 + sl.stop], (hw.tid, "qtB")))
        khB = HB(lambda hd, sl: V(hw.h[:, 4 + hd // 2, 3 * D + (hd % 2) * 512 + sl.start:3 * D + (hd % 2) * 512 + sl.stop], (hw.tid, "khB")))
        vTB = HB(lambda hd, sl: V(hwo.h[:, hd // 2, (hd % 2) * 512 + sl.start:(hd % 2) * 512 + sl.stop], (hwo.tid, "vTB")))
        eTotB = hwo.view(hwo.h[:, 4, 0:128].bitcast(F32).rearrange("p (a b) -> p a b", a=8), "eB")
        bufs = [(qtA, khA, vTA, eTotA), (qtB, khB, vTB, eTotB)]
        pbf = [psum[i].view(psum[i].h[:, 0:512].bitcast(BF16)) for i in range(8)]

        def load_hw(blocks):
            for bi, sb in enumerate(blocks):
                for kc in range(8):
                    P.dma("sp", (hww if bi < 3 else hw)[:, kc, bi * D:(bi + 1) * D], hwin_b[kc * 128:(kc + 1) * 128, sb * D:(sb + 1) * D])

        class TileJob:
            def __init__(self, src_col, W, which, rev, emit_out, xsrc, bset, readout=False, of_col=None):
                self.src_col, self.W, self.which, self.rev, self.emit_out = src_col, W, which, rev, emit_out
                self.xsrc, self.readout, self.of_col = xsrc, readout, of_col
                self.qt, self.kh, self.vT, self.eTot = bufs[bset]
                self.nch = W // 64
                self.order = list(range(self.nch))[::-1] if rev else list(range(self.nch))
                self.mask = maskB if rev else maskF
                self.alone = False

            def load_h(self):
                W = self.W
                P.dma("sp", h[:, :, 0:W], self.xsrc[:, :, self.src_col:self.src_col + W])

            def prologue(self, load=True):
                W = self.W
                if load:
                    self.load_h()
                norm_mod(h, W, modAv(1, 0, self.which), mod(1, 0, self.which), hn, sq, tmpf, psum[7])

            def head(self, hd, standalone=False):
                W, nch, rev = self.W, self.nch, self.rev
                qt, kh, vT, eTot = self.qt, self.kh, self.vT, self.eTot
                par = hd % 2
                sg, gT, kT, LT, eL, e2 = sg2[par], gT2[par], kT2[par], LT2[par], eL2[par], e22[par]
                pz = psum[2] if (standalone and par) else psum[6]
                for kc in range(8):
                    P.mm(pz[:, 0:W], hww[:, kc, D + hd * 128:D + (hd + 1) * 128], hn[:, kc, 0:W], start=(kc == 0), stop=(kc == 7))
                P.act(sg[:, 0:W], pz[:, 0:W], AF.Sigmoid)
                P.act(gT[:, 0:W], sg[:, 0:W], AF.Ln, scale=svv(o_l1 + hd, 1), bias=svv(o_lb + hd, 1))
                P.ts("dve", kT[:, 0:W], sg[:, 0:W], svv(o_nl + hd, 1), ALU.mult, svv(o_l1 + hd, 1), ALU.add)
                P.scan(LT[:, 0:W], rmask[:, 0:W], gT[:, 0:W], 0.0, ALU.mult, ALU.add)
                LTc = V(LT.h[:, 0:W].rearrange("p (c t) -> p c t", t=64), LT[:, :].key)
                P.act(eTot[:, hd, 0:nch], V(LTc.ap[:, :, 63], LTc.key), AF.Exp)
                Lsrc = LT
                if rev:
                    P.tt("dve", gT[:, 0:W], gT[:, 0:W], LT[:, 0:W], ALU.subtract)
                    tot = V(LTc.ap[:, :, 63:64].to_broadcast([128, nch, 64]), LTc.key)
                    P.tt("dve", sg[:, 0:W].re("p (c t) -> p c t", t=64), gT[:, 0:W].re("p (c t) -> p c t", t=64), tot, ALU.add)
                    Lsrc = sg
                P.act(eL[:, 0:W], Lsrc[:, 0:W], AF.Exp)
                P.act(e2[:, 0:W], Lsrc[:, 0:W], AF.Exp, scale=-1.0)
                P.tt("dve", kh[:, hd, slice(0, W)], kT[:, 0:W], e2[:, 0:W], ALU.mult)
                pq = psum[3] if (standalone and par) else psum[7]
                for kc in range(8):
                    P.mm(pq[:, 0:W], hww[:, kc, hd * 128:(hd + 1) * 128], hn[:, kc, 0:W], start=(kc == 0), stop=(kc == 7))
                P.tt("dve", qt[:, hd, slice(0, W)], pq[:, 0:W], eL[:, 0:W], ALU.mult)
                pv = psum[2] if (standalone and par) else psum[6]
                for kc in range(8):
                    P.mm(pv[:, 0:W], hww[:, kc, 2 * D + hd * 128:2 * D + (hd + 1) * 128], hn[:, kc, 0:W], start=(kc == 0), stop=(kc == 7))
                P.copy("act", vT[:, hd, slice(0, W)], pv[:, 0:W])

            def pre(self, k):
                ci = self.order[k]
                cs = slice(ci * 64, (ci + 1) * 64)
                qt, kh, vT = self.qt, self.kh, self.vT
                ktok, vtok, Am = ktok2[k % 2], vtok2[k % 2], Am2[k % 2]
                if self.emit_out:
                    for hd in range(8):
                        P.mm(psum[4][0:64, hd * 64:(hd + 1) * 64], kh[:, hd, cs], qt[:, hd, cs])
                for hd in range(8):
                    P.tr(pbf[2][0:64, hd * 128:(hd + 1) * 128], kh[:, hd, cs], identb[:, :])
                for hd in range(8):
                    P.tr(pbf[3][0:64, hd * 128:(hd + 1) * 128], vT[:, hd, cs], identb[:, :])
                P.copy("act", ktok[0:64, :, :], pbf[2][0:64, :].re("p (h d) -> p h d", d=128))
                P.copy("dve", vtok[0:64, :, :], pbf[3][0:64, :].re("p (h d) -> p h d", d=128))
                if self.emit_out:
                    P.tt("dve", Am[0:64, :, :], psum[4][0:64, 0:512].re("p (h t) -> p h t", t=64),
                         V(self.mask.h[0:64, :].unsqueeze(1).to_broadcast([64, 8, 64]), self.mask[:, :].key), ALU.mult)

            def preU(self, k):
                ktok, vtok = ktok2[k % 2], vtok2[k % 2]
                up = 3 * (k % 2) if self.alone else 0
                for hd in range(8):
                    pb = psum[2 * up + hd // 4]
                    P.mm(pb[:, (hd % 4) * 128:(hd % 4 + 1) * 128], ktok[0:64, hd, :], vtok[0:64, hd, :])

            def post(self, k):
                ci = self.order[k]
                cs = slice(ci * 64, (ci + 1) * 64)
                qt = self.qt
                ktok, vtok, Am = ktok2[k % 2], vtok2[k % 2], Am2[k % 2]
                Scur = Sbf2[sc[0] % 2]
                Snew = Sbf2[(sc[0] + 1) % 2]
                sc[0] += 1
                up = 3 * (k % 2) if self.alone else 0
                U = V(pp[up][:, :].rearrange("p (h d) -> p h d", d=128), (psum[2 * up].tid, None), extra=((psum[2 * up + 1].tid, None),))
                P.add("dve", lambda e, U=U: e.tensor_tensor(out=S32[:, :, :].ap, in0=U.ap, in1=S32[:, :, :].ap, op=ALU.add),
                      reads=[U.key, U.extra[0], S32[:, :, :].key], writes=[S32[:, :, :].key])
                et = V(self.eTot.h[:, :, ci:ci + 1].to_broadcast([128, 8, 128]), self.eTot[:, :, :].key)
                P.tt("dve", S32[:, :, :], S32[:, :, :], et, ALU.mult)
                P.copy("act", Snew[:, :, :], S32[:, :, :])
                if self.emit_out:
                    for hd in range(8):
                        P.mm(psum[5][:, hd * 64:(hd + 1) * 64], Scur[:, hd, :], qt[:, hd, cs], start=True, stop=False)
                        P.mm(psum[5][:, hd * 64:(hd + 1) * 64], vtok[0:64, hd, :], Am[0:64, hd, :], start=False, stop=True)
                    po = psum[5][:, 0:512].re("p (h t) -> p h t", t=64)
                    if self.of_col is None:
                        P.copy("act", oTt[:, :, cs], po)
                    else:
                        P.tt("dve", oTt[:, :, cs], po, oTt[:, :, cs], ALU.add)

            def epilogue(self):
                if not self.readout:
                    return
                qt = self.qt
                og = qt
                P.dma("sp", Rst[:, :, :], self.xsrc[:, :, self.src_col:self.src_col + 512])
                def epA(hd):
                    par = hd % 2
                    pss = psum[1] if par else psum[5]
                    pg = psum[7] if par else psum[6]
                    P.act(sq[:, hd, :], oTt[:, hd, :], AF.Square)
                    P.mm(pss[:, :], ones_b[:, :], sq[:, hd, :])
                    for kc in range(8):
                        P.mm(pg[:, :], hw[:, kc, 3 * D + hd * 128:3 * D + (hd + 1) * 128], hn[:, kc, :], start=(kc == 0), stop=(kc == 7))
                    P.act(sg2[par][:, :], pg[:, :], AF.Sigmoid)

                def epB(hd):
                    par = hd % 2
                    r0, r1, gsp = gT2[par], kT2[par], sg2[par]
                    pss = psum[1] if par else psum[5]
                    P.act(r0[:, :], pss[:, :], AF.Ln, scale=1.0 / 128.0, bias=epsv[:, 0:1])
                    P.act(r0[:, :], r0[:, :], AF.Exp, scale=-0.5)
                    P.stt(r1[:, :], oTt[:, hd, :], svv(o_onw + hd, 1), r0[:, :], ALU.mult, ALU.mult)
                    P.tt("dve", og[:, hd, slice(0, 512)], r1[:, :], gsp[:, :], ALU.mult)

                epA(0)
                for hd in range(8):
                    if hd + 1 < 8:
                        epA(hd + 1)
                    epB(hd)
                for oc in range(8):
                    pb = psum[2 + oc % 2]
                    for kc in range(8):
                        P.mm(pb[:, :], hwow[:, kc, oc * 128:(oc + 1) * 128], og[:, kc, slice(0, 512)], start=(kc == 0), stop=(kc == 7))
                    P.stt(Rst[:, oc, :], pb[:, :], mod(1, 2, 0)(oc), Rst[:, oc, :], ALU.mult, ALU.add)
                P.dma("sp", H3[:, :, self.src_col:self.src_col + 512], Rst[:, :, :])

        def run_chunks(job, nxt):
            job.alone = nxt is None
            n = job.nch
            job.pre(0)
            job.preU(0)
            for k in range(n):
                if k + 1 < n:
                    job.pre(k + 1)
                job.post(k)
                if nxt is not None and k < 8:
                    if k == 0:
                        nxt.prologue()
                    nxt.head(k)
                if k + 1 < n:
                    job.preU(k + 1)
            if nxt is not None:
                for hd in range(n, 8):
                    if n == 0:
                        nxt.prologue()
                    nxt.head(hd)

        load_hw([0, 2, 4])
        P.memset("dve", S32[:, :, :], 0.0)
        P.memset("dve", Sbf2[0][:, :, :], 0.0)
        jobs = [TileJob(HALF, CTX, 1, False, False, H2, 0)]
        for ti in range(HALF // 512):
            jobs.append(TileJob(ti * 512, 512, 0, False, True, H2, (ti + 1) % 2))
        jobs[0].prologue()
        for hd in range(8):
            jobs[0].head(hd, standalone=True)
        for ji, job in enumerate(jobs):
            nxt = jobs[ji + 1] if ji + 1 < len(jobs) else None
            run_chunks(job, nxt)
            if job.emit_out:
                P.dma("sp", OF[:, :, job.src_col:job.src_col + 512], oTt[:, :, :])
        P.dma("sp", SXs[:, :], S32[:, :, :].re("p h d -> p (h d)"))
        P.add("pool", lambda e: e.collective_compute("AllGather", ALU.bypass, ins=[SX_src.ap().opt()], outs=[SX_dst.ap().opt()],
                                                     replica_groups=[[2 * i, 2 * i + 1] for i in range(NB)]),
              reads=[SXs[:, :].key], writes=[SXd[:, :].key], cc=True)
        load_hw([0, 3, 4, 1])
        for kc in range(8):
            P.dma("sp", hwo[:, kc, :], hwo_b[kc * 128:(kc + 1) * 128, :])
        g0 = h[:, 0:2, :].re("p a b -> p (a b)")
        g1 = oTt[:, 0:2, :].re("p a b -> p (a b)")
        P.dma("sp", g0, SXd[0:128, :])
        P.dma("sp", g1, SXd[128:256, :])
        P.ts("dve", g0, g0, parw[:, 0:1], ALU.mult)
        P.stt(S32[:, :, :].re("p h d -> p (h d)"), g1, parw[:, 1:2], g0, ALU.mult, ALU.add)
        P.copy("act", Sbf2[sc[0] % 2][:, :, :], S32[:, :, :])
        tis = list(reversed(range(HALF // 512)))
        jobs2 = [TileJob(ti * 512, 512, 0, True, True, H2, 0, readout=True, of_col=ti * 512) for ti in tis]
        jobs2[0].load_h()
        for ji, job in enumerate(jobs2):
            job.prologue(load=False)
            P.dma("sp", oTt[:, :, :], OF[:, :, job.src_col:job.src_col + 512])
            if ji + 1 < len(jobs2):
                jobs2[ji + 1].load_h()
            for hd in range(8):
                job.head(hd, standalone=True)
            run_chunks(job, None)
            job.epilogue()
        A.release(mH)

    if "H" in PHASES:
        hgrn_layer()

    if "F1" in PHASES:
        ffn_phase(1, H3, yout, lat_tiles, final=True)

    if dbg_fn is not None:
        dbg_fn(locals())

    P.barrier()
    P.emit()
    return nc, P


_CACHE = {}


def _rope_tables(pos):
    inv = (np.float32(10000.0) ** (-(np.arange(16, dtype=np.float32) * np.float32(2.0) / np.float32(32.0)))).astype(np.float32)
    row = (pos // 64).astype(np.float32)
    col = (pos % 64).astype(np.float32)
    ar = row[:, None] * inv[None, :]
    ac = col[:, None] * inv[None, :]
    cr, sr, cc, sc = np.cos(ar), np.sin(ar), np.cos(ac), np.sin(ac)
    C = np.concatenate([cr, cr, cc, cc], axis=1).T.astype(np.float32)
    S = np.concatenate([-sr, sr, -sc, sc], axis=1).T.astype(np.float32)
    C = np.concatenate([C, C], axis=0)
    S = np.concatenate([S, S], axis=0)
    return np.ascontiguousarray(np.stack([C, S], axis=0))


def _fm(v):
    return np.ascontiguousarray(np.asarray(v).reshape(8, 128).T)


def kernel(x, c, ctx, c_ctx, ada_w, ada_b, norm_mix_w, norm_ffn_w, attn_w_qkv, attn_q_norm,
           attn_k_norm, attn_w_o, hgrn_w_in, hgrn_lb_logits, hgrn_out_norm, hgrn_w_o,
           ffn_w_in, ffn_w_out, final_norm_w):
    in_maps, gather = make_inputs(x, c, ctx, c_ctx, ada_w, ada_b, norm_mix_w, norm_ffn_w, attn_w_qkv, attn_q_norm,
                                  attn_k_norm, attn_w_o, hgrn_w_in, hgrn_lb_logits, hgrn_out_norm, hgrn_w_o,
                                  ffn_w_in, ffn_w_out, final_norm_w)
    if "nc" not in _CACHE:
        _CACHE["nc"] = build_program()[0]
    nc = _CACHE["nc"]
    res = run_bass_kernel_spmd(nc, in_maps, core_ids=list(range(2 * NB)))
    return gather([r["y"] for r in res.results])


def make_inputs(x, c, ctx, c_ctx, ada_w, ada_b, norm_mix_w, norm_ffn_w, attn_w_qkv, attn_q_norm,
                attn_k_norm, attn_w_o, hgrn_w_in, hgrn_lb_logits, hgrn_out_norm, hgrn_w_o,
                ffn_w_in, ffn_w_out, final_norm_w):
    f = lambda a: np.ascontiguousarray(np.asarray(a, dtype=np.float32))
    x, c, ctx, c_ctx = f(x), f(c), f(ctx), f(c_ctx)
    idx0 = np.arange(HALF)
    idx1 = SEQ - 1 - np.arange(HALF)
    rope = [_rope_tables(idx0), _rope_tables(idx1)]
    pim = np.zeros((128, 128), np.float32)
    for m in range(128):
        d = m % 64
        pi = d + 16 if (d % 32) < 16 else d - 16
        pim[(m // 64) * 64 + pi, m] = 1.0
    wqkv = f(attn_w_qkv)[0]
    qcols = np.concatenate([np.arange(h * 64, (h + 1) * 64) for h in HEAD_ORDER])
    wqkv_dev = np.ascontiguousarray(np.concatenate([wqkv[:, qcols], wqkv[:, 1024:]], axis=1))
    wo = f(attn_w_o)[0]
    wo_dev = np.ascontiguousarray(np.stack([wo[h * 64:(h + 1) * 64, :] for h in HEAD_ORDER], axis=1))
    qkn = np.ascontiguousarray(np.stack([np.tile(f(attn_q_norm)[0], 2), np.tile(f(attn_k_norm)[0], 2)], axis=1))
    hwin = f(hgrn_w_in)[0]
    hwin_sw = np.ascontiguousarray(np.concatenate([hwin[:, :2 * D], hwin[:, 3 * D:4 * D], hwin[:, 2 * D:3 * D], hwin[:, 4 * D:]], axis=1))
    adab = np.ascontiguousarray(np.stack([f(ada_b)[l].reshape(48, 128).T for l in range(2)], axis=1))
    nmw = np.ascontiguousarray(np.stack([_fm(f(norm_mix_w)[l]) for l in range(2)], axis=1))
    nfw = np.ascontiguousarray(np.stack([_fm(f(norm_ffn_w)[l]) for l in range(2)], axis=1))
    lbl = np.ascontiguousarray(np.stack([_fm(f(hgrn_lb_logits)[l]) for l in range(2)], axis=1))
    common = {
        "pimat": pim, "ada_w": f(ada_w), "ada_b": adab, "nmw": nmw, "nfw": nfw, "fnw": _fm(f(final_norm_w)),
        "wqkv": wqkv_dev, "qkn": qkn, "wo": wo_dev, "lbl": lbl, "onw": _fm(f(hgrn_out_norm)[0]),
        "hwo": f(hgrn_w_o)[0], "fwin": f(ffn_w_in), "fwout": f(ffn_w_out),
    }
    in_maps = []
    for core in range(2 * NB):
        b, s = core // 2, core % 2
        own = idx0 if s == 0 else idx1
        par = idx1 if s == 0 else idx0
        m = dict(common)
        m["xo"] = np.ascontiguousarray(x[b][own])
        m["cx"] = np.ascontiguousarray(ctx[b] if s == 0 else ctx[b][::-1])
        m["cvec"] = np.ascontiguousarray(np.stack([_fm(c[b]), _fm(c_ctx)], axis=2))
        m["rope_o"] = rope[s]
        m["hwin"] = hwin if s == 0 else hwin_sw
        m["parw"] = np.ascontiguousarray(np.tile(np.array([[0.0, 1.0]] if s == 0 else [[1.0, 0.0]], np.float32), (128, 1)))
        in_maps.append(m)

    def gather(ys):
        out = np.empty((NB, SEQ, D), np.float32)
        for core in range(2 * NB):
            b, s = core // 2, core % 2
            own = idx0 if s == 0 else idx1
            out[b][own] = ys[core]
        return out

    return in_maps, gather
```

```python
import os
import numpy as np
import concourse.bass as bass
import concourse.mybir as mybir
from concourse.bass_utils import run_bass_kernel_spmd

F32 = mybir.dt.float32
BF16 = mybir.dt.bfloat16
AF = mybir.ActivationFunctionType
ALU = mybir.AluOpType

D = 1024
SEQ = 8192
HALF = 4096
CTX = 256
NB = 4
FF = 2816
EPS = 1e-6
HEAD_ORDER = [0, 4, 1, 5, 2, 6, 3, 7, 8, 12, 9, 13, 10, 14, 11, 15]
ENGS = ["pe", "act", "dve", "pool", "sp"]
DMAQ = ("sp", "pool")
NDS = 8


class V:
    __slots__ = ("ap", "key", "extra")

    def __init__(self, ap, key, extra=()):
        self.ap = ap
        self.key = key
        self.extra = extra

    def bc(self, shape):
        return V(self.ap.to_broadcast(list(shape)), self.key)

    def re(self, pat, **kw):
        return V(self.ap.rearrange(pat, **kw), self.key)


class Tile:
    def __init__(self, h, tid, sub=None):
        self.h = h
        self.tid = tid
        self.sub = sub

    def __getitem__(self, idx):
        return V(self.h[idx], (self.tid, self.sub))

    def s(self, sub):
        return Tile(self.h, self.tid, sub)

    def view(self, ap, sub=None):
        return Tile(ap, self.tid, sub)


class Op:
    __slots__ = ("fn", "deps", "sig", "dma", "dsem", "dval", "cnt", "cc")

    def __init__(self, fn, deps, dma, cc=False):
        self.fn = fn
        self.deps = deps
        self.sig = False
        self.dma = dma
        self.dsem = None
        self.dval = 0
        self.cnt = 0
        self.cc = cc


class Prog:
    def __init__(self, nc):
        self.nc = nc
        self.ops = {e: [] for e in ENGS}
        self.last_w = {}
        self.readers = {}
        self.subs = {}
        self.ntid = 0
        self.ndma = {q: [] for q in DMAQ}
        self.dma_since_barrier = []
        self.last_real = {}
        self.excl = set()

    def newtid(self):
        self.ntid += 1
        return self.ntid

    def _conf(self, key):
        tid, sub = key
        ss = self.subs.setdefault(tid, set())
        ss.add(sub)
        if sub is None:
            return [(tid, s) for s in ss]
        return [(tid, sub), (tid, None)] if None in ss else [(tid, sub)]

    def add(self, eng, fn, reads=(), writes=(), dma=False, cc=False):
        idx = len(self.ops[eng])
        deps = set()
        xr = [k for k in reads if k[0] in self.excl]
        if xr:
            reads = [k for k in reads if k[0] not in self.excl]
            writes = list(writes) + xr
        for k in reads:
            for ck in self._conf(k):
                w = self.last_w.get(ck)
                if w is not None:
                    deps.add(w)
        for k in writes:
            for ck in self._conf(k):
                w = self.last_w.get(ck)
                if w is not None:
                    deps.add(w)
                for r in self.readers.get(ck, ()):
                    deps.add(r)
        if cc:
            self.dma_since_barrier.append((eng, idx))
        elif dma:
            lst = self.ndma[eng]
            if len(lst) >= NDS:
                deps.add((eng, lst[len(lst) - NDS]))
            lst.append(idx)
            self.dma_since_barrier.append((eng, idx))
        deps.discard((eng, idx))
        if eng == "pe":
            deps = {d for d in deps if d[0] != "pe"}
        op = Op(fn, deps, dma or cc, cc)
        self.ops[eng].append(op)
        self.last_real[eng] = idx
        me = (eng, idx)
        for k in writes:
            if k[1] is None:
                for ck in self._conf(k):
                    self.last_w[ck] = me
                    self.readers[ck] = []
            else:
                self.last_w[k] = me
                self.readers[k] = []
        for k in reads:
            rl = self.readers.setdefault(k, [])
            if not dma:
                rl[:] = [r for r in rl if r[0] != eng or self.ops[r[0]][r[1]].dma]
            rl.append(me)
        return me

    def barrier(self):
        lasts = [(e, i) for e, i in self.last_real.items()]
        dmas = list(self.dma_since_barrier)
        self.dma_since_barrier = []
        for e in ENGS:
            deps = set(lasts) | set(dmas)
            op = Op(None, deps, False)
            self.ops[e].append(op)

    def mm(self, out, lhsT, rhs, start=True, stop=True):
        self.add("pe", lambda e: e.matmul(out.ap, lhsT=lhsT.ap, rhs=rhs.ap, start=start, stop=stop),
                 reads=[lhsT.key, rhs.key], writes=[out.key])

    def tr(self, out, in_, ident):
        self.add("pe", lambda e: e.transpose(out.ap, in_.ap, ident.ap), reads=[in_.key, ident.key], writes=[out.key])

    def act(self, out, in_, func, scale=1.0, bias=0.0):
        reads = [in_.key] + list(in_.extra)
        sc = scale
        bi = bias
        if isinstance(scale, V):
            reads.append(scale.key)
            sc = scale.ap
        if isinstance(bias, V):
            reads.append(bias.key)
            bi = bias.ap
        self.add("act", lambda e: e.activation(out=out.ap, in_=in_.ap, func=func, bias=bi, scale=sc),
                 reads=reads, writes=[out.key])

    def copy(self, eng, out, in_):
        if eng == "act":
            self.add("act", lambda e: e.copy(out=out.ap, in_=in_.ap), reads=[in_.key], writes=[out.key])
        else:
            self.add(eng, lambda e: e.tensor_copy(out=out.ap, in_=in_.ap), reads=[in_.key], writes=[out.key])

    def tt(self, eng, out, in0, in1, op):
        self.add(eng, lambda e: e.tensor_tensor(out=out.ap, in0=in0.ap, in1=in1.ap, op=op),
                 reads=[in0.key, in1.key], writes=[out.key])

    def ts(self, eng, out, in0, s1, op0, s2=None, op1=None):
        reads = [in0.key]
        a1 = s1
        a2 = s2
        if isinstance(s1, V):
            reads.append(s1.key)
            a1 = s1.ap
        if isinstance(s2, V):
            reads.append(s2.key)
            a2 = s2.ap
        if op1 is None:
            self.add(eng, lambda e: e.tensor_scalar(out=out.ap, in0=in0.ap, scalar1=a1, scalar2=None, op0=op0),
                     reads=reads, writes=[out.key])
        else:
            self.add(eng, lambda e: e.tensor_scalar(out=out.ap, in0=in0.ap, scalar1=a1, scalar2=a2, op0=op0, op1=op1),
                     reads=reads, writes=[out.key])

    def stt(self, out, in0, scalar, in1, op0, op1):
        reads = [in0.key, in1.key]
        sc = scalar
        if isinstance(scalar, V):
            reads.append(scalar.key)
            sc = scalar.ap
        self.add("dve", lambda e: e.scalar_tensor_tensor(out=out.ap, in0=in0.ap, scalar=sc, in1=in1.ap, op0=op0, op1=op1),
                 reads=reads, writes=[out.key])

    def scan(self, out, d0, d1, initial, op0, op1):
        self.add("dve", lambda e: e.tensor_tensor_scan(out=out.ap, data0=d0.ap, data1=d1.ap, initial=initial, op0=op0, op1=op1),
                 reads=[d0.key, d1.key], writes=[out.key])

    def recip(self, out, in_):
        self.add("dve", lambda e: e.reciprocal(out=out.ap, in_=in_.ap), reads=[in_.key], writes=[out.key])

    def shuf(self, out, in_, mask):
        self.add("dve", lambda e: e.stream_shuffle(out=out.ap, in_=in_.ap, mask=mask), reads=[in_.key], writes=[out.key])

    def memset(self, eng, out, val):
        self.add(eng, lambda e: e.memset(out.ap, val), writes=[out.key])

    def dma(self, q, out, in_):
        if q == "pool":
            self.add(q, lambda e: e.dma_start(out=out.ap, in_=in_.ap, max_dma_last_dim=4096), reads=[in_.key], writes=[out.key], dma=True)
        else:
            self.add(q, lambda e: e.dma_start(out=out.ap, in_=in_.ap), reads=[in_.key], writes=[out.key], dma=True)

    def emit(self, extra_ctx=None):
        nc = self.nc
        ops = self.ops
        for e in ENGS:
            for op in ops[e]:
                for (e2, i2) in op.deps:
                    ops[e2][i2].sig = True
        for e in ENGS:
            c = 0
            for op in ops[e]:
                if op.sig and not op.dma:
                    c += 1
                op.cnt = c
        from contextlib import ExitStack
        with ExitStack() as es:
            sems = {e: es.enter_context(nc.semaphore("s_" + e)) for e in ENGS}
            dsems = {q: [es.enter_context(nc.semaphore("d_%s%d" % (q, j))) for j in range(NDS)] for q in DMAQ}
            for q in DMAQ:
                for n, idx in enumerate(self.ndma[q]):
                    op = ops[q][idx]
                    op.dsem = dsems[q][n % NDS]
                    op.dval = 16 * (n // NDS + 1)
            ncc = 0
            for e in ENGS:
                for op in ops[e]:
                    if op.cc:
                        op.dsem = es.enter_context(nc.semaphore("ccs%d" % ncc))
                        ncc += 1
                        op.dval = 1
            block = es.enter_context(nc.Block())
            self.nwaits = 0

            def run(ename, eng):
                known = {}
                for idx, op in enumerate(ops[ename]):
                    need = {}
                    for (e2, i2) in op.deps:
                        o2 = ops[e2][i2]
                        if o2.dma:
                            sem, val = o2.dsem, o2.dval
                        else:
                            sem, val = sems[e2], o2.cnt
                        k = id(sem)
                        if known.get(k, 0) >= val:
                            continue
                        if k not in need or need[k][1] < val:
                            need[k] = (sem, val)
                    for k, (sem, val) in need.items():
                        eng.wait_ge(sem, val)
                        known[k] = val
                        self.nwaits += 1
                    if op.fn is None:
                        assert not op.sig
                        continue
                    ins = op.fn(eng)
                    if op.cc:
                        ins.then_inc(op.dsem)
                    elif op.dma:
                        ins.then_inc(op.dsem, 16)
                    elif op.sig:
                        ins.then_inc(sems[ename], 1)

            @block.tensor
            def _(t):
                run("pe", t)

            @block.scalar
            def _(a):
                run("act", a)

            @block.vector
            def _(v):
                run("dve", v)

            @block.gpsimd
            def _(g):
                run("pool", g)

            @block.sync
            def _(s):
                run("sp", s)


class Arena:
    def __init__(self, nc, P, nbytes):
        self.P = P
        self.nbytes = nbytes
        self.h = nc.alloc_sbuf_tensor("arena", [128, nbytes // 4], F32)
        self.off = 0

    def alloc(self, shape, dtype, parts=128):
        n = 1
        for x in shape:
            n *= x
        esz = 4 if dtype == F32 else 2
        nb = (n * esz + 63) // 64 * 64
        assert self.off + nb <= self.nbytes, ("SBUF arena overflow", self.off, nb, self.nbytes)
        w0 = self.off // 4
        self.off += nb
        ap = self.h[0:parts, w0:w0 + nb // 4]
        if dtype != F32:
            ap = ap.bitcast(dtype)
        ap = ap[:, 0:n]
        if len(shape) == 2:
            ap = ap.rearrange("p (a b) -> p a b", a=shape[0])
        elif len(shape) == 3:
            ap = ap.rearrange("p (a b c) -> p a b c", a=shape[0], b=shape[1])
        elif len(shape) == 4:
            ap = ap.rearrange("p (a b c d) -> p a b c d", a=shape[0], b=shape[1], c=shape[2])
        return Tile(ap, self.P.newtid())

    def mark(self):
        return self.off

    def release(self, m):
        self.P.barrier()
        self.off = m


def build_program(PHASES=("A", "B", "F0", "H", "F1"), dbg_fn=None):
    nc = bass.Bass("TRN2", target_bir_lowering=False)
    P = Prog(nc)

    def din(name, shape, dt=F32):
        return Tile(nc.dram_tensor(name, list(shape), dt, kind="ExternalInput").ap(), P.newtid())

    def dscr(name, shape, dt=F32):
        if os.environ.get("K_DBG") == "1":
            return Tile(nc.dram_tensor(name, list(shape), dt, kind="ExternalOutput").ap(), P.newtid())
        return Tile(nc.dram_tensor(name, list(shape), dt), P.newtid())

    xo = din("xo", [HALF, D])
    cx = din("cx", [CTX, D])
    cvec = din("cvec", [128, 8, 2])
    rope_o = din("rope_o", [2, 128, HALF])
    pimat_d = din("pimat", [128, 128])
    ada_w = din("ada_w", [2, D, 6 * D])
    ada_b = din("ada_b", [128, 2, 48])
    nmw_d = din("nmw", [128, 2, 8])
    nfw_d = din("nfw", [128, 2, 8])
    fnw_d = din("fnw", [128, 8])
    wqkv_d = din("wqkv", [D, 1536])
    qkn_d = din("qkn", [128, 2])
    wo_d = din("wo", [64, 16, D])
    hwin_d = din("hwin", [D, 5 * D])
    lbl_d = din("lbl", [128, 2, 8])
    onw_d = din("onw", [128, 8])
    hwo_d = din("hwo", [D, D])
    fwin_d = din("fwin", [2, D, 2 * FF])
    fwout_d = din("fwout", [2, FF, D])
    yout = Tile(nc.dram_tensor("y", [HALF, D], F32, kind="ExternalOutput").ap(), P.newtid())

    NTOK = HALF + CTX
    H0 = dscr("H0", [128, 8, NTOK])
    QS = dscr("QS", [128, 8, NTOK], BF16)
    H1 = dscr("H1", [128, 8, NTOK])
    H2 = dscr("H2", [128, 8, NTOK])
    OF = dscr("OF", [128, 8, HALF])
    H3 = dscr("H3", [128, 8, HALF])
    if os.environ.get("K_DBG") == "1":
        H4 = dscr("H4", [128, 8, HALF])
        H5 = dscr("H5", [128, 8, HALF])
    NOWN = HALF // 128
    fwin_b = dscr("fwin_b", [2, D, 2 * FF], BF16)
    fwout_r = dscr("fwout_r", [2, 8, 128, 22 * 128], BF16)
    hwin_b = dscr("hwin_b", [D, 5 * D], BF16)
    hwo_b = dscr("hwo_b", [D, D], BF16)
    KX_src = nc.dram_tensor("KXs", [256, HALF], BF16)
    KX_dst = nc.dram_tensor("KXd", [512, HALF], BF16)
    NVH = NOWN // 2
    VX_src = [nc.dram_tensor("VXs%d" % i, [128, NVH * 260], BF16) for i in range(2)]
    VX_dst = [nc.dram_tensor("VXd%d" % i, [256, NVH * 260], BF16) for i in range(2)]
    KXs, KXd = Tile(KX_src, P.newtid()), Tile(KX_dst, P.newtid())
    VXs = [Tile(t_, P.newtid()) for t_ in VX_src]
    VXd = [Tile(t_, P.newtid()) for t_ in VX_dst]
    SX_src = nc.dram_tensor("SXs", [128, 1024], F32)
    SX_dst = nc.dram_tensor("SXd", [256, 1024], F32)
    SXs = Tile(SX_src, P.newtid())
    SXd = Tile(SX_dst, P.newtid())
    parw_d = din("parw", [128, 2])

    A = Arena(nc, P, 207 * 1024)
    pp = [nc.alloc_psum_tensor("pp%d" % i, [128, 1024], F32) for i in range(4)]
    psum = [Tile(pp[i // 2][:, (i % 2) * 512:(i % 2 + 1) * 512], P.newtid()) for i in range(8)]

    def ppair(k, W):
        return V(pp[k][:, :].rearrange("p (a b) -> p a b", a=2)[:, :, 0:W], (psum[2 * k].tid, None), extra=((psum[2 * k + 1].tid, None),))
    for t_ in psum:
        P.excl.add(t_.tid)

    ident = A.alloc([128], F32)
    identb = A.alloc([128], BF16)
    ones_b = A.alloc([128], BF16)
    blk_b = A.alloc([128], BF16)
    ones_f = A.alloc([128], F32)
    pimat = A.alloc([128], F32)
    zero_f = A.alloc([128], F32)
    P.memset("pool", zero_f[:, :], 0.0)
    P.memset("pool", ones_f[:, :], 1.0)
    P.add("pool", lambda e: e.affine_select(out=ident[:, :].ap, in_=zero_f[:, :].ap, pattern=[[-1, 128]],
                                            compare_op=ALU.not_equal, fill=1.0, base=0, channel_multiplier=1),
          reads=[zero_f[:, :].key], writes=[ident[:, :].key])
    P.copy("pool", identb[:, :], ident[:, :])
    P.copy("pool", ones_b[:, :], ones_f[:, :])
    P.memset("pool", blk_b[:, :], 0.0)
    P.memset("pool", blk_b[0:64, 0:64], 1.0)
    P.memset("pool", blk_b[64:128, 64:128], 1.0)
    P.dma("sp", pimat[:, :], pimat_d[:, :])

    smallv = A.alloc([256], F32)
    sv_off = [0]

    def small(n):
        o = sv_off[0]
        sv_off[0] += n
        assert sv_off[0] <= 256
        return o

    o_cv = small(16)
    o_nmw = small(16)
    o_nfw = small(16)
    o_fnw = small(8)
    o_qkn = small(2)
    o_lbl = small(16)
    o_onw = small(8)
    o_lb = small(8)
    o_l1 = small(8)
    o_nl = small(8)
    o_csil = small(16)
    o_parw = small(2)
    sv = smallv

    def svv(o, n):
        return sv[:, o:o + n]

    P.dma("sp", svv(o_cv, 16), cvec[:, :, :].re("p a b -> p (a b)"))
    P.dma("sp", svv(o_nmw, 16), nmw_d[:, :, :].re("p a b -> p (a b)"))
    P.dma("sp", svv(o_nfw, 16), nfw_d[:, :, :].re("p a b -> p (a b)"))
    P.dma("sp", svv(o_fnw, 8), fnw_d[:, :])
    P.dma("sp", svv(o_qkn, 2), qkn_d[:, :])
    P.dma("sp", svv(o_lbl, 16), lbl_d[:, :, :].re("p a b -> p (a b)"))
    P.dma("sp", svv(o_onw, 8), onw_d[:, :])
    P.dma("sp", svv(o_parw, 2), parw_d[:, :])
    parw = sv.view(sv.h[:, o_parw:o_parw + 2])
    adab = A.alloc([96], F32)
    P.dma("sp", adab[:, :], ada_b[:, :, :].re("p a b -> p (a b)"))
    lbe = A.alloc([24], F32)
    P.act(lbe[:, 0:16], svv(o_lbl, 16), AF.Exp)
    P.tt("dve", lbe[:, 16:24], lbe[:, 0:8], lbe[:, 8:16], ALU.add)
    P.recip(lbe[:, 16:24], lbe[:, 16:24])
    P.tt("dve", svv(o_lb, 8), lbe[:, 8:16], lbe[:, 16:24], ALU.mult)
    P.ts("dve", svv(o_l1, 8), svv(o_lb, 8), -1.0, ALU.mult, 1.0, ALU.add)
    P.ts("dve", svv(o_nl, 8), svv(o_l1, 8), -1.0, ALU.mult)
    P.act(svv(o_csil, 16), svv(o_cv, 16), AF.Silu)

    modv = A.alloc([2, 48, 2], F32)
    modA = A.alloc([2, 2, 8, 2], F32)
    m0 = A.mark()
    wblk = [A.alloc([8, 512], F32) for _ in range(2)]
    modrow = A.alloc([6 * D], F32, parts=2)
    csil = svv(o_csil, 16)
    n_ada = 0
    for l in range(2):
        for blk in range(12):
            wb = wblk[n_ada % 2]
            n_ada += 1
            P.dma("sp", wb[:, :, :], ada_w[l, :, blk * 512:(blk + 1) * 512].re("(kc p) n -> p kc n", p=128))
            prow = psum[2 + blk % 2]
            for kc in range(8):
                P.mm(prow[0:2, :], V(sv.h[:, o_csil + kc * 2:o_csil + kc * 2 + 2], csil.key), wb[:, kc, :],
                     start=(kc == 0), stop=(kc == 7))
            P.copy("act" if blk % 2 else "dve", modrow[0:2, blk * 512:(blk + 1) * 512], prow[0:2, :])
        for jj in range(48):
            P.tr(psum[l][:, jj * 2:jj * 2 + 2], modrow[0:2, jj * 128:(jj + 1) * 128], ident[0:2, 0:2])
        P.tt("dve", modv[:, l, :, :], psum[l][:, 0:96].re("p (a b) -> p a b", b=2),
             adab[:, l * 48:(l + 1) * 48].re("p (a b) -> p a b", b=1).bc([128, 48, 2]), ALU.add)
        for nrm, (jsc, ow) in enumerate(((8, o_nmw), (32, o_nfw))):
            wv = sv[:, ow + l * 8:ow + l * 8 + 8].re("p (a b) -> p a b", b=1).bc([128, 8, 2])
            P.stt(modA[:, l, nrm, :, :], modv[:, l, jsc:jsc + 8, :], 1.0, wv, ALU.add, ALU.mult)
    A.release(m0)

    def mod(l, kind, which):
        return lambda kc: modv[:, l, kind * 8 + kc, which:which + 1]

    def modAv(l, nrm, which):
        return lambda kc: modA[:, l, nrm, kc, which:which + 1]

    rr = [0]

    def evac_eng():
        rr[0] += 1
        return "act" if rr[0] % 2 else "dve"

    def load_xT(src_rows, W, xin, xT, pbanks):
        nj = W // 128
        P.dma("sp", xin[:, 0:nj, :], src_rows.re("(j p) d -> p j d", p=128))
        for kc in range(8):
            pb = pbanks[kc % len(pbanks)]
            for j in range(nj):
                P.tr(pb[:, j * 128:(j + 1) * 128], xin[:, j, kc * 128:(kc + 1) * 128], ident[:, :])
            P.copy(evac_eng(), xT[:, kc, 0:W], pb[:, 0:W])

    def norm_mod(xT, W, Af, Bf, hn, sq, tmpf, pbank, nfeat=1024.0):
        for kc in range(8):
            P.act(sq[:, kc, 0:W], xT[:, kc, 0:W], AF.Square)
        for kc in range(8):
            P.mm(pbank[:, 0:W], ones_b[:, :], sq[:, kc, 0:W], start=(kc == 0), stop=(kc == 7))
        P.act(tmpf[0][:, 0:W], pbank[:, 0:W], AF.Ln, scale=1.0 / nfeat, bias=epsv[:, 0:1])
        P.act(tmpf[0][:, 0:W], tmpf[0][:, 0:W], AF.Exp, scale=-0.5)
        for kc in range(8):
            t = tmpf[1 + kc % 2]
            P.tt("dve", t[:, 0:W], xT[:, kc, 0:W], tmpf[0][:, 0:W], ALU.mult)
            P.ts("pool", hn[:, kc, 0:W], t[:, 0:W], Af(kc), ALU.mult, (0.0 if Bf is None else Bf(kc)), ALU.add)

    epsv = A.alloc([1], F32)
    P.memset("pool", epsv[:, :], EPS)

    def layer0():
        mL0 = A.mark()
        NKC = (2 * HALF + CTX) // 128
        KT = A.alloc([2, NKC * 128], BF16)
        VA = A.alloc([NKC, 4, 65], BF16)
        P.memset("pool", VA[:, :, :, 64:65], 1.0)
        mA = A.mark()
        wqkv = A.alloc([8, 1536], BF16)
        for kc in range(8):
            P.dma("pool", wqkv[:, kc, :], wqkv_d[kc * 128:(kc + 1) * 128, :])
        xin = [A.alloc([4, 1024], F32) for _ in range(2)]
        xT2 = [A.alloc([8, 512], F32) for _ in range(2)]
        hn = A.alloc([8, 512], BF16)
        rtab = [A.alloc([2, 512], F32) for _ in range(2)]
        qT = A.alloc([8, 512], BF16)
        sq = qT
        sqh2 = [A.alloc([512], BF16) for _ in range(2)]
        kf2 = [A.alloc([512], F32) for _ in range(2)]
        rs2 = [A.alloc([512], F32) for _ in range(2)]
        t12 = [A.alloc([512], F32) for _ in range(2)]
        t22 = [A.alloc([512], F32) for _ in range(2)]
        tmpf = [A.alloc([512], F32), t12[0], t22[0]]
        qkc = [0]

        def qknorm_rope(ps, W, wv, rt, outv):
            SUB = 9
            par = qkc[0] % 2
            qkc[0] += 1
            sqh, kf, rs, t1, t2 = sqh2[par], kf2[par], rs2[par], t12[par], t22[par]
            pssq = psum[5] if par == 0 else psum[0]
            pkp = psum[6] if par == 0 else psum[1]
            P.act(sqh[:, 0:W], ps[:, 0:W], AF.Square)
            if SUB < 2:
                return
            P.ts("dve", kf[:, 0:W], ps[:, 0:W], wv, ALU.mult)
            P.mm(pssq[:, 0:W], blk_b[:, :], sqh[:, 0:W])
            P.act(rs[:, 0:W], pssq[:, 0:W], AF.Ln, scale=1.0 / 64.0, bias=epsv[:, 0:1])
            P.act(rs[:, 0:W], rs[:, 0:W], AF.Exp, scale=-0.5)
            if SUB < 3:
                return
            if rt is not None:
                P.mm(pkp[:, 0:W], pimat[:, :], kf[:, 0:W])
                P.tt("dve", t1[:, 0:W], kf[:, 0:W], rt[:, 0, 0:W], ALU.mult)
                P.tt("dve", t2[:, 0:W], pkp[:, 0:W], rt[:, 1, 0:W], ALU.mult)
                P.tt("pool", t1[:, 0:W], t1[:, 0:W], t2[:, 0:W], ALU.add)
                P.tt("dve", outv, t1[:, 0:W], rs[:, 0:W], ALU.mult)
            else:
                P.tt("dve", outv, kf[:, 0:W], rs[:, 0:W], ALU.mult)

        KTx, KTc, VAx, VAc = KT.s("x"), KT.s("c"), VA.s("x"), VA.s("c")
        def exchange_kv():
            for c in range(2):
                P.dma("sp", KXs[c * 128:(c + 1) * 128, :], KTx[:, c, 0:HALF])
            for i in range(2):
                P.dma("sp", VXs[i][:, :], VAx[:, i * NVH:(i + 1) * NVH, :, :].re("p j h d -> p (j h d)"))
            grp = [[2 * i, 2 * i + 1] for i in range(NB)]
            P.add("pool", lambda e: e.collective_compute("AllGather", ALU.bypass, ins=[KX_src.ap().opt()], outs=[KX_dst.ap().opt()],
                                                         replica_groups=grp),
                  reads=[KXs[:, :].key], writes=[KXd[:, :].key], cc=True)
            for i in range(2):
                P.add("pool", lambda e, i=i: e.collective_compute("AllGather", ALU.bypass, ins=[VX_src[i].ap().opt()], outs=[VX_dst[i].ap().opt()],
                                                                  replica_groups=grp),
                      reads=[VXs[i][:, :].key], writes=[VXd[i][:, :].key], cc=True)
            for r in range(2):
                for c in range(2):
                    P.dma("sp", KTx[:, c, r * HALF:(r + 1) * HALF], KXd[(2 * r + c) * 128:(2 * r + c + 1) * 128, :])
                for i in range(2):
                    P.dma("sp", VAx[:, r * NOWN + i * NVH:r * NOWN + (i + 1) * NVH, :, :].re("p j h d -> p (j h d)"), VXd[i][r * 128:(r + 1) * 128, :])

        tiles = [("own", i) for i in range(HALF // 512)] + [("ctx", 0)]
        for tn, (kind, ti) in enumerate(tiles):
            W = 512 if kind != "ctx" else CTX
            which = 1 if kind == "ctx" else 0
            if kind == "own":
                src = xo[ti * 512:(ti + 1) * 512, :]
                kbase = ti * 512
                hcol = ti * 512
            elif kind == "par":
                src = xp[ti * 512:(ti + 1) * 512, :]
                kbase = HALF + ti * 512
                hcol = None
            else:
                src = cx[:, :]
                kbase = 2 * HALF
                hcol = HALF
            xi = xin[tn % 2]
            xT = xT2[tn % 2]
            load_xT(src, W, xi, xT, [psum[0], psum[1]])
            rt = None
            if kind != "ctx":
                rt = rtab[tn % 2]
                rsrc = rope_o
                P.dma("sp", rt[:, :, :], rsrc[:, :, ti * 512:(ti + 1) * 512].re("a p n -> p a n"))
            if hcol is not None:
                P.dma("sp", H0[:, :, hcol:hcol + W], xT[:, :, 0:W])
            LVL = int(os.environ.get("K_LVL", "9"))
            if LVL < 2:
                continue
            norm_mod(xT, W, modAv(0, 0, which), mod(0, 0, which), hn, sq, tmpf, psum[2])
            if LVL < 3:
                continue
            for c in range(2):
                pb = psum[3 + c % 2]
                for kc in range(8):
                    P.mm(pb[:, 0:W], wqkv[:, kc, 1024 + c * 128:1024 + (c + 1) * 128], hn[:, kc, 0:W],
                         start=(kc == 0), stop=(kc == 7))
                qknorm_rope(pb, W, svv(o_qkn + 1, 1), rt, (KTc if kind == "ctx" else KTx)[:, c, kbase:kbase + W])
            for j in range(W // 128 if LVL >= 4 else 0):
                pb = psum[7]
                for kc in range(8):
                    P.mm(pb[:, 0:256], hn[:, kc, j * 128:(j + 1) * 128], wqkv[:, kc, 1280:1536],
                         start=(kc == 0), stop=(kc == 7))
                P.copy(evac_eng(), (VAc if kind == "ctx" else VAx)[:, kbase // 128 + j, :, 0:64], pb[:, 0:256].re("p (h d) -> p h d", d=64))
            if kind == "own" and ti == HALF // 512 - 1:
                exchange_kv()
            if hcol is not None and LVL >= 5:
                for c in range(8):
                    pb = psum[3 + c % 2]
                    for kc in range(8):
                        P.mm(pb[:, 0:W], wqkv[:, kc, c * 128:(c + 1) * 128], hn[:, kc, 0:W],
                             start=(kc == 0), stop=(kc == 7))
                    qknorm_rope(pb, W, svv(o_qkn, 1), rt, qT[:, c, 0:W])
                P.dma("sp", QS[:, :, hcol:hcol + W], qT[:, :, 0:W])
        A.release(mA)

        if "B" not in PHASES:
            A.release(mL0)
            return
        wo = A.alloc([16, 1024], BF16, parts=64)
        P.dma("pool", wo[:, :, :], wo_d[:, :, :])
        for l in range(2):
            for kc in range(8):
                P.dma("pool", fwin_b[l, kc * 128:(kc + 1) * 128, :], fwin_d[l, kc * 128:(kc + 1) * 128, :])
            for oc in range(8):
                P.add("pool", lambda e, l=l, oc=oc: e.dma_start(
                          out=fwout_r.h[l, oc, :, :].rearrange("p (a b) -> p a b", b=128),
                          in_=fwout_d.h[l, :, oc * 128:(oc + 1) * 128].rearrange("(a p) b -> p a b", p=128)),
                      reads=[fwout_d[l, :, :].key], writes=[(fwout_r.tid, (l, oc))], dma=True)
            if l == 0:
                for kc in range(8):
                    P.dma("pool", hwin_b[kc * 128:(kc + 1) * 128, :], hwin_d[kc * 128:(kc + 1) * 128, :])
                    P.dma("pool", hwo_b[kc * 128:(kc + 1) * 128, :], hwo_d[kc * 128:(kc + 1) * 128, :])
        qTb = [A.alloc([8, 512], BF16) for _ in range(2)]
        xTb = [A.alloc([8, 512], F32) for _ in range(2)]
        PT = [A.alloc([2, 512], BF16) for _ in range(3)]
        osb = A.alloc([2, 512], F32)
        rinv = A.alloc([2, 512], F32)
        rb = A.alloc([2, 512], F32)
        P.memset("pool", rinv[:, :, :], 1.0)
        oT = A.alloc([16, 512], BF16)
        hmid = A.alloc([8, 512], F32)
        SCALE = 64 ** -0.5
        qtiles = [("own", i) for i in range(HALF // 512)] + [("ctx", 0)]
        for tn, (kind, ti) in enumerate(qtiles):
            W = 512 if kind == "own" else CTX
            which = 0 if kind == "own" else 1
            hcol = ti * 512 if kind == "own" else HALF
            kcs = list(range(NKC)) if kind == "own" else [NKC - 2, NKC - 1]
            qb = qTb[tn % 2]
            xb = xTb[tn % 2]
            P.dma("sp", qb[:, :, 0:W], QS[:, :, hcol:hcol + W])
            P.dma("sp", xb[:, :, 0:W], H0[:, :, hcol:hcol + W])
            step = 0
            for c in range(8):
                kvc = c // 4
                Sb = [(psum[0], psum[1]), (psum[2], psum[3]), (psum[4], psum[5])]
                Ob = (psum[6], psum[7])

                def QK(i, st):
                    kc = kcs[i]
                    sa, sbb = Sb[st % 3]
                    P.mm(sa[:, 0:W], KT[0:64, kvc, kc * 128:(kc + 1) * 128], qb[0:64, c, 0:W])
                    P.mm(sbb[:, 0:W], KT[64:128, kvc, kc * 128:(kc + 1) * 128], qb[64:128, c, 0:W])

                def EXP(i, st):
                    pt = PT[st % 3]
                    P.act(pt[:, :, 0:W], ppair(st % 3, W), AF.Exp, scale=SCALE)

                def PV(i, st):
                    kc = kcs[i]
                    pt = PT[st % 3]
                    for ab in range(2):
                        P.mm(Ob[ab][0:65, 0:W], VA[:, kc, 2 * kvc + ab, 0:65], pt[:, ab, 0:W],
                             start=(i == 0), stop=(i == len(kcs) - 1))

                n = len(kcs)
                for j in range(min(2, n)):
                    QK(j, step + j)
                for i in range(n):
                    if i + 2 < n:
                        QK(i + 2, step + i + 2)
                    EXP(i, step + i)
                    PV(i, step + i)
                step += n
                for ab in range(2):
                    P.copy("dve", osb[0:65, ab, 0:W], Ob[ab][0:65, 0:W])
                P.recip(rinv[64:65, :, 0:W], osb[64:65, :, 0:W])
                P.shuf(rb[0:32, :, 0:W], rinv[64:96, :, 0:W], [0] * 32)
                P.shuf(rb[32:64, :, 0:W], rinv[64:96, :, 0:W], [0] * 32)
                for ab in range(2):
                    P.tt("dve", oT[0:64, 2 * c + ab, 0:W], osb[0:64, ab, 0:W], rb[0:64, ab, 0:W], ALU.mult)
            for oc in range(8):
                pb = psum[oc % 6]
                for j in range(16):
                    P.mm(pb[:, 0:W], wo[0:64, j, oc * 128:(oc + 1) * 128], oT[0:64, j, 0:W], start=(j == 0), stop=(j == 15))
                P.stt(hmid[:, oc, 0:W], pb[:, 0:W], mod(0, 2, which)(oc), xb[:, oc, 0:W], ALU.mult, ALU.add)
            P.dma("sp", H1[:, :, hcol:hcol + W], hmid[:, :, 0:W])
        A.release(mL0)


    if "A" in PHASES:
        layer0()

    def ffn_phase(l, src, dst, tilespecs, final=False):
        m = A.mark()
        win = A.alloc([8, 2 * FF], BF16)
        for kc in range(8):
            P.dma("sp", win[:, kc, :], fwin_b[l, kc * 128:(kc + 1) * 128, :])
        wob = [A.alloc([22, 128], BF16) for _ in range(3)]
        h2 = [A.alloc([8, 512], F32) for _ in range(2)]
        hn2 = [A.alloc([8, 512], BF16) for _ in range(2)]
        sq = A.alloc([8, 512], BF16)
        tmpf = [A.alloc([512], F32) for _ in range(3)]
        sa = [A.alloc([512], F32) for _ in range(2)]
        sT = A.alloc([22, 512], BF16)
        if final:
            fo = sT.view(sT.h[:, 0:16, :].rearrange("p a b -> p (a b)").bitcast(F32).rearrange("p (a b) -> p a b", a=8))
        nt = len(tilespecs)
        nwo = [0]

        def load(t):
            col, W, which = tilespecs[t]
            P.dma("sp", h2[t % 2][:, :, 0:W], src[:, :, col:col + W])

        def norm(t):
            col, W, which = tilespecs[t]
            norm_mod(h2[t % 2], W, modAv(l, 1, which), mod(l, 3, which), hn2[t % 2], sq, tmpf, psum[0])

        def stage_in(t):
            col, W, which = tilespecs[t]
            hn = hn2[t % 2]
            for hc in range(22):
                pa = psum[1 + 2 * (hc % 2)]
                pu = psum[2 + 2 * (hc % 2)]
                for kc in range(8):
                    P.mm(pa[:, 0:W], win[:, kc, hc * 128:(hc + 1) * 128], hn[:, kc, 0:W], start=(kc == 0), stop=(kc == 7))
                for kc in range(8):
                    P.mm(pu[:, 0:W], win[:, kc, FF + hc * 128:FF + (hc + 1) * 128], hn[:, kc, 0:W], start=(kc == 0), stop=(kc == 7))
                s_ = sa[hc % 2]
                P.act(s_[:, 0:W], pa[:, 0:W], AF.Silu)
                P.tt("dve", sT[:, hc, 0:W], s_[:, 0:W], pu[:, 0:W], ALU.mult)

        def stage_out(t):
            col, W, which = tilespecs[t]
            h = h2[t % 2]
            for oc in range(8):
                wb = wob[nwo[0] % 3]
                nwo[0] += 1
                P.dma("sp", wb[:, :, :].re("p a b -> p (a b)"), fwout_r.s((l, oc))[l, oc, :, :])
                pb = psum[5 + oc % 2]
                for hc in range(22):
                    P.mm(pb[:, 0:W], wb[:, hc, :], sT[:, hc, 0:W], start=(hc == 0), stop=(hc == 21))
                P.stt(h[:, oc, 0:W], pb[:, 0:W], mod(l, 5, which)(oc), h[:, oc, 0:W], ALU.mult, ALU.add)
            if not final:
                P.dma("sp", dst[:, :, col:col + W], h[:, :, 0:W])
            else:
                norm_mod(h, W, lambda kc: svv(o_fnw + kc, 1), lambda kc: zero_f[:, 0:1], fo, sq, tmpf, psum[0])
                for j in range(W // 128):
                    for half in range(2):
                        pb = psum[1 + (2 * j + half) % 4]
                        for q in range(4):
                            kc = half * 4 + q
                            P.tr(pb[:, q * 128:(q + 1) * 128], fo[:, kc, j * 128:(j + 1) * 128], ident[:, :])
                        yb = sa[(2 * j + half) % 2]
                        P.copy(evac_eng(), yb[:, :], pb[:, :])
                        P.dma("sp", dst[col + j * 128:col + (j + 1) * 128, half * 512:(half + 1) * 512], yb[:, :])

        load(0)
        if nt > 1:
            load(1)
        norm(0)
        for t in range(nt):
            stage_in(t)
            if t + 1 < nt and not final:
                norm(t + 1)
            stage_out(t)
            if t + 1 < nt and final:
                norm(t + 1)
            if t + 2 < nt:
                load(t + 2)
        A.release(m)

    lat_tiles = [(i * 512, 512, 0) for i in range(HALF // 512)]
    if "F0" in PHASES:
        ffn_phase(0, H1, H2, lat_tiles + [(HALF, CTX, 1)])

    def hgrn_layer():
        mH = A.mark()
        hw = A.alloc([8, 4 * D], BF16)
        hwo = A.alloc([8, D], BF16)
        S32 = A.alloc([8, 128], F32)
        Sbf2 = [A.alloc([8, 128], BF16) for _ in range(2)]
        sc = [0]
        maskF = A.alloc([64], F32, parts=64)
        maskB = A.alloc([64], F32, parts=64)
        rmask = A.alloc([512], BF16)
        P.memset("pool", maskF[:, :], 1.0)
        P.memset("pool", maskB[:, :], 1.0)
        P.add("pool", lambda e: e.affine_select(out=maskF[:, :].ap, in_=maskF[:, :].ap, pattern=[[1, 64]],
                                                compare_op=ALU.is_ge, fill=0.0, base=0, channel_multiplier=-1),
              reads=[maskF[:, :].key], writes=[maskF[:, :].key])
        P.add("pool", lambda e: e.affine_select(out=maskB[:, :].ap, in_=maskB[:, :].ap, pattern=[[-1, 64]],
                                                compare_op=ALU.is_ge, fill=0.0, base=0, channel_multiplier=1),
              reads=[maskB[:, :].key], writes=[maskB[:, :].key])
        P.memset("pool", rmask[:, :], 1.0)
        P.memset("pool", V(rmask.h[:, :].rearrange("p (c t) -> p c t", t=64)[:, :, 0:1], rmask[:, :].key), 0.0)
        h = A.alloc([8, 512], F32)
        sq = A.alloc([8, 512], BF16)
        hn = A.alloc([8, 512], BF16)
        sg2 = [A.alloc([512], F32) for _ in range(2)]
        gT2 = [A.alloc([512], F32) for _ in range(2)]
        kT2 = [A.alloc([512], F32) for _ in range(2)]
        LT2 = [A.alloc([512], F32) for _ in range(2)]
        eL2 = [A.alloc([512], F32) for _ in range(2)]
        e22 = [A.alloc([512], F32) for _ in range(2)]
        tmpf = [gT2[0], gT2[1], kT2[0]]
        gs = sg2[0]
        eTotA = A.alloc([8, 8], F32)
        qtA = A.alloc([8, 512], BF16)
        kvA = A.alloc([16, 512], BF16)
        khA = kvA.view(kvA.h[:, 0:8, :], "kh")
        vTA = kvA.view(kvA.h[:, 8:16, :], "vT")
        Rst = kvA.view(kvA.h[:, :, :].rearrange("p a b -> p (a b)").bitcast(F32).rearrange("p (a b) -> p a b", a=8))
        ktok2 = [A.alloc([8, 128], BF16, parts=64) for _ in range(2)]
        vtok2 = [A.alloc([8, 128], BF16, parts=64) for _ in range(2)]
        Am2 = [A.alloc([8, 64], BF16, parts=64) for _ in range(2)]
        oTt = A.alloc([8, 512], F32)
        hww = hw.s("w")
        hwow = hwo.s("w")

        class HB:
            def __init__(self, fn):
                self.fn = fn

            def __getitem__(self, idx):
                _, hd, sl = idx
                return self.fn(hd, sl)

        def carve(t, a0, sub):
            key = (t.tid, sub)
            return HB(lambda hd, sl: V(t.h[:, a0 + hd // 2, (hd % 2) * 512 + sl.start:(hd % 2) * 512 + sl.stop], key))

        qtB = carve(hw, 0, "qtB") if False else HB(lambda hd, sl: V(hw.h[:, hd // 2, 3 * D + (hd % 2) * 512 + sl.start:3 * D + (hd % 2) * 512 + sl.stop], (hw.tid, "qtB")))
        khB = HB(lambda hd, sl: V(hw.h[:, 4 + hd // 2, 3 * D + (hd % 2) * 512 + sl.start:3 * D + (hd % 2) * 512 + sl.stop], (hw.tid, "khB")))
        vTB = HB(lambda hd, sl: V(hwo.h[:, hd // 2, (hd % 2) * 512 + sl.start:(hd % 2) * 512 + sl.stop], (hwo.tid, "vTB")))
        eTotB = hwo.view(hwo.h[:, 4, 0:128].bitcast(F32).rearrange("p (a b) -> p a b", a=8), "eB")
        bufs = [(qtA, khA, vTA, eTotA), (qtB, khB, vTB, eTotB)]
        pbf = [psum[i].view(psum[i].h[:, 0:512].bitcast(BF16)) for i in range(8)]

        def load_hw(blocks):
            for bi, sb in enumerate(blocks):
                for kc in range(8):
                    P.dma("sp", (hww if bi < 3 else hw)[:, kc, bi * D:(bi + 1) * D], hwin_b[kc * 128:(kc + 1) * 128, sb * D:(sb + 1) * D])

        class TileJob:
            def __init__(self, src_col, W, which, rev, emit_out, xsrc, bset, readout=False, of_col=None):
                self.src_col, self.W, self.which, self.rev, self.emit_out = src_col, W, which, rev, emit_out
                self.xsrc, self.readout, self.of_col = xsrc, readout, of_col
                self.qt, self.kh, self.vT, self.eTot = bufs[bset]
                self.nch = W // 64
                self.order = list(range(self.nch))[::-1] if rev else list(range(self.nch))
                self.mask = maskB if rev else maskF
                self.alone = False

            def load_h(self):
                W = self.W
                P.dma("sp", h[:, :, 0:W], self.xsrc[:, :, self.src_col:self.src_col + W])

            def prologue(self, load=True):
                W = self.W
                if load:
                    self.load_h()
                norm_mod(h, W, modAv(1, 0, self.which), mod(1, 0, self.which), hn, sq, tmpf, psum[7])

            def head(self, hd, standalone=False):
                W, nch, rev = self.W, self.nch, self.rev
                qt, kh, vT, eTot = self.qt, self.kh, self.vT, self.eTot
                par = hd % 2
                sg, gT, kT, LT, eL, e2 = sg2[par], gT2[par], kT2[par], LT2[par], eL2[par], e22[par]
                pz = psum[2] if (standalone and par) else psum[6]
                for kc in range(8):
                    P.mm(pz[:, 0:W], hww[:, kc, D + hd * 128:D + (hd + 1) * 128], hn[:, kc, 0:W], start=(kc == 0), stop=(kc == 7))
                P.act(sg[:, 0:W], pz[:, 0:W], AF.Sigmoid)
                P.act(gT[:, 0:W], sg[:, 0:W], AF.Ln, scale=svv(o_l1 + hd, 1), bias=svv(o_lb + hd, 1))
                P.ts("dve", kT[:, 0:W], sg[:, 0:W], svv(o_nl + hd, 1), ALU.mult, svv(o_l1 + hd, 1), ALU.add)
                P.scan(LT[:, 0:W], rmask[:, 0:W], gT[:, 0:W], 0.0, ALU.mult, ALU.add)
                LTc = V(LT.h[:, 0:W].rearrange("p (c t) -> p c t", t=64), LT[:, :].key)
                P.act(eTot[:, hd, 0:nch], V(LTc.ap[:, :, 63], LTc.key), AF.Exp)
                Lsrc = LT
                if rev:
                    P.tt("dve", gT[:, 0:W], gT[:, 0:W], LT[:, 0:W], ALU.subtract)
                    tot = V(LTc.ap[:, :, 63:64].to_broadcast([128, nch, 64]), LTc.key)
                    P.tt("dve", sg[:, 0:W].re("p (c t) -> p c t", t=64), gT[:, 0:W].re("p (c t) -> p c t", t=64), tot, ALU.add)
                    Lsrc = sg
                P.act(eL[:, 0:W], Lsrc[:, 0:W], AF.Exp)
                P.act(e2[:, 0:W], Lsrc[:, 0:W], AF.Exp, scale=-1.0)
                P.tt("dve", kh[:, hd, slice(0, W)], kT[:, 0:W], e2[:, 0:W], ALU.mult)
                pq = psum[3] if (standalone and par) else psum[7]
                for kc in range(8):
                    P.mm(pq[:, 0:W], hww[:, kc, hd * 128:(hd + 1) * 128], hn[:, kc, 0:W], start=(kc == 0), stop=(kc == 7))
                P.tt("dve", qt[:, hd, slice(0, W)], pq[:, 0:W], eL[:, 0:W], ALU.mult)
                pv = psum[2] if (standalone and par) else psum[6]
                for kc in range(8):
                    P.mm(pv[:, 0:W], hww[:, kc, 2 * D + hd * 128:2 * D + (hd + 1) * 128], hn[:, kc, 0:W], start=(kc == 0), stop=(kc == 7))
                P.copy("act", vT[:, hd, slice(0, W)], pv[:, 0:W])

            def pre(self, k):
                ci = self.order[k]
                cs = slice(ci * 64, (ci + 1) * 64)
                qt, kh, vT = self.qt, self.kh, self.vT
                ktok, vtok, Am = ktok2[k % 2], vtok2[k % 2], Am2[k % 2]
                if self.emit_out:
                    for hd in range(8):
                        P.mm(psum[4][0:64, hd * 64:(hd + 1) * 64], kh[:, hd, cs], qt[:, hd, cs])
                for hd in range(8):
                    P.tr(pbf[2][0:64, hd * 128:(hd + 1) * 128], kh[:, hd, cs], identb[:, :])
                for hd in range(8):
                    P.tr(pbf[3][0:64, hd * 128:(hd + 1) * 128], vT[:, hd, cs], identb[:, :])
                P.copy("act", ktok[0:64, :, :], pbf[2][0:64, :].re("p (h d) -> p h d", d=128))
                P.copy("dve", vtok[0:64, :, :], pbf[3][0:64, :].re("p (h d) -> p h d", d=128))
                if self.emit_out:
                    P.tt("dve", Am[0:64, :, :], psum[4][0:64, 0:512].re("p (h t) -> p h t", t=64),
                         V(self.mask.h[0:64, :].unsqueeze(1).to_broadcast([64, 8, 64]), self.mask[:, :].key), ALU.mult)

            def preU(self, k):
                ktok, vtok = ktok2[k % 2], vtok2[k % 2]
                up = 3 * (k % 2) if self.alone else 0
                for hd in range(8):
                    pb = psum[2 * up + hd // 4]
                    P.mm(pb[:, (hd % 4) * 128:(hd % 4 + 1) * 128], ktok[0:64, hd, :], vtok[0:64, hd, :])

            def post(self, k):
                ci = self.order[k]
                cs = slice(ci * 64, (ci + 1) * 64)
                qt = self.qt
                ktok, vtok, Am = ktok2[k % 2], vtok2[k % 2], Am2[k % 2]
                Scur = Sbf2[sc[0] % 2]
                Snew = Sbf2[(sc[0] + 1) % 2]
                sc[0] += 1
                up = 3 * (k % 2) if self.alone else 0
                U = V(pp[up][:, :].rearrange("p (h d) -> p h d", d=128), (psum[2 * up].tid, None), extra=((psum[2 * up + 1].tid, None),))
                P.add("dve", lambda e, U=U: e.tensor_tensor(out=S32[:, :, :].ap, in0=U.ap, in1=S32[:, :, :].ap, op=ALU.add),
                      reads=[U.key, U.extra[0], S32[:, :, :].key], writes=[S32[:, :, :].key])
                et = V(self.eTot.h[:, :, ci:ci + 1].to_broadcast([128, 8, 128]), self.eTot[:, :, :].key)
                P.tt("dve", S32[:, :, :], S32[:, :, :], et, ALU.mult)
                P.copy("act", Snew[:, :, :], S32[:, :, :])
                if self.emit_out:
                    for hd in range(8):
                        P.mm(psum[5][:, hd * 64:(hd + 1) * 64], Scur[:, hd, :], qt[:, hd, cs], start=True, stop=False)
                        P.mm(psum[5][:, hd * 64:(hd + 1) * 64], vtok[0:64, hd, :], Am[0:64, hd, :], start=False, stop=True)
                    po = psum[5][:, 0:512].re("p (h t) -> p h t", t=64)
                    if self.of_col is None:
                        P.copy("act", oTt[:, :, cs], po)
                    else:
                        P.tt("dve", oTt[:, :, cs], po, oTt[:, :, cs], ALU.add)

            def epilogue(self):
                if not self.readout:
                    return
                qt = self.qt
                og = qt
                P.dma("sp", Rst[:, :, :], self.xsrc[:, :, self.src_col:self.src_col + 512])
                for hd in range(8):
                    par = hd % 2
                    r0 = gT2[par]
                    r1 = kT2[par]
                    gsp = sg2[par]
                    pss = psum[1] if par else psum[5]
                    pg = psum[7] if par else psum[6]
                    P.act(sq[:, hd, :], oTt[:, hd, :], AF.Square)
                    P.mm(pss[:, :], ones_b[:, :], sq[:, hd, :])
                    for kc in range(8):
                        P.mm(pg[:, :], hw[:, kc, 3 * D + hd * 128:3 * D + (hd + 1) * 128], hn[:, kc, :], start=(kc == 0), stop=(kc == 7))
                    P.act(r0[:, :], pss[:, :], AF.Ln, scale=1.0 / 128.0, bias=epsv[:, 0:1])
                    P.act(r0[:, :], r0[:, :], AF.Exp, scale=-0.5)
                    P.act(gsp[:, :], pg[:, :], AF.Sigmoid)
                    P.stt(r1[:, :], oTt[:, hd, :], svv(o_onw + hd, 1), r0[:, :], ALU.mult, ALU.mult)
                    P.tt("dve", og[:, hd, slice(0, 512)], r1[:, :], gsp[:, :], ALU.mult)
                for oc in range(8):
                    pb = psum[2 + oc % 2]
                    for kc in range(8):
                        P.mm(pb[:, :], hwow[:, kc, oc * 128:(oc + 1) * 128], og[:, kc, slice(0, 512)], start=(kc == 0), stop=(kc == 7))
                    P.stt(Rst[:, oc, :], pb[:, :], mod(1, 2, 0)(oc), Rst[:, oc, :], ALU.mult, ALU.add)
                P.dma("sp", H3[:, :, self.src_col:self.src_col + 512], Rst[:, :, :])

        def run_chunks(job, nxt):
            job.alone = nxt is None
            n = job.nch
            job.pre(0)
            job.preU(0)
            for k in range(n):
                if k + 1 < n:
                    job.pre(k + 1)
                job.post(k)
                if nxt is not None and k < 8:
                    if k == 0:
                        nxt.prologue()
                    nxt.head(k)
                if k + 1 < n:
                    job.preU(k + 1)
            if nxt is not None:
                for hd in range(n, 8):
                    if n == 0:
                        nxt.prologue()
                    nxt.head(hd)

        load_hw([0, 2, 4])
        P.memset("dve", S32[:, :, :], 0.0)
        P.memset("dve", Sbf2[0][:, :, :], 0.0)
        jobs = [TileJob(HALF, CTX, 1, False, False, H2, 0)]
        for ti in range(HALF // 512):
            jobs.append(TileJob(ti * 512, 512, 0, False, True, H2, (ti + 1) % 2))
        jobs[0].prologue()
        for hd in range(8):
            jobs[0].head(hd, standalone=True)
        for ji, job in enumerate(jobs):
            nxt = jobs[ji + 1] if ji + 1 < len(jobs) else None
            run_chunks(job, nxt)
            if job.emit_out:
                P.dma("sp", OF[:, :, job.src_col:job.src_col + 512], oTt[:, :, :])
        P.dma("sp", SXs[:, :], S32[:, :, :].re("p h d -> p (h d)"))
        P.add("pool", lambda e: e.collective_compute("AllGather", ALU.bypass, ins=[SX_src.ap().opt()], outs=[SX_dst.ap().opt()],
                                                     replica_groups=[[2 * i, 2 * i + 1] for i in range(NB)]),
              reads=[SXs[:, :].key], writes=[SXd[:, :].key], cc=True)
        load_hw([0, 3, 4, 1])
        for kc in range(8):
            P.dma("sp", hwo[:, kc, :], hwo_b[kc * 128:(kc + 1) * 128, :])
        g0 = h[:, 0:2, :].re("p a b -> p (a b)")
        g1 = oTt[:, 0:2, :].re("p a b -> p (a b)")
        P.dma("sp", g0, SXd[0:128, :])
        P.dma("sp", g1, SXd[128:256, :])
        P.ts("dve", g0, g0, parw[:, 0:1], ALU.mult)
        P.stt(S32[:, :, :].re("p h d -> p (h d)"), g1, parw[:, 1:2], g0, ALU.mult, ALU.add)
        P.copy("act", Sbf2[sc[0] % 2][:, :, :], S32[:, :, :])
        tis = list(reversed(range(HALF // 512)))
        jobs2 = [TileJob(ti * 512, 512, 0, True, True, H2, 0, readout=True, of_col=ti * 512) for ti in tis]
        jobs2[0].load_h()
        for ji, job in enumerate(jobs2):
            job.prologue(load=False)
            P.dma("sp", oTt[:, :, :], OF[:, :, job.src_col:job.src_col + 512])
            if ji + 1 < len(jobs2):
                jobs2[ji + 1].load_h()
            for hd in range(8):
                job.head(hd, standalone=True)
            run_chunks(job, None)
            job.epilogue()
        A.release(mH)

    if "H" in PHASES:
        hgrn_layer()

    if "F1" in PHASES:
        ffn_phase(1, H3, yout, lat_tiles, final=True)

    if dbg_fn is not None:
        dbg_fn(locals())

    P.barrier()
    P.emit()
    return nc, P


_CACHE = {}


def _rope_tables(pos):
    inv = (np.float32(10000.0) ** (-(np.arange(16, dtype=np.float32) * np.float32(2.0) / np.float32(32.0)))).astype(np.float32)
    row = (pos // 64).astype(np.float32)
    col = (pos % 64).astype(np.float32)
    ar = row[:, None] * inv[None, :]
    ac = col[:, None] * inv[None, :]
    cr, sr, cc, sc = np.cos(ar), np.sin(ar), np.cos(ac), np.sin(ac)
    C = np.concatenate([cr, cr, cc, cc], axis=1).T.astype(np.float32)
    S = np.concatenate([-sr, sr, -sc, sc], axis=1).T.astype(np.float32)
    C = np.concatenate([C, C], axis=0)
    S = np.concatenate([S, S], axis=0)
    return np.ascontiguousarray(np.stack([C, S], axis=0))


def _fm(v):
    return np.ascontiguousarray(np.asarray(v).reshape(8, 128).T)


def kernel(x, c, ctx, c_ctx, ada_w, ada_b, norm_mix_w, norm_ffn_w, attn_w_qkv, attn_q_norm,
           attn_k_norm, attn_w_o, hgrn_w_in, hgrn_lb_logits, hgrn_out_norm, hgrn_w_o,
           ffn_w_in, ffn_w_out, final_norm_w):
    in_maps, gather = make_inputs(x, c, ctx, c_ctx, ada_w, ada_b, norm_mix_w, norm_ffn_w, attn_w_qkv, attn_q_norm,
                                  attn_k_norm, attn_w_o, hgrn_w_in, hgrn_lb_logits, hgrn_out_norm, hgrn_w_o,
                                  ffn_w_in, ffn_w_out, final_norm_w)
    if "nc" not in _CACHE:
        _CACHE["nc"] = build_program()[0]
    nc = _CACHE["nc"]
    res = run_bass_kernel_spmd(nc, in_maps, core_ids=list(range(2 * NB)))
    return gather([r["y"] for r in res.results])


def make_inputs(x, c, ctx, c_ctx, ada_w, ada_b, norm_mix_w, norm_ffn_w, attn_w_qkv, attn_q_norm,
                attn_k_norm, attn_w_o, hgrn_w_in, hgrn_lb_logits, hgrn_out_norm, hgrn_w_o,
                ffn_w_in, ffn_w_out, final_norm_w):
    f = lambda a: np.ascontiguousarray(np.asarray(a, dtype=np.float32))
    x, c, ctx, c_ctx = f(x), f(c), f(ctx), f(c_ctx)
    idx0 = np.arange(HALF)
    idx1 = SEQ - 1 - np.arange(HALF)
    rope = [_rope_tables(idx0), _rope_tables(idx1)]
    pim = np.zeros((128, 128), np.float32)
    for m in range(128):
        d = m % 64
        pi = d + 16 if (d % 32) < 16 else d - 16
        pim[(m // 64) * 64 + pi, m] = 1.0
    wqkv = f(attn_w_qkv)[0]
    qcols = np.concatenate([np.arange(h * 64, (h + 1) * 64) for h in HEAD_ORDER])
    wqkv_dev = np.ascontiguousarray(np.concatenate([wqkv[:, qcols], wqkv[:, 1024:]], axis=1))
    wo = f(attn_w_o)[0]
    wo_dev = np.ascontiguousarray(np.stack([wo[h * 64:(h + 1) * 64, :] for h in HEAD_ORDER], axis=1))
    qkn = np.ascontiguousarray(np.stack([np.tile(f(attn_q_norm)[0], 2), np.tile(f(attn_k_norm)[0], 2)], axis=1))
    hwin = f(hgrn_w_in)[0]
    hwin_sw = np.ascontiguousarray(np.concatenate([hwin[:, :2 * D], hwin[:, 3 * D:4 * D], hwin[:, 2 * D:3 * D], hwin[:, 4 * D:]], axis=1))
    adab = np.ascontiguousarray(np.stack([f(ada_b)[l].reshape(48, 128).T for l in range(2)], axis=1))
    nmw = np.ascontiguousarray(np.stack([_fm(f(norm_mix_w)[l]) for l in range(2)], axis=1))
    nfw = np.ascontiguousarray(np.stack([_fm(f(norm_ffn_w)[l]) for l in range(2)], axis=1))
    lbl = np.ascontiguousarray(np.stack([_fm(f(hgrn_lb_logits)[l]) for l in range(2)], axis=1))
    common = {
        "pimat": pim, "ada_w": f(ada_w), "ada_b": adab, "nmw": nmw, "nfw": nfw, "fnw": _fm(f(final_norm_w)),
        "wqkv": wqkv_dev, "qkn": qkn, "wo": wo_dev, "lbl": lbl, "onw": _fm(f(hgrn_out_norm)[0]),
        "hwo": f(hgrn_w_o)[0], "fwin": f(ffn_w_in), "fwout": f(ffn_w_out),
    }
    in_maps = []
    for core in range(2 * NB):
        b, s = core // 2, core % 2
        own = idx0 if s == 0 else idx1
        par = idx1 if s == 0 else idx0
        m = dict(common)
        m["xo"] = np.ascontiguousarray(x[b][own])
        m["cx"] = np.ascontiguousarray(ctx[b] if s == 0 else ctx[b][::-1])
        m["cvec"] = np.ascontiguousarray(np.stack([_fm(c[b]), _fm(c_ctx)], axis=2))
        m["rope_o"] = rope[s]
        m["hwin"] = hwin if s == 0 else hwin_sw
        m["parw"] = np.ascontiguousarray(np.tile(np.array([[0.0, 1.0]] if s == 0 else [[1.0, 0.0]], np.float32), (128, 1)))
        in_maps.append(m)

    def gather(ys):
        out = np.empty((NB, SEQ, D), np.float32)
        for core in range(2 * NB):
            b, s = core // 2, core % 2
            own = idx0 if s == 0 else idx1
            out[b][own] = ys[core]
        return out

    return in_maps, gather
```

```python
import os
import numpy as np
import concourse.bass as bass
import concourse.mybir as mybir
from concourse.bass_utils import run_bass_kernel_spmd

F32 = mybir.dt.float32
BF16 = mybir.dt.bfloat16
AF = mybir.ActivationFunctionType
ALU = mybir.AluOpType

D = 1024
SEQ = 8192
HALF = 4096
CTX = 256
NB = 4
FF = 2816
EPS = 1e-6
HEAD_ORDER = [0, 4, 1, 5, 2, 6, 3, 7, 8, 12, 9, 13, 10, 14, 11, 15]
ENGS = ["pe", "act", "dve", "pool", "sp"]
DMAQ = ("sp", "pool")
NDS = 8


class V:
    __slots__ = ("ap", "key", "extra")

    def __init__(self, ap, key, extra=()):
        self.ap = ap
        self.key = key
        self.extra = extra

    def bc(self, shape):
        return V(self.ap.to_broadcast(list(shape)), self.key)

    def re(self, pat, **kw):
        return V(self.ap.rearrange(pat, **kw), self.key)


class Tile:
    def __init__(self, h, tid, sub=None):
        self.h = h
        self.tid = tid
        self.sub = sub

    def __getitem__(self, idx):
        return V(self.h[idx], (self.tid, self.sub))

    def s(self, sub):
        return Tile(self.h, self.tid, sub)

    def view(self, ap, sub=None):
        return Tile(ap, self.tid, sub)


class Op:
    __slots__ = ("fn", "deps", "sig", "dma", "dsem", "dval", "cnt", "cc")

    def __init__(self, fn, deps, dma, cc=False):
        self.fn = fn
        self.deps = deps
        self.sig = False
        self.dma = dma
        self.dsem = None
        self.dval = 0
        self.cnt = 0
        self.cc = cc


class Prog:
    def __init__(self, nc):
        self.nc = nc
        self.ops = {e: [] for e in ENGS}
        self.last_w = {}
        self.readers = {}
        self.subs = {}
        self.ntid = 0
        self.ndma = {q: [] for q in DMAQ}
        self.dma_since_barrier = []
        self.last_real = {}
        self.excl = set()

    def newtid(self):
        self.ntid += 1
        return self.ntid

    def _conf(self, key):
        tid, sub = key
        ss = self.subs.setdefault(tid, set())
        ss.add(sub)
        if sub is None:
            return [(tid, s) for s in ss]
        return [(tid, sub), (tid, None)] if None in ss else [(tid, sub)]

    def add(self, eng, fn, reads=(), writes=(), dma=False, cc=False):
        idx = len(self.ops[eng])
        deps = set()
        xr = [k for k in reads if k[0] in self.excl]
        if xr:
            reads = [k for k in reads if k[0] not in self.excl]
            writes = list(writes) + xr
        for k in reads:
            for ck in self._conf(k):
                w = self.last_w.get(ck)
                if w is not None:
                    deps.add(w)
        for k in writes:
            for ck in self._conf(k):
                w = self.last_w.get(ck)
                if w is not None:
                    deps.add(w)
                for r in self.readers.get(ck, ()):
                    deps.add(r)
        if cc:
            self.dma_since_barrier.append((eng, idx))
        elif dma:
            lst = self.ndma[eng]
            if len(lst) >= NDS:
                deps.add((eng, lst[len(lst) - NDS]))
            lst.append(idx)
            self.dma_since_barrier.append((eng, idx))
        deps.discard((eng, idx))
        if eng == "pe":
            deps = {d for d in deps if d[0] != "pe"}
        op = Op(fn, deps, dma or cc, cc)
        self.ops[eng].append(op)
        self.last_real[eng] = idx
        me = (eng, idx)
        for k in writes:
            if k[1] is None:
                for ck in self._conf(k):
                    self.last_w[ck] = me
                    self.readers[ck] = []
            else:
                self.last_w[k] = me
                self.readers[k] = []
        for k in reads:
            rl = self.readers.setdefault(k, [])
            if not dma:
                rl[:] = [r for r in rl if r[0] != eng or self.ops[r[0]][r[1]].dma]
            rl.append(me)
        return me

    def barrier(self):
        lasts = [(e, i) for e, i in self.last_real.items()]
        dmas = list(self.dma_since_barrier)
        self.dma_since_barrier = []
        for e in ENGS:
            deps = set(lasts) | set(dmas)
            op = Op(None, deps, False)
            self.ops[e].append(op)

    def mm(self, out, lhsT, rhs, start=True, stop=True):
        self.add("pe", lambda e: e.matmul(out.ap, lhsT=lhsT.ap, rhs=rhs.ap, start=start, stop=stop),
                 reads=[lhsT.key, rhs.key], writes=[out.key])

    def tr(self, out, in_, ident):
        self.add("pe", lambda e: e.transpose(out.ap, in_.ap, ident.ap), reads=[in_.key, ident.key], writes=[out.key])

    def act(self, out, in_, func, scale=1.0, bias=0.0):
        reads = [in_.key] + list(in_.extra)
        sc = scale
        bi = bias
        if isinstance(scale, V):
            reads.append(scale.key)
            sc = scale.ap
        if isinstance(bias, V):
            reads.append(bias.key)
            bi = bias.ap
        self.add("act", lambda e: e.activation(out=out.ap, in_=in_.ap, func=func, bias=bi, scale=sc),
                 reads=reads, writes=[out.key])

    def copy(self, eng, out, in_):
        if eng == "act":
            self.add("act", lambda e: e.copy(out=out.ap, in_=in_.ap), reads=[in_.key], writes=[out.key])
        else:
            self.add(eng, lambda e: e.tensor_copy(out=out.ap, in_=in_.ap), reads=[in_.key], writes=[out.key])

    def tt(self, eng, out, in0, in1, op):
        self.add(eng, lambda e: e.tensor_tensor(out=out.ap, in0=in0.ap, in1=in1.ap, op=op),
                 reads=[in0.key, in1.key], writes=[out.key])

    def ts(self, eng, out, in0, s1, op0, s2=None, op1=None):
        reads = [in0.key]
        a1 = s1
        a2 = s2
        if isinstance(s1, V):
            reads.append(s1.key)
            a1 = s1.ap
        if isinstance(s2, V):
            reads.append(s2.key)
            a2 = s2.ap
        if op1 is None:
            self.add(eng, lambda e: e.tensor_scalar(out=out.ap, in0=in0.ap, scalar1=a1, scalar2=None, op0=op0),
                     reads=reads, writes=[out.key])
        else:
            self.add(eng, lambda e: e.tensor_scalar(out=out.ap, in0=in0.ap, scalar1=a1, scalar2=a2, op0=op0, op1=op1),
                     reads=reads, writes=[out.key])

    def stt(self, out, in0, scalar, in1, op0, op1):
        reads = [in0.key, in1.key]
        sc = scalar
        if isinstance(scalar, V):
            reads.append(scalar.key)
            sc = scalar.ap
        self.add("dve", lambda e: e.scalar_tensor_tensor(out=out.ap, in0=in0.ap, scalar=sc, in1=in1.ap, op0=op0, op1=op1),
                 reads=reads, writes=[out.key])

    def scan(self, out, d0, d1, initial, op0, op1):
        self.add("dve", lambda e: e.tensor_tensor_scan(out=out.ap, data0=d0.ap, data1=d1.ap, initial=initial, op0=op0, op1=op1),
                 reads=[d0.key, d1.key], writes=[out.key])

    def recip(self, out, in_):
        self.add("dve", lambda e: e.reciprocal(out=out.ap, in_=in_.ap), reads=[in_.key], writes=[out.key])

    def shuf(self, out, in_, mask):
        self.add("dve", lambda e: e.stream_shuffle(out=out.ap, in_=in_.ap, mask=mask), reads=[in_.key], writes=[out.key])

    def memset(self, eng, out, val):
        self.add(eng, lambda e: e.memset(out.ap, val), writes=[out.key])

    def dma(self, q, out, in_):
        if q == "pool":
            self.add(q, lambda e: e.dma_start(out=out.ap, in_=in_.ap, max_dma_last_dim=4096), reads=[in_.key], writes=[out.key], dma=True)
        else:
            self.add(q, lambda e: e.dma_start(out=out.ap, in_=in_.ap), reads=[in_.key], writes=[out.key], dma=True)

    def emit(self, extra_ctx=None):
        nc = self.nc
        ops = self.ops
        for e in ENGS:
            for op in ops[e]:
                for (e2, i2) in op.deps:
                    ops[e2][i2].sig = True
        for e in ENGS:
            c = 0
            for op in ops[e]:
                if op.sig and not op.dma:
                    c += 1
                op.cnt = c
        from contextlib import ExitStack
        with ExitStack() as es:
            sems = {e: es.enter_context(nc.semaphore("s_" + e)) for e in ENGS}
            dsems = {q: [es.enter_context(nc.semaphore("d_%s%d" % (q, j))) for j in range(NDS)] for q in DMAQ}
            for q in DMAQ:
                for n, idx in enumerate(self.ndma[q]):
                    op = ops[q][idx]
                    op.dsem = dsems[q][n % NDS]
                    op.dval = 16 * (n // NDS + 1)
            ncc = 0
            for e in ENGS:
                for op in ops[e]:
                    if op.cc:
                        op.dsem = es.enter_context(nc.semaphore("ccs%d" % ncc))
                        ncc += 1
                        op.dval = 1
            block = es.enter_context(nc.Block())
            self.nwaits = 0

            def run(ename, eng):
                known = {}
                for idx, op in enumerate(ops[ename]):
                    need = {}
                    for (e2, i2) in op.deps:
                        o2 = ops[e2][i2]
                        if o2.dma:
                            sem, val = o2.dsem, o2.dval
                        else:
                            sem, val = sems[e2], o2.cnt
                        k = id(sem)
                        if known.get(k, 0) >= val:
                            continue
                        if k not in need or need[k][1] < val:
                            need[k] = (sem, val)
                    for k, (sem, val) in need.items():
                        eng.wait_ge(sem, val)
                        known[k] = val
                        self.nwaits += 1
                    if op.fn is None:
                        assert not op.sig
                        continue
                    ins = op.fn(eng)
                    if op.cc:
                        ins.then_inc(op.dsem)
                    elif op.dma:
                        ins.then_inc(op.dsem, 16)
                    elif op.sig:
                        ins.then_inc(sems[ename], 1)

            @block.tensor
            def _(t):
                run("pe", t)

            @block.scalar
            def _(a):
                run("act", a)

            @block.vector
            def _(v):
                run("dve", v)

            @block.gpsimd
            def _(g):
                run("pool", g)

            @block.sync
            def _(s):
                run("sp", s)


class Arena:
    def __init__(self, nc, P, nbytes):
        self.P = P
        self.nbytes = nbytes
        self.h = nc.alloc_sbuf_tensor("arena", [128, nbytes // 4], F32)
        self.off = 0

    def alloc(self, shape, dtype, parts=128):
        n = 1
        for x in shape:
            n *= x
        esz = 4 if dtype == F32 else 2
        nb = (n * esz + 63) // 64 * 64
        assert self.off + nb <= self.nbytes, ("SBUF arena overflow", self.off, nb, self.nbytes)
        w0 = self.off // 4
        self.off += nb
        ap = self.h[0:parts, w0:w0 + nb // 4]
        if dtype != F32:
            ap = ap.bitcast(dtype)
        ap = ap[:, 0:n]
        if len(shape) == 2:
            ap = ap.rearrange("p (a b) -> p a b", a=shape[0])
        elif len(shape) == 3:
            ap = ap.rearrange("p (a b c) -> p a b c", a=shape[0], b=shape[1])
        elif len(shape) == 4:
            ap = ap.rearrange("p (a b c d) -> p a b c d", a=shape[0], b=shape[1], c=shape[2])
        return Tile(ap, self.P.newtid())

    def mark(self):
        return self.off

    def release(self, m):
        self.P.barrier()
        self.off = m


def build_program(PHASES=("A", "B", "F0", "H", "F1"), dbg_fn=None):
    nc = bass.Bass("TRN2", target_bir_lowering=False)
    P = Prog(nc)

    def din(name, shape, dt=F32):
        return Tile(nc.dram_tensor(name, list(shape), dt, kind="ExternalInput").ap(), P.newtid())

    def dscr(name, shape, dt=F32):
        if os.environ.get("K_DBG") == "1":
            return Tile(nc.dram_tensor(name, list(shape), dt, kind="ExternalOutput").ap(), P.newtid())
        return Tile(nc.dram_tensor(name, list(shape), dt), P.newtid())

    xo = din("xo", [HALF, D])
    cx = din("cx", [CTX, D])
    cvec = din("cvec", [128, 8, 2])
    rope_o = din("rope_o", [2, 128, HALF])
    pimat_d = din("pimat", [128, 128])
    ada_w = din("ada_w", [2, D, 6 * D])
    ada_b = din("ada_b", [128, 2, 48])
    nmw_d = din("nmw", [128, 2, 8])
    nfw_d = din("nfw", [128, 2, 8])
    fnw_d = din("fnw", [128, 8])
    wqkv_d = din("wqkv", [D, 1536])
    qkn_d = din("qkn", [128, 2])
    wo_d = din("wo", [64, 16, D])
    hwin_d = din("hwin", [D, 5 * D])
    lbl_d = din("lbl", [128, 2, 8])
    onw_d = din("onw", [128, 8])
    hwo_d = din("hwo", [D, D])
    fwin_d = din("fwin", [2, D, 2 * FF])
    fwout_d = din("fwout", [2, FF, D])
    yout = Tile(nc.dram_tensor("y", [HALF, D], F32, kind="ExternalOutput").ap(), P.newtid())

    NTOK = HALF + CTX
    H0 = dscr("H0", [128, 8, NTOK])
    QS = dscr("QS", [128, 8, NTOK], BF16)
    H1 = dscr("H1", [128, 8, NTOK])
    H2 = dscr("H2", [128, 8, NTOK])
    OF = dscr("OF", [128, 8, HALF])
    H3 = dscr("H3", [128, 8, HALF])
    if os.environ.get("K_DBG") == "1":
        H4 = dscr("H4", [128, 8, HALF])
        H5 = dscr("H5", [128, 8, HALF])
    NOWN = HALF // 128
    fwin_b = dscr("fwin_b", [2, D, 2 * FF], BF16)
    fwout_r = dscr("fwout_r", [2, 8, 128, 22 * 128], BF16)
    hwin_b = dscr("hwin_b", [D, 5 * D], BF16)
    hwo_b = dscr("hwo_b", [D, D], BF16)
    KX_src = nc.dram_tensor("KXs", [256, HALF], BF16)
    KX_dst = nc.dram_tensor("KXd", [512, HALF], BF16)
    NVH = NOWN // 2
    VX_src = [nc.dram_tensor("VXs%d" % i, [128, NVH * 260], BF16) for i in range(2)]
    VX_dst = [nc.dram_tensor("VXd%d" % i, [256, NVH * 260], BF16) for i in range(2)]
    KXs, KXd = Tile(KX_src, P.newtid()), Tile(KX_dst, P.newtid())
    VXs = [Tile(t_, P.newtid()) for t_ in VX_src]
    VXd = [Tile(t_, P.newtid()) for t_ in VX_dst]
    SX_src = nc.dram_tensor("SXs", [128, 1024], F32)
    SX_dst = nc.dram_tensor("SXd", [256, 1024], F32)
    SXs = Tile(SX_src, P.newtid())
    SXd = Tile(SX_dst, P.newtid())
    parw_d = din("parw", [128, 2])

    A = Arena(nc, P, 207 * 1024)
    pp = [nc.alloc_psum_tensor("pp%d" % i, [128, 1024], F32) for i in range(4)]
    psum = [Tile(pp[i // 2][:, (i % 2) * 512:(i % 2 + 1) * 512], P.newtid()) for i in range(8)]

    def ppair(k, W):
        return V(pp[k][:, :].rearrange("p (a b) -> p a b", a=2)[:, :, 0:W], (psum[2 * k].tid, None), extra=((psum[2 * k + 1].tid, None),))
    for t_ in psum:
        P.excl.add(t_.tid)

    ident = A.alloc([128], F32)
    identb = A.alloc([128], BF16)
    ones_b = A.alloc([128], BF16)
    blk_b = A.alloc([128], BF16)
    ones_f = A.alloc([128], F32)
    pimat = A.alloc([128], F32)
    zero_f = A.alloc([128], F32)
    P.memset("pool", zero_f[:, :], 0.0)
    P.memset("pool", ones_f[:, :], 1.0)
    P.add("pool", lambda e: e.affine_select(out=ident[:, :].ap, in_=zero_f[:, :].ap, pattern=[[-1, 128]],
                                            compare_op=ALU.not_equal, fill=1.0, base=0, channel_multiplier=1),
          reads=[zero_f[:, :].key], writes=[ident[:, :].key])
    P.copy("pool", identb[:, :], ident[:, :])
    P.copy("pool", ones_b[:, :], ones_f[:, :])
    P.memset("pool", blk_b[:, :], 0.0)
    P.memset("pool", blk_b[0:64, 0:64], 1.0)
    P.memset("pool", blk_b[64:128, 64:128], 1.0)
    P.dma("sp", pimat[:, :], pimat_d[:, :])

    smallv = A.alloc([256], F32)
    sv_off = [0]

    def small(n):
        o = sv_off[0]
        sv_off[0] += n
        assert sv_off[0] <= 256
        return o

    o_cv = small(16)
    o_nmw = small(16)
    o_nfw = small(16)
    o_fnw = small(8)
    o_qkn = small(2)
    o_lbl = small(16)
    o_onw = small(8)
    o_lb = small(8)
    o_l1 = small(8)
    o_nl = small(8)
    o_csil = small(16)
    o_parw = small(2)
    sv = smallv

    def svv(o, n):
        return sv[:, o:o + n]

    P.dma("sp", svv(o_cv, 16), cvec[:, :, :].re("p a b -> p (a b)"))
    P.dma("sp", svv(o_nmw, 16), nmw_d[:, :, :].re("p a b -> p (a b)"))
    P.dma("sp", svv(o_nfw, 16), nfw_d[:, :, :].re("p a b -> p (a b)"))
    P.dma("sp", svv(o_fnw, 8), fnw_d[:, :])
    P.dma("sp", svv(o_qkn, 2), qkn_d[:, :])
    P.dma("sp", svv(o_lbl, 16), lbl_d[:, :, :].re("p a b -> p (a b)"))
    P.dma("sp", svv(o_onw, 8), onw_d[:, :])
    P.dma("sp", svv(o_parw, 2), parw_d[:, :])
    parw = sv.view(sv.h[:, o_parw:o_parw + 2])
    adab = A.alloc([96], F32)
    P.dma("sp", adab[:, :], ada_b[:, :, :].re("p a b -> p (a b)"))
    lbe = A.alloc([24], F32)
    P.act(lbe[:, 0:16], svv(o_lbl, 16), AF.Exp)
    P.tt("dve", lbe[:, 16:24], lbe[:, 0:8], lbe[:, 8:16], ALU.add)
    P.recip(lbe[:, 16:24], lbe[:, 16:24])
    P.tt("dve", svv(o_lb, 8), lbe[:, 8:16], lbe[:, 16:24], ALU.mult)
    P.ts("dve", svv(o_l1, 8), svv(o_lb, 8), -1.0, ALU.mult, 1.0, ALU.add)
    P.ts("dve", svv(o_nl, 8), svv(o_l1, 8), -1.0, ALU.mult)
    P.act(svv(o_csil, 16), svv(o_cv, 16), AF.Silu)

    modv = A.alloc([2, 48, 2], F32)
    modA = A.alloc([2, 2, 8, 2], F32)
    m0 = A.mark()
    wblk = [A.alloc([8, 512], F32) for _ in range(2)]
    modrow = A.alloc([6 * D], F32, parts=2)
    csil = svv(o_csil, 16)
    n_ada = 0
    for l in range(2):
        for blk in range(12):
            wb = wblk[n_ada % 2]
            n_ada += 1
            P.dma("sp", wb[:, :, :], ada_w[l, :, blk * 512:(blk + 1) * 512].re("(kc p) n -> p kc n", p=128))
            prow = psum[2 + blk % 2]
            for kc in range(8):
                P.mm(prow[0:2, :], V(sv.h[:, o_csil + kc * 2:o_csil + kc * 2 + 2], csil.key), wb[:, kc, :],
                     start=(kc == 0), stop=(kc == 7))
            P.copy("act" if blk % 2 else "dve", modrow[0:2, blk * 512:(blk + 1) * 512], prow[0:2, :])
        for jj in range(48):
            P.tr(psum[l][:, jj * 2:jj * 2 + 2], modrow[0:2, jj * 128:(jj + 1) * 128], ident[0:2, 0:2])
        P.tt("dve", modv[:, l, :, :], psum[l][:, 0:96].re("p (a b) -> p a b", b=2),
             adab[:, l * 48:(l + 1) * 48].re("p (a b) -> p a b", b=1).bc([128, 48, 2]), ALU.add)
        for nrm, (jsc, ow) in enumerate(((8, o_nmw), (32, o_nfw))):
            wv = sv[:, ow + l * 8:ow + l * 8 + 8].re("p (a b) -> p a b", b=1).bc([128, 8, 2])
            P.stt(modA[:, l, nrm, :, :], modv[:, l, jsc:jsc + 8, :], 1.0, wv, ALU.add, ALU.mult)
    A.release(m0)

    def mod(l, kind, which):
        return lambda kc: modv[:, l, kind * 8 + kc, which:which + 1]

    def modAv(l, nrm, which):
        return lambda kc: modA[:, l, nrm, kc, which:which + 1]

    rr = [0]

    def evac_eng():
        rr[0] += 1
        return "act" if rr[0] % 2 else "dve"

    def load_xT(src_rows, W, xin, xT, pbanks):
        nj = W // 128
        P.dma("sp", xin[:, 0:nj, :], src_rows.re("(j p) d -> p j d", p=128))
        for kc in range(8):
            pb = pbanks[kc % len(pbanks)]
            for j in range(nj):
                P.tr(pb[:, j * 128:(j + 1) * 128], xin[:, j, kc * 128:(kc + 1) * 128], ident[:, :])
            P.copy(evac_eng(), xT[:, kc, 0:W], pb[:, 0:W])

    def norm_mod(xT, W, Af, Bf, hn, sq, tmpf, pbank, nfeat=1024.0):
        for kc in range(8):
            P.act(sq[:, kc, 0:W], xT[:, kc, 0:W], AF.Square)
        for kc in range(8):
            P.mm(pbank[:, 0:W], ones_b[:, :], sq[:, kc, 0:W], start=(kc == 0), stop=(kc == 7))
        P.act(tmpf[0][:, 0:W], pbank[:, 0:W], AF.Ln, scale=1.0 / nfeat, bias=epsv[:, 0:1])
        P.act(tmpf[0][:, 0:W], tmpf[0][:, 0:W], AF.Exp, scale=-0.5)
        for kc in range(8):
            t = tmpf[1 + kc % 2]
            P.tt("dve", t[:, 0:W], xT[:, kc, 0:W], tmpf[0][:, 0:W], ALU.mult)
            P.ts("pool", hn[:, kc, 0:W], t[:, 0:W], Af(kc), ALU.mult, (0.0 if Bf is None else Bf(kc)), ALU.add)

    epsv = A.alloc([1], F32)
    P.memset("pool", epsv[:, :], EPS)

    def layer0():
        mL0 = A.mark()
        NKC = (2 * HALF + CTX) // 128
        KT = A.alloc([2, NKC * 128], BF16)
        VA = A.alloc([NKC, 4, 65], BF16)
        P.memset("pool", VA[:, :, :, 64:65], 1.0)
        mA = A.mark()
        wqkv = A.alloc([8, 1536], BF16)
        for kc in range(8):
            P.dma("pool", wqkv[:, kc, :], wqkv_d[kc * 128:(kc + 1) * 128, :])
        xin = [A.alloc([4, 1024], F32) for _ in range(2)]
        xT2 = [A.alloc([8, 512], F32) for _ in range(2)]
        hn = A.alloc([8, 512], BF16)
        rtab = [A.alloc([2, 512], F32) for _ in range(2)]
        qT = A.alloc([8, 512], BF16)
        sq = qT
        sqh2 = [A.alloc([512], BF16) for _ in range(2)]
        kf2 = [A.alloc([512], F32) for _ in range(2)]
        rs2 = [A.alloc([512], F32) for _ in range(2)]
        t12 = [A.alloc([512], F32) for _ in range(2)]
        t22 = [A.alloc([512], F32) for _ in range(2)]
        tmpf = [A.alloc([512], F32), t12[0], t22[0]]
        qkc = [0]

        def qknorm_rope(ps, W, wv, rt, outv):
            SUB = 9
            par = qkc[0] % 2
            qkc[0] += 1
            sqh, kf, rs, t1, t2 = sqh2[par], kf2[par], rs2[par], t12[par], t22[par]
            pssq = psum[5] if par == 0 else psum[0]
            pkp = psum[6] if par == 0 else psum[1]
            P.act(sqh[:, 0:W], ps[:, 0:W], AF.Square)
            if SUB < 2:
                return
            P.ts("dve", kf[:, 0:W], ps[:, 0:W], wv, ALU.mult)
            P.mm(pssq[:, 0:W], blk_b[:, :], sqh[:, 0:W])
            P.act(rs[:, 0:W], pssq[:, 0:W], AF.Ln, scale=1.0 / 64.0, bias=epsv[:, 0:1])
            P.act(rs[:, 0:W], rs[:, 0:W], AF.Exp, scale=-0.5)
            if SUB < 3:
                return
            if rt is not None:
                P.mm(pkp[:, 0:W], pimat[:, :], kf[:, 0:W])
                P.tt("dve", t1[:, 0:W], kf[:, 0:W], rt[:, 0, 0:W], ALU.mult)
                P.tt("dve", t2[:, 0:W], pkp[:, 0:W], rt[:, 1, 0:W], ALU.mult)
                P.tt("pool", t1[:, 0:W], t1[:, 0:W], t2[:, 0:W], ALU.add)
                P.tt("dve", outv, t1[:, 0:W], rs[:, 0:W], ALU.mult)
            else:
                P.tt("dve", outv, kf[:, 0:W], rs[:, 0:W], ALU.mult)

        KTx, KTc, VAx, VAc = KT.s("x"), KT.s("c"), VA.s("x"), VA.s("c")
        def exchange_kv():
            for c in range(2):
                P.dma("sp", KXs[c * 128:(c + 1) * 128, :], KTx[:, c, 0:HALF])
            for i in range(2):
                P.dma("sp", VXs[i][:, :], VAx[:, i * NVH:(i + 1) * NVH, :, :].re("p j h d -> p (j h d)"))
            grp = [[2 * i, 2 * i + 1] for i in range(NB)]
            P.add("pool", lambda e: e.collective_compute("AllGather", ALU.bypass, ins=[KX_src.ap().opt()], outs=[KX_dst.ap().opt()],
                                                         replica_groups=grp),
                  reads=[KXs[:, :].key], writes=[KXd[:, :].key], cc=True)
            for i in range(2):
                P.add("pool", lambda e, i=i: e.collective_compute("AllGather", ALU.bypass, ins=[VX_src[i].ap().opt()], outs=[VX_dst[i].ap().opt()],
                                                                  replica_groups=grp),
                      reads=[VXs[i][:, :].key], writes=[VXd[i][:, :].key], cc=True)
            for r in range(2):
                for c in range(2):
                    P.dma("sp", KTx[:, c, r * HALF:(r + 1) * HALF], KXd[(2 * r + c) * 128:(2 * r + c + 1) * 128, :])
                for i in range(2):
                    P.dma("sp", VAx[:, r * NOWN + i * NVH:r * NOWN + (i + 1) * NVH, :, :].re("p j h d -> p (j h d)"), VXd[i][r * 128:(r + 1) * 128, :])

        tiles = [("own", i) for i in range(HALF // 512)] + [("ctx", 0)]
        for tn, (kind, ti) in enumerate(tiles):
            W = 512 if kind != "ctx" else CTX
            which = 1 if kind == "ctx" else 0
            if kind == "own":
                src = xo[ti * 512:(ti + 1) * 512, :]
                kbase = ti * 512
                hcol = ti * 512
            elif kind == "par":
                src = xp[ti * 512:(ti + 1) * 512, :]
                kbase = HALF + ti * 512
                hcol = None
            else:
                src = cx[:, :]
                kbase = 2 * HALF
                hcol = HALF
            xi = xin[tn % 2]
            xT = xT2[tn % 2]
            load_xT(src, W, xi, xT, [psum[0], psum[1]])
            rt = None
            if kind != "ctx":
                rt = rtab[tn % 2]
                rsrc = rope_o
                P.dma("sp", rt[:, :, :], rsrc[:, :, ti * 512:(ti + 1) * 512].re("a p n -> p a n"))
            if hcol is not None:
                P.dma("sp", H0[:, :, hcol:hcol + W], xT[:, :, 0:W])
            LVL = int(os.environ.get("K_LVL", "9"))
            if LVL < 2:
                continue
            norm_mod(xT, W, modAv(0, 0, which), mod(0, 0, which), hn, sq, tmpf, psum[2])
            if LVL < 3:
                continue
            for c in range(2):
                pb = psum[3 + c % 2]
                for kc in range(8):
                    P.mm(pb[:, 0:W], wqkv[:, kc, 1024 + c * 128:1024 + (c + 1) * 128], hn[:, kc, 0:W],
                         start=(kc == 0), stop=(kc == 7))
                qknorm_rope(pb, W, svv(o_qkn + 1, 1), rt, (KTc if kind == "ctx" else KTx)[:, c, kbase:kbase + W])
            for j in range(W // 128 if LVL >= 4 else 0):
                pb = psum[7]
                for kc in range(8):
                    P.mm(pb[:, 0:256], hn[:, kc, j * 128:(j + 1) * 128], wqkv[:, kc, 1280:1536],
                         start=(kc == 0), stop=(kc == 7))
                P.copy(evac_eng(), (VAc if kind == "ctx" else VAx)[:, kbase // 128 + j, :, 0:64], pb[:, 0:256].re("p (h d) -> p h d", d=64))
            if kind == "own" and ti == HALF // 512 - 1:
                exchange_kv()
            if hcol is not None and LVL >= 5:
                for c in range(8):
                    pb = psum[3 + c % 2]
                    for kc in range(8):
                        P.mm(pb[:, 0:W], wqkv[:, kc, c * 128:(c + 1) * 128], hn[:, kc, 0:W],
                             start=(kc == 0), stop=(kc == 7))
                    qknorm_rope(pb, W, svv(o_qkn, 1), rt, qT[:, c, 0:W])
                P.dma("sp", QS[:, :, hcol:hcol + W], qT[:, :, 0:W])
        A.release(mA)

        if "B" not in PHASES:
            A.release(mL0)
            return
        wo = A.alloc([16, 1024], BF16, parts=64)
        P.dma("pool", wo[:, :, :], wo_d[:, :, :])
        for l in range(2):
            for kc in range(8):
                P.dma("pool", fwin_b[l, kc * 128:(kc + 1) * 128, :], fwin_d[l, kc * 128:(kc + 1) * 128, :])
            for oc in range(8):
                P.add("pool", lambda e, l=l, oc=oc: e.dma_start(
                          out=fwout_r.h[l, oc, :, :].rearrange("p (a b) -> p a b", b=128),
                          in_=fwout_d.h[l, :, oc * 128:(oc + 1) * 128].rearrange("(a p) b -> p a b", p=128)),
                      reads=[fwout_d[l, :, :].key], writes=[(fwout_r.tid, (l, oc))], dma=True)
            if l == 0:
                for kc in range(8):
                    P.dma("pool", hwin_b[kc * 128:(kc + 1) * 128, :], hwin_d[kc * 128:(kc + 1) * 128, :])
                    P.dma("pool", hwo_b[kc * 128:(kc + 1) * 128, :], hwo_d[kc * 128:(kc + 1) * 128, :])
        qTb = [A.alloc([8, 512], BF16) for _ in range(2)]
        xTb = [A.alloc([8, 512], F32) for _ in range(2)]
        PT = [A.alloc([2, 512], BF16) for _ in range(3)]
        osb = A.alloc([2, 512], F32)
        rinv = A.alloc([2, 512], F32)
        rb = A.alloc([2, 512], F32)
        P.memset("pool", rinv[:, :, :], 1.0)
        oT = A.alloc([16, 512], BF16)
        hmid = A.alloc([8, 512], F32)
        SCALE = 64 ** -0.5
        qtiles = [("own", i) for i in range(HALF // 512)] + [("ctx", 0)]
        for tn, (kind, ti) in enumerate(qtiles):
            W = 512 if kind == "own" else CTX
            which = 0 if kind == "own" else 1
            hcol = ti * 512 if kind == "own" else HALF
            kcs = list(range(NKC)) if kind == "own" else [NKC - 2, NKC - 1]
            qb = qTb[tn % 2]
            xb = xTb[tn % 2]
            P.dma("sp", qb[:, :, 0:W], QS[:, :, hcol:hcol + W])
            P.dma("sp", xb[:, :, 0:W], H0[:, :, hcol:hcol + W])
            step = 0
            for c in range(8):
                kvc = c // 4
                Sb = [(psum[0], psum[1]), (psum[2], psum[3]), (psum[4], psum[5])]
                Ob = (psum[6], psum[7])

                def QK(i, st):
                    kc = kcs[i]
                    sa, sbb = Sb[st % 3]
                    P.mm(sa[:, 0:W], KT[0:64, kvc, kc * 128:(kc + 1) * 128], qb[0:64, c, 0:W])
                    P.mm(sbb[:, 0:W], KT[64:128, kvc, kc * 128:(kc + 1) * 128], qb[64:128, c, 0:W])

                def EXP(i, st):
                    pt = PT[st % 3]
                    P.act(pt[:, :, 0:W], ppair(st % 3, W), AF.Exp, scale=SCALE)

                def PV(i, st):
                    kc = kcs[i]
                    pt = PT[st % 3]
                    for ab in range(2):
                        P.mm(Ob[ab][0:65, 0:W], VA[:, kc, 2 * kvc + ab, 0:65], pt[:, ab, 0:W],
                             start=(i == 0), stop=(i == len(kcs) - 1))

                n = len(kcs)
                for j in range(min(2, n)):
                    QK(j, step + j)
                for i in range(n):
                    if i + 2 < n:
                        QK(i + 2, step + i + 2)
                    EXP(i, step + i)
                    PV(i, step + i)
                step += n
                for ab in range(2):
                    P.copy("dve", osb[0:65, ab, 0:W], Ob[ab][0:65, 0:W])
                P.recip(rinv[64:65, :, 0:W], osb[64:65, :, 0:W])
                P.shuf(rb[0:32, :, 0:W], rinv[64:96, :, 0:W], [0] * 32)
                P.shuf(rb[32:64, :, 0:W], rinv[64:96, :, 0:W], [0] * 32)
                for ab in range(2):
                    P.tt("dve", oT[0:64, 2 * c + ab, 0:W], osb[0:64, ab, 0:W], rb[0:64, ab, 0:W], ALU.mult)
            for oc in range(8):
                pb = psum[oc % 6]
                for j in range(16):
                    P.mm(pb[:, 0:W], wo[0:64, j, oc * 128:(oc + 1) * 128], oT[0:64, j, 0:W], start=(j == 0), stop=(j == 15))
                P.stt(hmid[:, oc, 0:W], pb[:, 0:W], mod(0, 2, which)(oc), xb[:, oc, 0:W], ALU.mult, ALU.add)
            P.dma("sp", H1[:, :, hcol:hcol + W], hmid[:, :, 0:W])
        A.release(mL0)


    if "A" in PHASES:
        layer0()

    def ffn_phase(l, src, dst, tilespecs, final=False):
        m = A.mark()
        win = A.alloc([8, 2 * FF], BF16)
        for kc in range(8):
            P.dma("sp", win[:, kc, :], fwin_b[l, kc * 128:(kc + 1) * 128, :])
        wob = [A.alloc([22, 128], BF16) for _ in range(3)]
        h2 = [A.alloc([8, 512], F32) for _ in range(2)]
        hn2 = [A.alloc([8, 512], BF16) for _ in range(2)]
        sq = A.alloc([8, 512], BF16)
        tmpf = [A.alloc([512], F32) for _ in range(3)]
        sa = [A.alloc([512], F32) for _ in range(2)]
        sT = A.alloc([22, 512], BF16)
        if final:
            fo = sT.view(sT.h[:, 0:16, :].rearrange("p a b -> p (a b)").bitcast(F32).rearrange("p (a b) -> p a b", a=8))
        nt = len(tilespecs)
        nwo = [0]

        def load(t):
            col, W, which = tilespecs[t]
            P.dma("sp", h2[t % 2][:, :, 0:W], src[:, :, col:col + W])

        def norm(t):
            col, W, which = tilespecs[t]
            norm_mod(h2[t % 2], W, modAv(l, 1, which), mod(l, 3, which), hn2[t % 2], sq, tmpf, psum[0])

        def stage_in(t):
            col, W, which = tilespecs[t]
            hn = hn2[t % 2]
            for hc in range(22):
                pa = psum[1 + 2 * (hc % 2)]
                pu = psum[2 + 2 * (hc % 2)]
                for kc in range(8):
                    P.mm(pa[:, 0:W], win[:, kc, hc * 128:(hc + 1) * 128], hn[:, kc, 0:W], start=(kc == 0), stop=(kc == 7))
                for kc in range(8):
                    P.mm(pu[:, 0:W], win[:, kc, FF + hc * 128:FF + (hc + 1) * 128], hn[:, kc, 0:W], start=(kc == 0), stop=(kc == 7))
                s_ = sa[hc % 2]
                P.act(s_[:, 0:W], pa[:, 0:W], AF.Silu)
                P.tt("dve", sT[:, hc, 0:W], s_[:, 0:W], pu[:, 0:W], ALU.mult)

        def stage_out(t):
            col, W, which = tilespecs[t]
            h = h2[t % 2]
            for oc in range(8):
                wb = wob[nwo[0] % 3]
                nwo[0] += 1
                P.dma("sp", wb[:, :, :].re("p a b -> p (a b)"), fwout_r.s((l, oc))[l, oc, :, :])
                pb = psum[5 + oc % 2]
                for hc in range(22):
                    P.mm(pb[:, 0:W], wb[:, hc, :], sT[:, hc, 0:W], start=(hc == 0), stop=(hc == 21))
                P.stt(h[:, oc, 0:W], pb[:, 0:W], mod(l, 5, which)(oc), h[:, oc, 0:W], ALU.mult, ALU.add)
            if not final:
                P.dma("sp", dst[:, :, col:col + W], h[:, :, 0:W])
            else:
                norm_mod(h, W, lambda kc: svv(o_fnw + kc, 1), lambda kc: zero_f[:, 0:1], fo, sq, tmpf, psum[0])
                for j in range(W // 128):
                    for half in range(2):
                        pb = psum[1 + (2 * j + half) % 4]
                        for q in range(4):
                            kc = half * 4 + q
                            P.tr(pb[:, q * 128:(q + 1) * 128], fo[:, kc, j * 128:(j + 1) * 128], ident[:, :])
                        yb = sa[(2 * j + half) % 2]
                        P.copy(evac_eng(), yb[:, :], pb[:, :])
                        P.dma("sp", dst[col + j * 128:col + (j + 1) * 128, half * 512:(half + 1) * 512], yb[:, :])

        load(0)
        if nt > 1:
            load(1)
        norm(0)
        for t in range(nt):
            stage_in(t)
            if t + 1 < nt and not final:
                norm(t + 1)
            stage_out(t)
            if t + 1 < nt and final:
                norm(t + 1)
            if t + 2 < nt:
                load(t + 2)
        A.release(m)

    lat_tiles = [(i * 512, 512, 0) for i in range(HALF // 512)]
    if "F0" in PHASES:
        ffn_phase(0, H1, H2, lat_tiles + [(HALF, CTX, 1)])

    def hgrn_layer():
        mH = A.mark()
        hw = A.alloc([8, 4 * D], BF16)
        hwo = A.alloc([8, D], BF16)
        S32 = A.alloc([8, 128], F32)
        Sbf2 = [A.alloc([8, 128], BF16) for _ in range(2)]
        sc = [0]
        maskF = A.alloc([64], F32, parts=64)
        maskB = A.alloc([64], F32, parts=64)
        rmask = A.alloc([512], BF16)
        P.memset("pool", maskF[:, :], 1.0)
        P.memset("pool", maskB[:, :], 1.0)
        P.add("pool", lambda e: e.affine_select(out=maskF[:, :].ap, in_=maskF[:, :].ap, pattern=[[1, 64]],
                                                compare_op=ALU.is_ge, fill=0.0, base=0, channel_multiplier=-1),
              reads=[maskF[:, :].key], writes=[maskF[:, :].key])
        P.add("pool", lambda e: e.affine_select(out=maskB[:, :].ap, in_=maskB[:, :].ap, pattern=[[-1, 64]],
                                                compare_op=ALU.is_ge, fill=0.0, base=0, channel_multiplier=1),
              reads=[maskB[:, :].key], writes=[maskB[:, :].key])
        P.memset("pool", rmask[:, :], 1.0)
        P.memset("pool", V(rmask.h[:, :].rearrange("p (c t) -> p c t", t=64)[:, :, 0:1], rmask[:, :].key), 0.0)
        h = A.alloc([8, 512], F32)
        sq = A.alloc([8, 512], BF16)
        hn = A.alloc([8, 512], BF16)
        sg2 = [A.alloc([512], F32) for _ in range(2)]
        gT2 = [A.alloc([512], F32) for _ in range(2)]
        kT2 = [A.alloc([512], F32) for _ in range(2)]
        LT2 = [A.alloc([512], F32) for _ in range(2)]
        eL2 = [A.alloc([512], F32) for _ in range(2)]
        e22 = [A.alloc([512], F32) for _ in range(2)]
        tmpf = [gT2[0], gT2[1], kT2[0]]
        gs = sg2[0]
        eTotA = A.alloc([8, 8], F32)
        qtA = A.alloc([8, 512], BF16)
        kvA = A.alloc([16, 512], BF16)
        khA = kvA.view(kvA.h[:, 0:8, :], "kh")
        vTA = kvA.view(kvA.h[:, 8:16, :], "vT")
        Rst = kvA.view(kvA.h[:, :, :].rearrange("p a b -> p (a b)").bitcast(F32).rearrange("p (a b) -> p a b", a=8))
        ktok2 = [A.alloc([8, 128], BF16, parts=64) for _ in range(2)]
        vtok2 = [A.alloc([8, 128], BF16, parts=64) for _ in range(2)]
        Am2 = [A.alloc([8, 64], BF16, parts=64) for _ in range(2)]
        oTt = A.alloc([8, 512], F32)
        hww = hw.s("w")
        hwow = hwo.s("w")

        class HB:
            def __init__(self, fn):
                self.fn = fn

            def __getitem__(self, idx):
                _, hd, sl = idx
                return self.fn(hd, sl)

        def carve(t, a0, sub):
            key = (t.tid, sub)
            return HB(lambda hd, sl: V(t.h[:, a0 + hd // 2, (hd % 2) * 512 + sl.start:(hd % 2) * 512 + sl.stop], key))

        qtB = carve(hw, 0, "qtB") if False else HB(lambda hd, sl: V(hw.h[:, hd // 2, 3 * D + (hd % 2) * 512 + sl.start:3 * D + (hd % 2) * 512 + sl.stop], (hw.tid, "qtB")))
        khB = HB(lambda hd, sl: V(hw.h[:, 4 + hd // 2, 3 * D + (hd % 2) * 512 + sl.start:3 * D + (hd % 2) * 512 + sl.stop], (hw.tid, "khB")))
        vTB = HB(lambda hd, sl: V(hwo.h[:, hd // 2, (hd % 2) * 512 + sl.start:(hd % 2) * 512 + sl.stop], (hwo.tid, "vTB")))
        eTotB = hwo.view(hwo.h[:, 4, 0:128].bitcast(F32).rearrange("p (a b) -> p a b", a=8), "eB")
        bufs = [(qtA, khA, vTA, eTotA), (qtB, khB, vTB, eTotB)]
        pbf = [psum[i].view(psum[i].h[:, 0:512].bitcast(BF16)) for i in range(8)]

        def load_hw(blocks):
            for bi, sb in enumerate(blocks):
                for kc in range(8):
                    P.dma("sp", (hww if bi < 3 else hw)[:, kc, bi * D:(bi + 1) * D], hwin_b[kc * 128:(kc + 1) * 128, sb * D:(sb + 1) * D])

        class TileJob:
            def __init__(self, src_col, W, which, rev, emit_out, xsrc, bset, readout=False, of_col=None):
                self.src_col, self.W, self.which, self.rev, self.emit_out = src_col, W, which, rev, emit_out
                self.xsrc, self.readout, self.of_col = xsrc, readout, of_col
                self.qt, self.kh, self.vT, self.eTot = bufs[bset]
                self.nch = W // 64
                self.order = list(range(self.nch))[::-1] if rev else list(range(self.nch))
                self.mask = maskB if rev else maskF
                self.alone = False

            def load_h(self):
                W = self.W
                P.dma("sp", h[:, :, 0:W], self.xsrc[:, :, self.src_col:self.src_col + W])

            def prologue(self, load=True):
                W = self.W
                if load:
                    self.load_h()
                norm_mod(h, W, modAv(1, 0, self.which), mod(1, 0, self.which), hn, sq, tmpf, psum[7])

            def head(self, hd, standalone=False):
                W, nch, rev = self.W, self.nch, self.rev
                qt, kh, vT, eTot = self.qt, self.kh, self.vT, self.eTot
                par = hd % 2
                sg, gT, kT, LT, eL, e2 = sg2[par], gT2[par], kT2[par], LT2[par], eL2[par], e22[par]
                pz = psum[2] if (standalone and par) else psum[6]
                for kc in range(8):
                    P.mm(pz[:, 0:W], hww[:, kc, D + hd * 128:D + (hd + 1) * 128], hn[:, kc, 0:W], start=(kc == 0), stop=(kc == 7))
                P.act(sg[:, 0:W], pz[:, 0:W], AF.Sigmoid)
                P.act(gT[:, 0:W], sg[:, 0:W], AF.Ln, scale=svv(o_l1 + hd, 1), bias=svv(o_lb + hd, 1))
                P.ts("dve", kT[:, 0:W], sg[:, 0:W], svv(o_nl + hd, 1), ALU.mult, svv(o_l1 + hd, 1), ALU.add)
                P.scan(LT[:, 0:W], rmask[:, 0:W], gT[:, 0:W], 0.0, ALU.mult, ALU.add)
                LTc = V(LT.h[:, 0:W].rearrange("p (c t) -> p c t", t=64), LT[:, :].key)
                P.act(eTot[:, hd, 0:nch], V(LTc.ap[:, :, 63], LTc.key), AF.Exp)
                Lsrc = LT
                if rev:
                    P.tt("dve", gT[:, 0:W], gT[:, 0:W], LT[:, 0:W], ALU.subtract)
                    tot = V(LTc.ap[:, :, 63:64].to_broadcast([128, nch, 64]), LTc.key)
                    P.tt("dve", sg[:, 0:W].re("p (c t) -> p c t", t=64), gT[:, 0:W].re("p (c t) -> p c t", t=64), tot, ALU.add)
                    Lsrc = sg
                P.act(eL[:, 0:W], Lsrc[:, 0:W], AF.Exp)
                P.act(e2[:, 0:W], Lsrc[:, 0:W], AF.Exp, scale=-1.0)
                P.tt("dve", kh[:, hd, slice(0, W)], kT[:, 0:W], e2[:, 0:W], ALU.mult)
                pq = psum[3] if (standalone and par) else psum[7]
                for kc in range(8):
                    P.mm(pq[:, 0:W], hww[:, kc, hd * 128:(hd + 1) * 128], hn[:, kc, 0:W], start=(kc == 0), stop=(kc == 7))
                P.tt("dve", qt[:, hd, slice(0, W)], pq[:, 0:W], eL[:, 0:W], ALU.mult)
                pv = psum[2] if (standalone and par) else psum[6]
                for kc in range(8):
                    P.mm(pv[:, 0:W], hww[:, kc, 2 * D + hd * 128:2 * D + (hd + 1) * 128], hn[:, kc, 0:W], start=(kc == 0), stop=(kc == 7))
                P.copy("act", vT[:, hd, slice(0, W)], pv[:, 0:W])

            def pre(self, k):
                ci = self.order[k]
                cs = slice(ci * 64, (ci + 1) * 64)
                qt, kh, vT = self.qt, self.kh, self.vT
                ktok, vtok, Am = ktok2[k % 2], vtok2[k % 2], Am2[k % 2]
                if self.emit_out:
                    for hd in range(8):
                        P.mm(psum[4][0:64, hd * 64:(hd + 1) * 64], kh[:, hd, cs], qt[:, hd, cs])
                for hd in range(8):
                    P.tr(pbf[2][0:64, hd * 128:(hd + 1) * 128], kh[:, hd, cs], identb[:, :])
                for hd in range(8):
                    P.tr(pbf[3][0:64, hd * 128:(hd + 1) * 128], vT[:, hd, cs], identb[:, :])
                P.copy("act", ktok[0:64, :, :], pbf[2][0:64, :].re("p (h d) -> p h d", d=128))
                P.copy("dve", vtok[0:64, :, :], pbf[3][0:64, :].re("p (h d) -> p h d", d=128))
                if self.emit_out:
                    P.tt("dve", Am[0:64, :, :], psum[4][0:64, 0:512].re("p (h t) -> p h t", t=64),
                         V(self.mask.h[0:64, :].unsqueeze(1).to_broadcast([64, 8, 64]), self.mask[:, :].key), ALU.mult)

            def preU(self, k):
                ktok, vtok = ktok2[k % 2], vtok2[k % 2]
                up = 3 * (k % 2) if self.alone else 0
                for hd in range(8):
                    pb = psum[2 * up + hd // 4]
                    P.mm(pb[:, (hd % 4) * 128:(hd % 4 + 1) * 128], ktok[0:64, hd, :], vtok[0:64, hd, :])

            def post(self, k):
                ci = self.order[k]
                cs = slice(ci * 64, (ci + 1) * 64)
                qt = self.qt
                ktok, vtok, Am = ktok2[k % 2], vtok2[k % 2], Am2[k % 2]
                Scur = Sbf2[sc[0] % 2]
                Snew = Sbf2[(sc[0] + 1) % 2]
                sc[0] += 1
                up = 3 * (k % 2) if self.alone else 0
                U = V(pp[up][:, :].rearrange("p (h d) -> p h d", d=128), (psum[2 * up].tid, None), extra=((psum[2 * up + 1].tid, None),))
                P.add("dve", lambda e, U=U: e.tensor_tensor(out=S32[:, :, :].ap, in0=U.ap, in1=S32[:, :, :].ap, op=ALU.add),
                      reads=[U.key, U.extra[0], S32[:, :, :].key], writes=[S32[:, :, :].key])
                et = V(self.eTot.h[:, :, ci:ci + 1].to_broadcast([128, 8, 128]), self.eTot[:, :, :].key)
                P.tt("dve", S32[:, :, :], S32[:, :, :], et, ALU.mult)
                P.copy("act", Snew[:, :, :], S32[:, :, :])
                if self.emit_out:
                    for hd in range(8):
                        P.mm(psum[5][:, hd * 64:(hd + 1) * 64], Scur[:, hd, :], qt[:, hd, cs], start=True, stop=False)
                        P.mm(psum[5][:, hd * 64:(hd + 1) * 64], vtok[0:64, hd, :], Am[0:64, hd, :], start=False, stop=True)
                    po = psum[5][:, 0:512].re("p (h t) -> p h t", t=64)
                    if self.of_col is None:
                        P.copy("act", oTt[:, :, cs], po)
                    else:
                        P.tt("dve", oTt[:, :, cs], po, oTt[:, :, cs], ALU.add)

            def epilogue(self):
                if not self.readout:
                    return
                qt = self.qt
                og = qt
                P.dma("sp", Rst[:, :, :], self.xsrc[:, :, self.src_col:self.src_col + 512])
                def epA(hd):
                    par = hd % 2
                    pss = psum[1] if par else psum[5]
                    pg = psum[7] if par else psum[6]
                    P.act(sq[:, hd, :], oTt[:, hd, :], AF.Square)
                    P.mm(pss[:, :], ones_b[:, :], sq[:, hd, :])
                    for kc in range(8):
                        P.mm(pg[:, :], hw[:, kc, 3 * D + hd * 128:3 * D + (hd + 1) * 128], hn[:, kc, :], start=(kc == 0), stop=(kc == 7))
                    P.act(sg2[par][:, :], pg[:, :], AF.Sigmoid)

                def epB(hd):
                    par = hd % 2
                    r0, r1, gsp = gT2[par], kT2[par], sg2[par]
                    pss = psum[1] if par else psum[5]
                    P.act(r0[:, :], pss[:, :], AF.Ln, scale=1.0 / 128.0, bias=epsv[:, 0:1])
                    P.act(r0[:, :], r0[:, :], AF.Exp, scale=-0.5)
                    P.stt(r1[:, :], oTt[:, hd, :], svv(o_onw + hd, 1), r0[:, :], ALU.mult, ALU.mult)
                    P.tt("dve", og[:, hd, slice(0, 512)], r1[:, :], gsp[:, :], ALU.mult)

                epA(0)
                for hd in range(8):
                    if hd + 1 < 8:
                        epA(hd + 1)
                    epB(hd)
                for oc in range(8):
                    pb = psum[2 + oc % 2]
                    for kc in range(8):
                        P.mm(pb[:, :], hwow[:, kc, oc * 128:(oc + 1) * 128], og[:, kc, slice(0, 512)], start=(kc == 0), stop=(kc == 7))
                    P.stt(Rst[:, oc, :], pb[:, :], mod(1, 2, 0)(oc), Rst[:, oc, :], ALU.mult, ALU.add)
                P.dma("sp", H3[:, :, self.src_col:self.src_col + 512], Rst[:, :, :])

        def run_chunks(job, nxt):
            job.alone = nxt is None
            n = job.nch
            job.pre(0)
            job.preU(0)
            for k in range(n):
                if k + 1 < n:
                    job.pre(k + 1)
                job.post(k)
                if nxt is not None and k < 8:
                    if k == 0:
                        nxt.prologue()
                    nxt.head(k)
                if k + 1 < n:
                    job.preU(k + 1)
            if nxt is not None:
                for hd in range(n, 8):
                    if n == 0:
                        nxt.prologue()
                    nxt.head(hd)

        load_hw([0, 2, 4])
        P.memset("dve", S32[:, :, :], 0.0)
        P.memset("dve", Sbf2[0][:, :, :], 0.0)
        jobs = [TileJob(HALF, CTX, 1, False, False, H2, 0)]
        for ti in range(HALF // 512):
            jobs.append(TileJob(ti * 512, 512, 0, False, True, H2, (ti + 1) % 2))
        jobs[0].prologue()
        for hd in range(8):
            jobs[0].head(hd, standalone=True)
        for ji, job in enumerate(jobs):
            nxt = jobs[ji + 1] if ji + 1 < len(jobs) else None
            run_chunks(job, nxt)
            if job.emit_out:
                P.dma("sp", OF[:, :, job.src_col:job.src_col + 512], oTt[:, :, :])
        P.dma("sp", SXs[:, :], S32[:, :, :].re("p h d -> p (h d)"))
        P.add("pool", lambda e: e.collective_compute("AllGather", ALU.bypass, ins=[SX_src.ap().opt()], outs=[SX_dst.ap().opt()],
                                                     replica_groups=[[2 * i, 2 * i + 1] for i in range(NB)]),
              reads=[SXs[:, :].key], writes=[SXd[:, :].key], cc=True)
        load_hw([0, 3, 4, 1])
        for kc in range(8):
            P.dma("sp", hwo[:, kc, :], hwo_b[kc * 128:(kc + 1) * 128, :])
        g0 = h[:, 0:2, :].re("p a b -> p (a b)")
        g1 = oTt[:, 0:2, :].re("p a b -> p (a b)")
        P.dma("sp", g0, SXd[0:128, :])
        P.dma("sp", g1, SXd[128:256, :])
        P.ts("dve", g0, g0, parw[:, 0:1], ALU.mult)
        P.stt(S32[:, :, :].re("p h d -> p (h d)"), g1, parw[:, 1:2], g0, ALU.mult, ALU.add)
        P.copy("act", Sbf2[sc[0] % 2][:, :, :], S32[:, :, :])
        tis = list(reversed(range(HALF // 512)))
        jobs2 = [TileJob(ti * 512, 512, 0, True, True, H2, 0, readout=True, of_col=ti * 512) for ti in tis]
        jobs2[0].load_h()
        for ji, job in enumerate(jobs2):
            job.prologue(load=False)
            P.dma("sp", oTt[:, :, :], OF[:, :, job.src_col:job.src_col + 512])
            if ji + 1 < len(jobs2):
                jobs2[ji + 1].load_h()
            for hd in range(8):
                job.head(hd, standalone=True)
            run_chunks(job, None)
            job.epilogue()
        A.release(mH)

    if "H" in PHASES:
        hgrn_layer()

    if "F1" in PHASES:
        ffn_phase(1, H3, yout, lat_tiles, final=True)

    if dbg_fn is not None:
        dbg_fn(locals())

    P.barrier()
    P.emit()
    return nc, P


_CACHE = {}


def _rope_tables(pos):
    inv = (np.float32(10000.0) ** (-(np.arange(16, dtype=np.float32) * np.float32(2.0) / np.float32(32.0)))).astype(np.float32)
    row = (pos // 64).astype(np.float32)
    col = (pos % 64).astype(np.float32)
    ar = row[:, None] * inv[None, :]
    ac = col[:, None] * inv[None, :]
    cr, sr, cc, sc = np.cos(ar), np.sin(ar), np.cos(ac), np.sin(ac)
    C = np.concatenate([cr, cr, cc, cc], axis=1).T.astype(np.float32)
    S = np.concatenate([-sr, sr, -sc, sc], axis=1).T.astype(np.float32)
    C = np.concatenate([C, C], axis=0)
    S = np.concatenate([S, S], axis=0)
    return np.ascontiguousarray(np.stack([C, S], axis=0))


def _fm(v):
    return np.ascontiguousarray(np.asarray(v).reshape(8, 128).T)


def kernel(x, c, ctx, c_ctx, ada_w, ada_b, norm_mix_w, norm_ffn_w, attn_w_qkv, attn_q_norm,
           attn_k_norm, attn_w_o, hgrn_w_in, hgrn_lb_logits, hgrn_out_norm, hgrn_w_o,
           ffn_w_in, ffn_w_out, final_norm_w):
    in_maps, gather = make_inputs(x, c, ctx, c_ctx, ada_w, ada_b, norm_mix_w, norm_ffn_w, attn_w_qkv, attn_q_norm,
                                  attn_k_norm, attn_w_o, hgrn_w_in, hgrn_lb_logits, hgrn_out_norm, hgrn_w_o,
                                  ffn_w_in, ffn_w_out, final_norm_w)
    if "nc" not in _CACHE:
        _CACHE["nc"] = build_program()[0]
    nc = _CACHE["nc"]
    res = run_bass_kernel_spmd(nc, in_maps, core_ids=list(range(2 * NB)))
    return gather([r["y"] for r in res.results])


def make_inputs(x, c, ctx, c_ctx, ada_w, ada_b, norm_mix_w, norm_ffn_w, attn_w_qkv, attn_q_norm,
                attn_k_norm, attn_w_o, hgrn_w_in, hgrn_lb_logits, hgrn_out_norm, hgrn_w_o,
                ffn_w_in, ffn_w_out, final_norm_w):
    f = lambda a: np.ascontiguousarray(np.asarray(a, dtype=np.float32))
    x, c, ctx, c_ctx = f(x), f(c), f(ctx), f(c_ctx)
    idx0 = np.arange(HALF)
    idx1 = SEQ - 1 - np.arange(HALF)
    rope = [_rope_tables(idx0), _rope_tables(idx1)]
    pim = np.zeros((128, 128), np.float32)
    for m in range(128):
        d = m % 64
        pi = d + 16 if (d % 32) < 16 else d - 16
        pim[(m // 64) * 64 + pi, m] = 1.0
    wqkv = f(attn_w_qkv)[0]
    qcols = np.concatenate([np.arange(h * 64, (h + 1) * 64) for h in HEAD_ORDER])
    wqkv_dev = np.ascontiguousarray(np.concatenate([wqkv[:, qcols], wqkv[:, 1024:]], axis=1))
    wo = f(attn_w_o)[0]
    wo_dev = np.ascontiguousarray(np.stack([wo[h * 64:(h + 1) * 64, :] for h in HEAD_ORDER], axis=1))
    qkn = np.ascontiguousarray(np.stack([np.tile(f(attn_q_norm)[0], 2), np.tile(f(attn_k_norm)[0], 2)], axis=1))
    hwin = f(hgrn_w_in)[0]
    hwin_sw = np.ascontiguousarray(np.concatenate([hwin[:, :2 * D], hwin[:, 3 * D:4 * D], hwin[:, 2 * D:3 * D], hwin[:, 4 * D:]], axis=1))
    adab = np.ascontiguousarray(np.stack([f(ada_b)[l].reshape(48, 128).T for l in range(2)], axis=1))
    nmw = np.ascontiguousarray(np.stack([_fm(f(norm_mix_w)[l]) for l in range(2)], axis=1))
    nfw = np.ascontiguousarray(np.stack([_fm(f(norm_ffn_w)[l]) for l in range(2)], axis=1))
    lbl = np.ascontiguousarray(np.stack([_fm(f(hgrn_lb_logits)[l]) for l in range(2)], axis=1))
    common = {
        "pimat": pim, "ada_w": f(ada_w), "ada_b": adab, "nmw": nmw, "nfw": nfw, "fnw": _fm(f(final_norm_w)),
        "wqkv": wqkv_dev, "qkn": qkn, "wo": wo_dev, "lbl": lbl, "onw": _fm(f(hgrn_out_norm)[0]),
        "hwo": f(hgrn_w_o)[0], "fwin": f(ffn_w_in), "fwout": f(ffn_w_out),
    }
    in_maps = []
    for core in range(2 * NB):
        b, s = core // 2, core % 2
        own = idx0 if s == 0 else idx1
        par = idx1 if s == 0 else idx0
        m = dict(common)
        m["xo"] = np.ascontiguousarray(x[b][own])
        m["cx"] = np.ascontiguousarray(ctx[b] if s == 0 else ctx[b][::-1])
        m["cvec"] = np.ascontiguousarray(np.stack([_fm(c[b]), _fm(c_ctx)], axis=2))
        m["rope_o"] = rope[s]
        m["hwin"] = hwin if s == 0 else hwin_sw
        m["parw"] = np.ascontiguousarray(np.tile(np.array([[0.0, 1.0]] if s == 0 else [[1.0, 0.0]], np.float32), (128, 1)))
        in_maps.append(m)

    def gather(ys):
        out = np.empty((NB, SEQ, D), np.float32)
        for core in range(2 * NB):
            b, s = core // 2, core % 2
            own = idx0 if s == 0 else idx1
            out[b][own] = ys[core]
        return out

    return in_maps, gather
```

```python
import os
import numpy as np
import concourse.bass as bass
import concourse.mybir as mybir
from concourse.bass_utils import run_bass_kernel_spmd

F32 = mybir.dt.float32
BF16 = mybir.dt.bfloat16
AF = mybir.ActivationFunctionType
ALU = mybir.AluOpType

D = 1024
SEQ = 8192
HALF = 4096
CTX = 256
NB = 4
FF = 2816
EPS = 1e-6
HEAD_ORDER = [0, 4, 1, 5, 2, 6, 3, 7, 8, 12, 9, 13, 10, 14, 11, 15]
ENGS = ["pe", "act", "dve", "pool", "sp"]
DMAQ = ("sp", "pool")
NDS = 16


class V:
    __slots__ = ("ap", "key", "extra")

    def __init__(self, ap, key, extra=()):
        self.ap = ap
        self.key = key
        self.extra = extra

    def bc(self, shape):
        return V(self.ap.to_broadcast(list(shape)), self.key)

    def re(self, pat, **kw):
        return V(self.ap.rearrange(pat, **kw), self.key)


class Tile:
    def __init__(self, h, tid, sub=None):
        self.h = h
        self.tid = tid
        self.sub = sub

    def __getitem__(self, idx):
        return V(self.h[idx], (self.tid, self.sub))

    def s(self, sub):
        return Tile(self.h, self.tid, sub)

    def view(self, ap, sub=None):
        return Tile(ap, self.tid, sub)


class Op:
    __slots__ = ("fn", "deps", "sig", "dma", "dsem", "dval", "cnt", "cc")

    def __init__(self, fn, deps, dma, cc=False):
        self.fn = fn
        self.deps = deps
        self.sig = False
        self.dma = dma
        self.dsem = None
        self.dval = 0
        self.cnt = 0
        self.cc = cc


class Prog:
    def __init__(self, nc):
        self.nc = nc
        self.ops = {e: [] for e in ENGS}
        self.last_w = {}
        self.readers = {}
        self.subs = {}
        self.ntid = 0
        self.ndma = {q: [] for q in DMAQ}
        self.dma_since_barrier = []
        self.last_real = {}
        self.excl = set()

    def newtid(self):
        self.ntid += 1
        return self.ntid

    def _conf(self, key):
        tid, sub = key
        ss = self.subs.setdefault(tid, set())
        ss.add(sub)
        if sub is None:
            return [(tid, s) for s in ss]
        return [(tid, sub), (tid, None)] if None in ss else [(tid, sub)]

    def add(self, eng, fn, reads=(), writes=(), dma=False, cc=False):
        idx = len(self.ops[eng])
        deps = set()
        xr = [k for k in reads if k[0] in self.excl]
        if xr:
            reads = [k for k in reads if k[0] not in self.excl]
            writes = list(writes) + xr
        for k in reads:
            for ck in self._conf(k):
                w = self.last_w.get(ck)
                if w is not None:
                    deps.add(w)
        for k in writes:
            for ck in self._conf(k):
                w = self.last_w.get(ck)
                if w is not None:
                    deps.add(w)
                for r in self.readers.get(ck, ()):
                    deps.add(r)
        if cc:
            self.dma_since_barrier.append((eng, idx))
        elif dma:
            lst = self.ndma[eng]
            if len(lst) >= NDS:
                deps.add((eng, lst[len(lst) - NDS]))
            lst.append(idx)
            self.dma_since_barrier.append((eng, idx))
        deps.discard((eng, idx))
        if eng == "pe":
            deps = {d for d in deps if d[0] != "pe"}
        op = Op(fn, deps, dma or cc, cc)
        self.ops[eng].append(op)
        self.last_real[eng] = idx
        me = (eng, idx)
        for k in writes:
            if k[1] is None:
                for ck in self._conf(k):
                    self.last_w[ck] = me
                    self.readers[ck] = []
            else:
                self.last_w[k] = me
                self.readers[k] = []
        for k in reads:
            rl = self.readers.setdefault(k, [])
            if not dma:
                rl[:] = [r for r in rl if r[0] != eng or self.ops[r[0]][r[1]].dma]
            rl.append(me)
        return me

    def barrier(self):
        lasts = [(e, i) for e, i in self.last_real.items()]
        dmas = list(self.dma_since_barrier)
        self.dma_since_barrier = []
        for e in ENGS:
            deps = set(lasts) | set(dmas)
            op = Op(None, deps, False)
            self.ops[e].append(op)

    def mm(self, out, lhsT, rhs, start=True, stop=True):
        self.add("pe", lambda e: e.matmul(out.ap, lhsT=lhsT.ap, rhs=rhs.ap, start=start, stop=stop),
                 reads=[lhsT.key, rhs.key], writes=[out.key])

    def tr(self, out, in_, ident):
        self.add("pe", lambda e: e.transpose(out.ap, in_.ap, ident.ap), reads=[in_.key, ident.key], writes=[out.key])

    def act(self, out, in_, func, scale=1.0, bias=0.0):
        reads = [in_.key] + list(in_.extra)
        sc = scale
        bi = bias
        if isinstance(scale, V):
            reads.append(scale.key)
            sc = scale.ap
        if isinstance(bias, V):
            reads.append(bias.key)
            bi = bias.ap
        self.add("act", lambda e: e.activation(out=out.ap, in_=in_.ap, func=func, bias=bi, scale=sc),
                 reads=reads, writes=[out.key])

    def copy(self, eng, out, in_):
        if eng == "act":
            self.add("act", lambda e: e.copy(out=out.ap, in_=in_.ap), reads=[in_.key], writes=[out.key])
        else:
            self.add(eng, lambda e: e.tensor_copy(out=out.ap, in_=in_.ap), reads=[in_.key], writes=[out.key])

    def tt(self, eng, out, in0, in1, op):
        self.add(eng, lambda e: e.tensor_tensor(out=out.ap, in0=in0.ap, in1=in1.ap, op=op),
                 reads=[in0.key, in1.key], writes=[out.key])

    def ts(self, eng, out, in0, s1, op0, s2=None, op1=None):
        reads = [in0.key]
        a1 = s1
        a2 = s2
        if isinstance(s1, V):
            reads.append(s1.key)
            a1 = s1.ap
        if isinstance(s2, V):
            reads.append(s2.key)
            a2 = s2.ap
        if op1 is None:
            self.add(eng, lambda e: e.tensor_scalar(out=out.ap, in0=in0.ap, scalar1=a1, scalar2=None, op0=op0),
                     reads=reads, writes=[out.key])
        else:
            self.add(eng, lambda e: e.tensor_scalar(out=out.ap, in0=in0.ap, scalar1=a1, scalar2=a2, op0=op0, op1=op1),
                     reads=reads, writes=[out.key])

    def stt(self, out, in0, scalar, in1, op0, op1):
        reads = [in0.key, in1.key]
        sc = scalar
        if isinstance(scalar, V):
            reads.append(scalar.key)
            sc = scalar.ap
        self.add("dve", lambda e: e.scalar_tensor_tensor(out=out.ap, in0=in0.ap, scalar=sc, in1=in1.ap, op0=op0, op1=op1),
                 reads=reads, writes=[out.key])

    def scan(self, out, d0, d1, initial, op0, op1):
        self.add("dve", lambda e: e.tensor_tensor_scan(out=out.ap, data0=d0.ap, data1=d1.ap, initial=initial, op0=op0, op1=op1),
                 reads=[d0.key, d1.key], writes=[out.key])

    def recip(self, out, in_):
        self.add("dve", lambda e: e.reciprocal(out=out.ap, in_=in_.ap), reads=[in_.key], writes=[out.key])

    def shuf(self, out, in_, mask):
        self.add("dve", lambda e: e.stream_shuffle(out=out.ap, in_=in_.ap, mask=mask), reads=[in_.key], writes=[out.key])

    def memset(self, eng, out, val):
        self.add(eng, lambda e: e.memset(out.ap, val), writes=[out.key])

    def dma(self, q, out, in_):
        if q == "pool":
            self.add(q, lambda e: e.dma_start(out=out.ap, in_=in_.ap, max_dma_last_dim=4096), reads=[in_.key], writes=[out.key], dma=True)
        else:
            self.add(q, lambda e: e.dma_start(out=out.ap, in_=in_.ap), reads=[in_.key], writes=[out.key], dma=True)

    def emit(self, extra_ctx=None):
        nc = self.nc
        ops = self.ops
        for e in ENGS:
            for op in ops[e]:
                for (e2, i2) in op.deps:
                    ops[e2][i2].sig = True
        for e in ENGS:
            c = 0
            for op in ops[e]:
                if op.sig and not op.dma:
                    c += 1
                op.cnt = c
        from contextlib import ExitStack
        with ExitStack() as es:
            sems = {e: es.enter_context(nc.semaphore("s_" + e)) for e in ENGS}
            dsems = {q: [es.enter_context(nc.semaphore("d_%s%d" % (q, j))) for j in range(NDS)] for q in DMAQ}
            for q in DMAQ:
                for n, idx in enumerate(self.ndma[q]):
                    op = ops[q][idx]
                    op.dsem = dsems[q][n % NDS]
                    op.dval = 16 * (n // NDS + 1)
            ncc = 0
            for e in ENGS:
                for op in ops[e]:
                    if op.cc:
                        op.dsem = es.enter_context(nc.semaphore("ccs%d" % ncc))
                        ncc += 1
                        op.dval = 1
            block = es.enter_context(nc.Block())
            self.nwaits = 0

            def run(ename, eng):
                known = {}
                for idx, op in enumerate(ops[ename]):
                    need = {}
                    for (e2, i2) in op.deps:
                        o2 = ops[e2][i2]
                        if o2.dma:
                            sem, val = o2.dsem, o2.dval
                        else:
                            sem, val = sems[e2], o2.cnt
                        k = id(sem)
                        if known.get(k, 0) >= val:
                            continue
                        if k not in need or need[k][1] < val:
                            need[k] = (sem, val)
                    for k, (sem, val) in need.items():
                        eng.wait_ge(sem, val)
                        known[k] = val
                        self.nwaits += 1
                    if op.fn is None:
                        assert not op.sig
                        continue
                    ins = op.fn(eng)
                    if op.cc:
                        ins.then_inc(op.dsem)
                    elif op.dma:
                        ins.then_inc(op.dsem, 16)
                    elif op.sig:
                        ins.then_inc(sems[ename], 1)

            @block.tensor
            def _(t):
                run("pe", t)

            @block.scalar
            def _(a):
                run("act", a)

            @block.vector
            def _(v):
                run("dve", v)

            @block.gpsimd
            def _(g):
                run("pool", g)

            @block.sync
            def _(s):
                run("sp", s)


class Arena:
    def __init__(self, nc, P, nbytes):
        self.P = P
        self.nbytes = nbytes
        self.h = nc.alloc_sbuf_tensor("arena", [128, nbytes // 4], F32)
        self.off = 0

    def alloc(self, shape, dtype, parts=128):
        n = 1
        for x in shape:
            n *= x
        esz = 4 if dtype == F32 else 2
        nb = (n * esz + 63) // 64 * 64
        assert self.off + nb <= self.nbytes, ("SBUF arena overflow", self.off, nb, self.nbytes)
        w0 = self.off // 4
        self.off += nb
        ap = self.h[0:parts, w0:w0 + nb // 4]
        if dtype != F32:
            ap = ap.bitcast(dtype)
        ap = ap[:, 0:n]
        if len(shape) == 2:
            ap = ap.rearrange("p (a b) -> p a b", a=shape[0])
        elif len(shape) == 3:
            ap = ap.rearrange("p (a b c) -> p a b c", a=shape[0], b=shape[1])
        elif len(shape) == 4:
            ap = ap.rearrange("p (a b c d) -> p a b c d", a=shape[0], b=shape[1], c=shape[2])
        return Tile(ap, self.P.newtid())

    def mark(self):
        return self.off

    def release(self, m):
        self.P.barrier()
        self.off = m


def build_program(PHASES=("A", "B", "F0", "H", "F1"), dbg_fn=None):
    nc = bass.Bass("TRN2", target_bir_lowering=False)
    P = Prog(nc)

    def din(name, shape, dt=F32):
        return Tile(nc.dram_tensor(name, list(shape), dt, kind="ExternalInput").ap(), P.newtid())

    def dscr(name, shape, dt=F32):
        if os.environ.get("K_DBG") == "1":
            return Tile(nc.dram_tensor(name, list(shape), dt, kind="ExternalOutput").ap(), P.newtid())
        return Tile(nc.dram_tensor(name, list(shape), dt), P.newtid())

    xo = din("xo", [HALF, D])
    cx = din("cx", [CTX, D])
    cvec = din("cvec", [128, 8, 2])
    rope_o = din("rope_o", [2, 128, HALF])
    pimat_d = din("pimat", [128, 128])
    ada_w = din("ada_w", [2, D, 6 * D])
    ada_b = din("ada_b", [128, 2, 48])
    nmw_d = din("nmw", [128, 2, 8])
    nfw_d = din("nfw", [128, 2, 8])
    fnw_d = din("fnw", [128, 8])
    wqkv_d = din("wqkv", [D, 1536])
    qkn_d = din("qkn", [128, 2])
    wo_d = din("wo", [64, 16, D])
    hwin_d = din("hwin", [D, 5 * D])
    lbl_d = din("lbl", [128, 2, 8])
    onw_d = din("onw", [128, 8])
    hwo_d = din("hwo", [D, D])
    fwin_d = din("fwin", [2, D, 2 * FF])
    fwout_d = din("fwout", [2, FF, D])
    yout = Tile(nc.dram_tensor("y", [HALF, D], F32, kind="ExternalOutput").ap(), P.newtid())

    NTOK = HALF + CTX
    H0 = dscr("H0", [128, 8, NTOK])
    QS = dscr("QS", [128, 8, NTOK], BF16)
    H1 = dscr("H1", [128, 8, NTOK])
    H2 = dscr("H2", [128, 8, NTOK])
    OF = dscr("OF", [128, 8, HALF])
    H3 = dscr("H3", [128, 8, HALF])
    if os.environ.get("K_DBG") == "1":
        H4 = dscr("H4", [128, 8, HALF])
        H5 = dscr("H5", [128, 8, HALF])
    NOWN = HALF // 128
    fwin_b = dscr("fwin_b", [2, D, 2 * FF], BF16)
    fwout_r = dscr("fwout_r", [2, 8, 128, 22 * 128], BF16)
    hwin_b = dscr("hwin_b", [D, 5 * D], BF16)
    hwo_b = dscr("hwo_b", [D, D], BF16)
    KX_src = nc.dram_tensor("KXs", [256, HALF], BF16)
    KX_dst = nc.dram_tensor("KXd", [512, HALF], BF16)
    NVH = NOWN // 2
    VX_src = [nc.dram_tensor("VXs%d" % i, [128, NVH * 260], BF16) for i in range(2)]
    VX_dst = [nc.dram_tensor("VXd%d" % i, [256, NVH * 260], BF16) for i in range(2)]
    KXs, KXd = Tile(KX_src, P.newtid()), Tile(KX_dst, P.newtid())
    VXs = [Tile(t_, P.newtid()) for t_ in VX_src]
    VXd = [Tile(t_, P.newtid()) for t_ in VX_dst]
    SX_src = nc.dram_tensor("SXs", [128, 1024], F32)
    SX_dst = nc.dram_tensor("SXd", [256, 1024], F32)
    SXs = Tile(SX_src, P.newtid())
    SXd = Tile(SX_dst, P.newtid())
    parw_d = din("parw", [128, 2])

    A = Arena(nc, P, 207 * 1024)
    pp = [nc.alloc_psum_tensor("pp%d" % i, [128, 1024], F32) for i in range(4)]
    psum = [Tile(pp[i // 2][:, (i % 2) * 512:(i % 2 + 1) * 512], P.newtid()) for i in range(8)]

    def ppair(k, W):
        return V(pp[k][:, :].rearrange("p (a b) -> p a b", a=2)[:, :, 0:W], (psum[2 * k].tid, None), extra=((psum[2 * k + 1].tid, None),))
    for t_ in psum:
        P.excl.add(t_.tid)

    ident = A.alloc([128], F32)
    identb = A.alloc([128], BF16)
    ones_b = A.alloc([128], BF16)
    blk_b = A.alloc([128], BF16)
    ones_f = A.alloc([128], F32)
    pimat = A.alloc([128], F32)
    zero_f = A.alloc([128], F32)
    P.memset("pool", zero_f[:, :], 0.0)
    P.memset("pool", ones_f[:, :], 1.0)
    P.add("pool", lambda e: e.affine_select(out=ident[:, :].ap, in_=zero_f[:, :].ap, pattern=[[-1, 128]],
                                            compare_op=ALU.not_equal, fill=1.0, base=0, channel_multiplier=1),
          reads=[zero_f[:, :].key], writes=[ident[:, :].key])
    P.copy("pool", identb[:, :], ident[:, :])
    P.copy("pool", ones_b[:, :], ones_f[:, :])
    P.memset("pool", blk_b[:, :], 0.0)
    P.memset("pool", blk_b[0:64, 0:64], 1.0)
    P.memset("pool", blk_b[64:128, 64:128], 1.0)
    P.dma("sp", pimat[:, :], pimat_d[:, :])

    smallv = A.alloc([256], F32)
    sv_off = [0]

    def small(n):
        o = sv_off[0]
        sv_off[0] += n
        assert sv_off[0] <= 256
        return o

    o_cv = small(16)
    o_nmw = small(16)
    o_nfw = small(16)
    o_fnw = small(8)
    o_qkn = small(2)
    o_lbl = small(16)
    o_onw = small(8)
    o_lb = small(8)
    o_l1 = small(8)
    o_nl = small(8)
    o_csil = small(16)
    o_parw = small(2)
    sv = smallv

    def svv(o, n):
        return sv[:, o:o + n]

    P.dma("sp", svv(o_cv, 16), cvec[:, :, :].re("p a b -> p (a b)"))
    P.dma("sp", svv(o_nmw, 16), nmw_d[:, :, :].re("p a b -> p (a b)"))
    P.dma("sp", svv(o_nfw, 16), nfw_d[:, :, :].re("p a b -> p (a b)"))
    P.dma("sp", svv(o_fnw, 8), fnw_d[:, :])
    P.dma("sp", svv(o_qkn, 2), qkn_d[:, :])
    P.dma("sp", svv(o_lbl, 16), lbl_d[:, :, :].re("p a b -> p (a b)"))
    P.dma("sp", svv(o_onw, 8), onw_d[:, :])
    P.dma("sp", svv(o_parw, 2), parw_d[:, :])
    parw = sv.view(sv.h[:, o_parw:o_parw + 2])
    adab = A.alloc([96], F32)
    P.dma("sp", adab[:, :], ada_b[:, :, :].re("p a b -> p (a b)"))
    lbe = A.alloc([24], F32)
    P.act(lbe[:, 0:16], svv(o_lbl, 16), AF.Exp)
    P.tt("dve", lbe[:, 16:24], lbe[:, 0:8], lbe[:, 8:16], ALU.add)
    P.recip(lbe[:, 16:24], lbe[:, 16:24])
    P.tt("dve", svv(o_lb, 8), lbe[:, 8:16], lbe[:, 16:24], ALU.mult)
    P.ts("dve", svv(o_l1, 8), svv(o_lb, 8), -1.0, ALU.mult, 1.0, ALU.add)
    P.ts("dve", svv(o_nl, 8), svv(o_l1, 8), -1.0, ALU.mult)
    P.act(svv(o_csil, 16), svv(o_cv, 16), AF.Silu)

    modv = A.alloc([2, 48, 2], F32)
    modA = A.alloc([2, 2, 8, 2], F32)
    m0 = A.mark()
    wblk = [A.alloc([8, 512], F32) for _ in range(2)]
    modrow = A.alloc([6 * D], F32, parts=2)
    csil = svv(o_csil, 16)
    n_ada = 0
    for l in range(2):
        for blk in range(12):
            wb = wblk[n_ada % 2]
            n_ada += 1
            P.dma("sp", wb[:, :, :], ada_w[l, :, blk * 512:(blk + 1) * 512].re("(kc p) n -> p kc n", p=128))
            prow = psum[2 + blk % 2]
            for kc in range(8):
                P.mm(prow[0:2, :], V(sv.h[:, o_csil + kc * 2:o_csil + kc * 2 + 2], csil.key), wb[:, kc, :],
                     start=(kc == 0), stop=(kc == 7))
            P.copy("act" if blk % 2 else "dve", modrow[0:2, blk * 512:(blk + 1) * 512], prow[0:2, :])
        for jj in range(48):
            P.tr(psum[l][:, jj * 2:jj * 2 + 2], modrow[0:2, jj * 128:(jj + 1) * 128], ident[0:2, 0:2])
        P.tt("dve", modv[:, l, :, :], psum[l][:, 0:96].re("p (a b) -> p a b", b=2),
             adab[:, l * 48:(l + 1) * 48].re("p (a b) -> p a b", b=1).bc([128, 48, 2]), ALU.add)
        for nrm, (jsc, ow) in enumerate(((8, o_nmw), (32, o_nfw))):
            wv = sv[:, ow + l * 8:ow + l * 8 + 8].re("p (a b) -> p a b", b=1).bc([128, 8, 2])
            P.stt(modA[:, l, nrm, :, :], modv[:, l, jsc:jsc + 8, :], 1.0, wv, ALU.add, ALU.mult)
    A.release(m0)

    def mod(l, kind, which):
        return lambda kc: modv[:, l, kind * 8 + kc, which:which + 1]

    def modAv(l, nrm, which):
        return lambda kc: modA[:, l, nrm, kc, which:which + 1]

    rr = [0]

    def evac_eng():
        rr[0] += 1
        return "act" if rr[0] % 2 else "dve"

    def load_xT(src_rows, W, xin, xT, pbanks):
        nj = W // 128
        P.dma("sp", xin[:, 0:nj, :], src_rows.re("(j p) d -> p j d", p=128))
        for kc in range(8):
            pb = pbanks[kc % len(pbanks)]
            for j in range(nj):
                P.tr(pb[:, j * 128:(j + 1) * 128], xin[:, j, kc * 128:(kc + 1) * 128], ident[:, :])
            P.copy(evac_eng(), xT[:, kc, 0:W], pb[:, 0:W])

    def norm_mod(xT, W, Af, Bf, hn, sq, tmpf, pbank, nfeat=1024.0):
        for kc in range(8):
            P.act(sq[:, kc, 0:W], xT[:, kc, 0:W], AF.Square)
        for kc in range(8):
            P.mm(pbank[:, 0:W], ones_b[:, :], sq[:, kc, 0:W], start=(kc == 0), stop=(kc == 7))
        P.act(tmpf[0][:, 0:W], pbank[:, 0:W], AF.Ln, scale=1.0 / nfeat, bias=epsv[:, 0:1])
        P.act(tmpf[0][:, 0:W], tmpf[0][:, 0:W], AF.Exp, scale=-0.5)
        for kc in range(8):
            t = tmpf[1 + kc % 2]
            P.tt("dve", t[:, 0:W], xT[:, kc, 0:W], tmpf[0][:, 0:W], ALU.mult)
            P.ts("pool", hn[:, kc, 0:W], t[:, 0:W], Af(kc), ALU.mult, (0.0 if Bf is None else Bf(kc)), ALU.add)

    epsv = A.alloc([1], F32)
    P.memset("pool", epsv[:, :], EPS)

    def layer0():
        mL0 = A.mark()
        NKC = (2 * HALF + CTX) // 128
        KT = A.alloc([2, NKC * 128], BF16)
        VA = A.alloc([NKC, 4, 65], BF16)
        P.memset("pool", VA[:, :, :, 64:65], 1.0)
        mA = A.mark()
        wqkv = A.alloc([8, 1536], BF16)
        for kc in range(8):
            P.dma("pool", wqkv[:, kc, :], wqkv_d[kc * 128:(kc + 1) * 128, :])
        xin = [A.alloc([4, 1024], F32) for _ in range(2)]
        xT2 = [A.alloc([8, 512], F32) for _ in range(2)]
        hn = A.alloc([8, 512], BF16)
        rtab = [A.alloc([2, 512], F32) for _ in range(2)]
        qT = A.alloc([8, 512], BF16)
        sq = qT
        sqh2 = [A.alloc([512], BF16) for _ in range(2)]
        kf2 = [A.alloc([512], F32) for _ in range(2)]
        rs2 = [A.alloc([512], F32) for _ in range(2)]
        t12 = [A.alloc([512], F32) for _ in range(2)]
        t22 = [A.alloc([512], F32) for _ in range(2)]
        tmpf = [A.alloc([512], F32), t12[0], t22[0]]
        qkc = [0]

        def qknorm_rope(ps, W, wv, rt, outv):
            SUB = 9
            par = qkc[0] % 2
            qkc[0] += 1
            sqh, kf, rs, t1, t2 = sqh2[par], kf2[par], rs2[par], t12[par], t22[par]
            pssq = psum[5] if par == 0 else psum[0]
            pkp = psum[6] if par == 0 else psum[1]
            P.act(sqh[:, 0:W], ps[:, 0:W], AF.Square)
            if SUB < 2:
                return
            P.ts("dve", kf[:, 0:W], ps[:, 0:W], wv, ALU.mult)
            P.mm(pssq[:, 0:W], blk_b[:, :], sqh[:, 0:W])
            P.act(rs[:, 0:W], pssq[:, 0:W], AF.Ln, scale=1.0 / 64.0, bias=epsv[:, 0:1])
            P.act(rs[:, 0:W], rs[:, 0:W], AF.Exp, scale=-0.5)
            if SUB < 3:
                return
            if rt is not None:
                P.mm(pkp[:, 0:W], pimat[:, :], kf[:, 0:W])
                P.tt("dve", t1[:, 0:W], kf[:, 0:W], rt[:, 0, 0:W], ALU.mult)
                P.tt("dve", t2[:, 0:W], pkp[:, 0:W], rt[:, 1, 0:W], ALU.mult)
                P.tt("pool", t1[:, 0:W], t1[:, 0:W], t2[:, 0:W], ALU.add)
                P.tt("dve", outv, t1[:, 0:W], rs[:, 0:W], ALU.mult)
            else:
                P.tt("dve", outv, kf[:, 0:W], rs[:, 0:W], ALU.mult)

        KTx, KTc, VAx, VAc = KT.s("x"), KT.s("c"), VA.s("x"), VA.s("c")
        def exchange_kv():
            for c in range(2):
                P.dma("sp", KXs[c * 128:(c + 1) * 128, :], KTx[:, c, 0:HALF])
            for i in range(2):
                P.dma("sp", VXs[i][:, :], VAx[:, i * NVH:(i + 1) * NVH, :, :].re("p j h d -> p (j h d)"))
            grp = [[2 * i, 2 * i + 1] for i in range(NB)]
            P.add("pool", lambda e: e.collective_compute("AllGather", ALU.bypass, ins=[KX_src.ap().opt()], outs=[KX_dst.ap().opt()],
                                                         replica_groups=grp),
                  reads=[KXs[:, :].key], writes=[KXd[:, :].key], cc=True)
            for i in range(2):
                P.add("pool", lambda e, i=i: e.collective_compute("AllGather", ALU.bypass, ins=[VX_src[i].ap().opt()], outs=[VX_dst[i].ap().opt()],
                                                                  replica_groups=grp),
                      reads=[VXs[i][:, :].key], writes=[VXd[i][:, :].key], cc=True)
            for r in range(2):
                for c in range(2):
                    P.dma("sp", KTx[:, c, r * HALF:(r + 1) * HALF], KXd[(2 * r + c) * 128:(2 * r + c + 1) * 128, :])
                for i in range(2):
                    P.dma("sp", VAx[:, r * NOWN + i * NVH:r * NOWN + (i + 1) * NVH, :, :].re("p j h d -> p (j h d)"), VXd[i][r * 128:(r + 1) * 128, :])

        tiles = [("own", i) for i in range(HALF // 512)] + [("ctx", 0)]
        for tn, (kind, ti) in enumerate(tiles):
            W = 512 if kind != "ctx" else CTX
            which = 1 if kind == "ctx" else 0
            if kind == "own":
                src = xo[ti * 512:(ti + 1) * 512, :]
                kbase = ti * 512
                hcol = ti * 512
            elif kind == "par":
                src = xp[ti * 512:(ti + 1) * 512, :]
                kbase = HALF + ti * 512
                hcol = None
            else:
                src = cx[:, :]
                kbase = 2 * HALF
                hcol = HALF
            xi = xin[tn % 2]
            xT = xT2[tn % 2]
            load_xT(src, W, xi, xT, [psum[0], psum[1]])
            rt = None
            if kind != "ctx":
                rt = rtab[tn % 2]
                rsrc = rope_o
                P.dma("sp", rt[:, :, :], rsrc[:, :, ti * 512:(ti + 1) * 512].re("a p n -> p a n"))
            if hcol is not None:
                P.dma("sp", H0[:, :, hcol:hcol + W], xT[:, :, 0:W])
            LVL = int(os.environ.get("K_LVL", "9"))
            if LVL < 2:
                continue
            norm_mod(xT, W, modAv(0, 0, which), mod(0, 0, which), hn, sq, tmpf, psum[2])
            if LVL < 3:
                continue
            for c in range(2):
                pb = psum[3 + c % 2]
                for kc in range(8):
                    P.mm(pb[:, 0:W], wqkv[:, kc, 1024 + c * 128:1024 + (c + 1) * 128], hn[:, kc, 0:W],
                         start=(kc == 0), stop=(kc == 7))
                qknorm_rope(pb, W, svv(o_qkn + 1, 1), rt, (KTc if kind == "ctx" else KTx)[:, c, kbase:kbase + W])
            for j in range(W // 128 if LVL >= 4 else 0):
                pb = psum[7]
                for kc in range(8):
                    P.mm(pb[:, 0:256], hn[:, kc, j * 128:(j + 1) * 128], wqkv[:, kc, 1280:1536],
                         start=(kc == 0), stop=(kc == 7))
                P.copy(evac_eng(), (VAc if kind == "ctx" else VAx)[:, kbase // 128 + j, :, 0:64], pb[:, 0:256].re("p (h d) -> p h d", d=64))
            if kind == "own" and ti == HALF // 512 - 1:
                exchange_kv()
            if hcol is not None and LVL >= 5:
                for c in range(8):
                    pb = psum[3 + c % 2]
                    for kc in range(8):
                        P.mm(pb[:, 0:W], wqkv[:, kc, c * 128:(c + 1) * 128], hn[:, kc, 0:W],
                             start=(kc == 0), stop=(kc == 7))
                    qknorm_rope(pb, W, svv(o_qkn, 1), rt, qT[:, c, 0:W])
                P.dma("sp", QS[:, :, hcol:hcol + W], qT[:, :, 0:W])
        A.release(mA)

        if "B" not in PHASES:
            A.release(mL0)
            return
        wo = A.alloc([16, 1024], BF16, parts=64)
        P.dma("pool", wo[:, :, :], wo_d[:, :, :])
        for l in range(2):
            for kc in range(8):
                P.dma("pool", fwin_b[l, kc * 128:(kc + 1) * 128, :], fwin_d[l, kc * 128:(kc + 1) * 128, :])
            for oc in range(8):
                P.add("pool", lambda e, l=l, oc=oc: e.dma_start(
                          out=fwout_r.h[l, oc, :, :].rearrange("p (a b) -> p a b", b=128),
                          in_=fwout_d.h[l, :, oc * 128:(oc + 1) * 128].rearrange("(a p) b -> p a b", p=128)),
                      reads=[fwout_d[l, :, :].key], writes=[(fwout_r.tid, (l, oc))], dma=True)
            if l == 0:
                for kc in range(8):
                    P.dma("pool", hwin_b[kc * 128:(kc + 1) * 128, :], hwin_d[kc * 128:(kc + 1) * 128, :])
                    P.dma("pool", hwo_b[kc * 128:(kc + 1) * 128, :], hwo_d[kc * 128:(kc + 1) * 128, :])
        qTb = [A.alloc([8, 512], BF16) for _ in range(2)]
        xTb = [A.alloc([8, 512], F32) for _ in range(2)]
        PT = [A.alloc([2, 512], BF16) for _ in range(3)]
        osb = A.alloc([2, 512], F32)
        rinv = A.alloc([2, 512], F32)
        rb = A.alloc([2, 512], F32)
        P.memset("pool", rinv[:, :, :], 1.0)
        oT = A.alloc([16, 512], BF16)
        hmid = A.alloc([8, 512], F32)
        SCALE = 64 ** -0.5
        qtiles = [("own", i) for i in range(HALF // 512)] + [("ctx", 0)]
        for tn, (kind, ti) in enumerate(qtiles):
            W = 512 if kind == "own" else CTX
            which = 0 if kind == "own" else 1
            hcol = ti * 512 if kind == "own" else HALF
            kcs = list(range(NKC)) if kind == "own" else [NKC - 2, NKC - 1]
            qb = qTb[tn % 2]
            xb = xTb[tn % 2]
            P.dma("sp", qb[:, :, 0:W], QS[:, :, hcol:hcol + W])
            P.dma("sp", xb[:, :, 0:W], H0[:, :, hcol:hcol + W])
            step = 0
            for c in range(8):
                kvc = c // 4
                Sb = [(psum[0], psum[1]), (psum[2], psum[3]), (psum[4], psum[5])]
                Ob = (psum[6], psum[7])

                def QK(i, st):
                    kc = kcs[i]
                    sa, sbb = Sb[st % 3]
                    P.mm(sa[:, 0:W], KT[0:64, kvc, kc * 128:(kc + 1) * 128], qb[0:64, c, 0:W])
                    P.mm(sbb[:, 0:W], KT[64:128, kvc, kc * 128:(kc + 1) * 128], qb[64:128, c, 0:W])

                def EXP(i, st):
                    pt = PT[st % 3]
                    P.act(pt[:, :, 0:W], ppair(st % 3, W), AF.Exp, scale=SCALE)

                def PV(i, st):
                    kc = kcs[i]
                    pt = PT[st % 3]
                    for ab in range(2):
                        P.mm(Ob[ab][0:65, 0:W], VA[:, kc, 2 * kvc + ab, 0:65], pt[:, ab, 0:W],
                             start=(i == 0), stop=(i == len(kcs) - 1))

                n = len(kcs)
                for j in range(min(2, n)):
                    QK(j, step + j)
                for i in range(n):
                    if i + 2 < n:
                        QK(i + 2, step + i + 2)
                    EXP(i, step + i)
                    PV(i, step + i)
                step += n
                for ab in range(2):
                    P.copy("dve", osb[0:65, ab, 0:W], Ob[ab][0:65, 0:W])
                P.recip(rinv[64:65, :, 0:W], osb[64:65, :, 0:W])
                P.shuf(rb[0:32, :, 0:W], rinv[64:96, :, 0:W], [0] * 32)
                P.shuf(rb[32:64, :, 0:W], rinv[64:96, :, 0:W], [0] * 32)
                for ab in range(2):
                    P.tt("dve", oT[0:64, 2 * c + ab, 0:W], osb[0:64, ab, 0:W], rb[0:64, ab, 0:W], ALU.mult)
            for oc in range(8):
                pb = psum[oc % 6]
                for j in range(16):
                    P.mm(pb[:, 0:W], wo[0:64, j, oc * 128:(oc + 1) * 128], oT[0:64, j, 0:W], start=(j == 0), stop=(j == 15))
                P.stt(hmid[:, oc, 0:W], pb[:, 0:W], mod(0, 2, which)(oc), xb[:, oc, 0:W], ALU.mult, ALU.add)
            P.dma("sp", H1[:, :, hcol:hcol + W], hmid[:, :, 0:W])
        A.release(mL0)


    if "A" in PHASES:
        layer0()

    def ffn_phase(l, src, dst, tilespecs, final=False):
        m = A.mark()
        win = A.alloc([8, 2 * FF], BF16)
        for kc in range(8):
            P.dma("sp", win[:, kc, :], fwin_b[l, kc * 128:(kc + 1) * 128, :])
        wob = [A.alloc([22, 128], BF16) for _ in range(3)]
        h2 = [A.alloc([8, 512], F32) for _ in range(2)]
        hn2 = [A.alloc([8, 512], BF16) for _ in range(2)]
        sq = A.alloc([8, 512], BF16)
        tmpf = [A.alloc([512], F32) for _ in range(3)]
        sa = [A.alloc([512], F32) for _ in range(2)]
        sT = A.alloc([22, 512], BF16)
        if final:
            fo = sT.view(sT.h[:, 0:16, :].rearrange("p a b -> p (a b)").bitcast(F32).rearrange("p (a b) -> p a b", a=8))
        nt = len(tilespecs)
        nwo = [0]

        def load(t):
            col, W, which = tilespecs[t]
            P.dma("sp", h2[t % 2][:, :, 0:W], src[:, :, col:col + W])

        def norm(t):
            col, W, which = tilespecs[t]
            norm_mod(h2[t % 2], W, modAv(l, 1, which), mod(l, 3, which), hn2[t % 2], sq, tmpf, psum[0])

        def stage_in(t):
            col, W, which = tilespecs[t]
            hn = hn2[t % 2]
            for hc in range(22):
                pa = psum[1 + 2 * (hc % 2)]
                pu = psum[2 + 2 * (hc % 2)]
                for kc in range(8):
                    P.mm(pa[:, 0:W], win[:, kc, hc * 128:(hc + 1) * 128], hn[:, kc, 0:W], start=(kc == 0), stop=(kc == 7))
                for kc in range(8):
                    P.mm(pu[:, 0:W], win[:, kc, FF + hc * 128:FF + (hc + 1) * 128], hn[:, kc, 0:W], start=(kc == 0), stop=(kc == 7))
                s_ = sa[hc % 2]
                P.act(s_[:, 0:W], pa[:, 0:W], AF.Silu)
                P.tt("dve", sT[:, hc, 0:W], s_[:, 0:W], pu[:, 0:W], ALU.mult)

        def stage_out(t):
            col, W, which = tilespecs[t]
            h = h2[t % 2]
            for oc in range(8):
                wb = wob[nwo[0] % 3]
                nwo[0] += 1
                P.dma("sp", wb[:, :, :].re("p a b -> p (a b)"), fwout_r.s((l, oc))[l, oc, :, :])
                pb = psum[5 + oc % 2]
                for hc in range(22):
                    P.mm(pb[:, 0:W], wb[:, hc, :], sT[:, hc, 0:W], start=(hc == 0), stop=(hc == 21))
                P.stt(h[:, oc, 0:W], pb[:, 0:W], mod(l, 5, which)(oc), h[:, oc, 0:W], ALU.mult, ALU.add)
            if not final:
                P.dma("sp", dst[:, :, col:col + W], h[:, :, 0:W])
            else:
                norm_mod(h, W, lambda kc: svv(o_fnw + kc, 1), lambda kc: zero_f[:, 0:1], fo, sq, tmpf, psum[0])
                for j in range(W // 128):
                    for half in range(2):
                        pb = psum[1 + (2 * j + half) % 4]
                        for q in range(4):
                            kc = half * 4 + q
                            P.tr(pb[:, q * 128:(q + 1) * 128], fo[:, kc, j * 128:(j + 1) * 128], ident[:, :])
                        yb = sa[(2 * j + half) % 2]
                        P.copy(evac_eng(), yb[:, :], pb[:, :])
                        P.dma("sp", dst[col + j * 128:col + (j + 1) * 128, half * 512:(half + 1) * 512], yb[:, :])

        load(0)
        if nt > 1:
            load(1)
        norm(0)
        for t in range(nt):
            stage_in(t)
            if t + 1 < nt and not final:
                norm(t + 1)
            stage_out(t)
            if t + 1 < nt and final:
                norm(t + 1)
            if t + 2 < nt:
                load(t + 2)
        A.release(m)

    lat_tiles = [(i * 512, 512, 0) for i in range(HALF // 512)]
    if "F0" in PHASES:
        ffn_phase(0, H1, H2, lat_tiles + [(HALF, CTX, 1)])

    def hgrn_layer():
        mH = A.mark()
        hw = A.alloc([8, 4 * D], BF16)
        hwo = A.alloc([8, D], BF16)
        S32 = A.alloc([8, 128], F32)
        Sbf2 = [A.alloc([8, 128], BF16) for _ in range(2)]
        sc = [0]
        maskF = A.alloc([64], F32, parts=64)
        maskB = A.alloc([64], F32, parts=64)
        rmask = A.alloc([512], BF16)
        P.memset("pool", maskF[:, :], 1.0)
        P.memset("pool", maskB[:, :], 1.0)
        P.add("pool", lambda e: e.affine_select(out=maskF[:, :].ap, in_=maskF[:, :].ap, pattern=[[1, 64]],
                                                compare_op=ALU.is_ge, fill=0.0, base=0, channel_multiplier=-1),
              reads=[maskF[:, :].key], writes=[maskF[:, :].key])
        P.add("pool", lambda e: e.affine_select(out=maskB[:, :].ap, in_=maskB[:, :].ap, pattern=[[-1, 64]],
                                                compare_op=ALU.is_ge, fill=0.0, base=0, channel_multiplier=1),
              reads=[maskB[:, :].key], writes=[maskB[:, :].key])
        P.memset("pool", rmask[:, :], 1.0)
        P.memset("pool", V(rmask.h[:, :].rearrange("p (c t) -> p c t", t=64)[:, :, 0:1], rmask[:, :].key), 0.0)
        h = A.alloc([8, 512], F32)
        sq = A.alloc([8, 512], BF16)
        hn = A.alloc([8, 512], BF16)
        sg2 = [A.alloc([512], F32) for _ in range(2)]
        gT2 = [A.alloc([512], F32) for _ in range(2)]
        kT2 = [A.alloc([512], F32) for _ in range(2)]
        LT2 = [A.alloc([512], F32) for _ in range(2)]
        eL2 = [A.alloc([512], F32) for _ in range(2)]
        e22 = [A.alloc([512], F32) for _ in range(2)]
        tmpf = [gT2[0], gT2[1], kT2[0]]
        gs = sg2[0]
        eTotA = A.alloc([8, 8], F32)
        qtA = A.alloc([8, 512], BF16)
        kvA = A.alloc([16, 512], BF16)
        khA = kvA.view(kvA.h[:, 0:8, :], "kh")
        vTA = kvA.view(kvA.h[:, 8:16, :], "vT")
        Rst = kvA.view(kvA.h[:, :, :].rearrange("p a b -> p (a b)").bitcast(F32).rearrange("p (a b) -> p a b", a=8))
        ktok2 = [A.alloc([8, 128], BF16, parts=64) for _ in range(2)]
        vtok2 = [A.alloc([8, 128], BF16, parts=64) for _ in range(2)]
        Am2 = [A.alloc([8, 64], BF16, parts=64) for _ in range(2)]
        oTt = A.alloc([8, 512], F32)
        hww = hw.s("w")
        hwow = hwo.s("w")

        class HB:
            def __init__(self, fn):
                self.fn = fn

            def __getitem__(self, idx):
                _, hd, sl = idx
                return self.fn(hd, sl)

        def carve(t, a0, sub):
            key = (t.tid, sub)
            return HB(lambda hd, sl: V(t.h[:, a0 + hd // 2, (hd % 2) * 512 + sl.start:(hd % 2) * 512 + sl.stop], key))

        qtB = carve(hw, 0, "qtB") if False else HB(lambda hd, sl: V(hw.h[:, hd // 2, 3 * D + (hd % 2) * 512 + sl.start:3 * D + (hd % 2) * 512 + sl.stop], (hw.tid, "qtB")))
        khB = HB(lambda hd, sl: V(hw.h[:, 4 + hd // 2, 3 * D + (hd % 2) * 512 + sl.start:3 * D + (hd % 2) * 512 + sl.stop], (hw.tid, "khB")))
        vTB = HB(lambda hd, sl: V(hwo.h[:, hd // 2, (hd % 2) * 512 + sl.start:(hd % 2) * 512 + sl.stop], (hwo.tid, "vTB")))
        eTotB = hwo.view(hwo.h[:, 4, 0:128].bitcast(F32).rearrange("p (a b) -> p a b", a=8), "eB")
        bufs = [(qtA, khA, vTA, eTotA), (qtB, khB, vTB, eTotB)]
        pbf = [psum[i].view(psum[i].h[:, 0:512].bitcast(BF16)) for i in range(8)]

        def load_hw(blocks):
            for bi, sb in enumerate(blocks):
                for kc in range(8):
                    P.dma("sp", (hww if bi < 3 else hw)[:, kc, bi * D:(bi + 1) * D], hwin_b[kc * 128:(kc + 1) * 128, sb * D:(sb + 1) * D])

        class TileJob:
            def __init__(self, src_col, W, which, rev, emit_out, xsrc, bset, readout=False, of_col=None):
                self.src_col, self.W, self.which, self.rev, self.emit_out = src_col, W, which, rev, emit_out
                self.xsrc, self.readout, self.of_col = xsrc, readout, of_col
                self.qt, self.kh, self.vT, self.eTot = bufs[bset]
                self.nch = W // 64
                self.order = list(range(self.nch))[::-1] if rev else list(range(self.nch))
                self.mask = maskB if rev else maskF
                self.alone = False

            def load_h(self):
                W = self.W
                P.dma("sp", h[:, :, 0:W], self.xsrc[:, :, self.src_col:self.src_col + W])

            def prologue(self, load=True):
                W = self.W
                if load:
                    self.load_h()
                norm_mod(h, W, modAv(1, 0, self.which), mod(1, 0, self.which), hn, sq, tmpf, psum[7])

            def head(self, hd, standalone=False):
                W, nch, rev = self.W, self.nch, self.rev
                qt, kh, vT, eTot = self.qt, self.kh, self.vT, self.eTot
                par = hd % 2
                sg, gT, kT, LT, eL, e2 = sg2[par], gT2[par], kT2[par], LT2[par], eL2[par], e22[par]
                pz = psum[2] if (standalone and par) else psum[6]
                for kc in range(8):
                    P.mm(pz[:, 0:W], hww[:, kc, D + hd * 128:D + (hd + 1) * 128], hn[:, kc, 0:W], start=(kc == 0), stop=(kc == 7))
                P.act(sg[:, 0:W], pz[:, 0:W], AF.Sigmoid)
                P.act(gT[:, 0:W], sg[:, 0:W], AF.Ln, scale=svv(o_l1 + hd, 1), bias=svv(o_lb + hd, 1))
                P.ts("dve", kT[:, 0:W], sg[:, 0:W], svv(o_nl + hd, 1), ALU.mult, svv(o_l1 + hd, 1), ALU.add)
                P.scan(LT[:, 0:W], rmask[:, 0:W], gT[:, 0:W], 0.0, ALU.mult, ALU.add)
                LTc = V(LT.h[:, 0:W].rearrange("p (c t) -> p c t", t=64), LT[:, :].key)
                P.act(eTot[:, hd, 0:nch], V(LTc.ap[:, :, 63], LTc.key), AF.Exp)
                Lsrc = LT
                if rev:
                    P.tt("dve", gT[:, 0:W], gT[:, 0:W], LT[:, 0:W], ALU.subtract)
                    tot = V(LTc.ap[:, :, 63:64].to_broadcast([128, nch, 64]), LTc.key)
                    P.tt("dve", sg[:, 0:W].re("p (c t) -> p c t", t=64), gT[:, 0:W].re("p (c t) -> p c t", t=64), tot, ALU.add)
                    Lsrc = sg
                P.act(eL[:, 0:W], Lsrc[:, 0:W], AF.Exp)
                P.act(e2[:, 0:W], Lsrc[:, 0:W], AF.Exp, scale=-1.0)
                P.tt("dve", kh[:, hd, slice(0, W)], kT[:, 0:W], e2[:, 0:W], ALU.mult)
                pq = psum[3] if (standalone and par) else psum[7]
                for kc in range(8):
                    P.mm(pq[:, 0:W], hww[:, kc, hd * 128:(hd + 1) * 128], hn[:, kc, 0:W], start=(kc == 0), stop=(kc == 7))
                P.tt("dve", qt[:, hd, slice(0, W)], pq[:, 0:W], eL[:, 0:W], ALU.mult)
                pv = psum[2] if (standalone and par) else psum[6]
                for kc in range(8):
                    P.mm(pv[:, 0:W], hww[:, kc, 2 * D + hd * 128:2 * D + (hd + 1) * 128], hn[:, kc, 0:W], start=(kc == 0), stop=(kc == 7))
                P.copy("act", vT[:, hd, slice(0, W)], pv[:, 0:W])

            def pre(self, k):
                ci = self.order[k]
                cs = slice(ci * 64, (ci + 1) * 64)
                qt, kh, vT = self.qt, self.kh, self.vT
                ktok, vtok, Am = ktok2[k % 2], vtok2[k % 2], Am2[k % 2]
                if self.emit_out:
                    for hd in range(8):
                        P.mm(psum[4][0:64, hd * 64:(hd + 1) * 64], kh[:, hd, cs], qt[:, hd, cs])
                for hd in range(8):
                    P.tr(pbf[2][0:64, hd * 128:(hd + 1) * 128], kh[:, hd, cs], identb[:, :])
                for hd in range(8):
                    P.tr(pbf[3][0:64, hd * 128:(hd + 1) * 128], vT[:, hd, cs], identb[:, :])
                P.copy("act", ktok[0:64, :, :], pbf[2][0:64, :].re("p (h d) -> p h d", d=128))
                P.copy("dve", vtok[0:64, :, :], pbf[3][0:64, :].re("p (h d) -> p h d", d=128))
                if self.emit_out:
                    P.tt("dve", Am[0:64, :, :], psum[4][0:64, 0:512].re("p (h t) -> p h t", t=64),
                         V(self.mask.h[0:64, :].unsqueeze(1).to_broadcast([64, 8, 64]), self.mask[:, :].key), ALU.mult)

            def preU(self, k):
                ktok, vtok = ktok2[k % 2], vtok2[k % 2]
                up = 3 * (k % 2) if self.alone else 0
                for hd in range(8):
                    pb = psum[2 * up + hd // 4]
                    P.mm(pb[:, (hd % 4) * 128:(hd % 4 + 1) * 128], ktok[0:64, hd, :], vtok[0:64, hd, :])

            def post(self, k):
                ci = self.order[k]
                cs = slice(ci * 64, (ci + 1) * 64)
                qt = self.qt
                ktok, vtok, Am = ktok2[k % 2], vtok2[k % 2], Am2[k % 2]
                Scur = Sbf2[sc[0] % 2]
                Snew = Sbf2[(sc[0] + 1) % 2]
                sc[0] += 1
                up = 3 * (k % 2) if self.alone else 0
                U = V(pp[up][:, :].rearrange("p (h d) -> p h d", d=128), (psum[2 * up].tid, None), extra=((psum[2 * up + 1].tid, None),))
                P.add("dve", lambda e, U=U: e.tensor_tensor(out=S32[:, :, :].ap, in0=U.ap, in1=S32[:, :, :].ap, op=ALU.add),
                      reads=[U.key, U.extra[0], S32[:, :, :].key], writes=[S32[:, :, :].key])
                et = V(self.eTot.h[:, :, ci:ci + 1].to_broadcast([128, 8, 128]), self.eTot[:, :, :].key)
                P.tt("dve", S32[:, :, :], S32[:, :, :], et, ALU.mult)
                P.copy("act", Snew[:, :, :], S32[:, :, :])
                if self.emit_out:
                    for hd in range(8):
                        P.mm(psum[5][:, hd * 64:(hd + 1) * 64], Scur[:, hd, :], qt[:, hd, cs], start=True, stop=False)
                        P.mm(psum[5][:, hd * 64:(hd + 1) * 64], vtok[0:64, hd, :], Am[0:64, hd, :], start=False, stop=True)
                    po = psum[5][:, 0:512].re("p (h t) -> p h t", t=64)
                    if self.of_col is None:
                        P.copy("act", oTt[:, :, cs], po)
                    else:
                        P.tt("dve", oTt[:, :, cs], po, oTt[:, :, cs], ALU.add)

            def epilogue(self):
                if not self.readout:
                    return
                qt = self.qt
                og = qt
                P.dma("sp", Rst[:, :, :], self.xsrc[:, :, self.src_col:self.src_col + 512])
                def epA(hd):
                    par = hd % 2
                    pss = psum[1] if par else psum[5]
                    pg = psum[7] if par else psum[6]
                    P.act(sq[:, hd, :], oTt[:, hd, :], AF.Square)
                    P.mm(pss[:, :], ones_b[:, :], sq[:, hd, :])
                    for kc in range(8):
                        P.mm(pg[:, :], hw[:, kc, 3 * D + hd * 128:3 * D + (hd + 1) * 128], hn[:, kc, :], start=(kc == 0), stop=(kc == 7))
                    P.act(sg2[par][:, :], pg[:, :], AF.Sigmoid)

                def epB(hd):
                    par = hd % 2
                    r0, r1, gsp = gT2[par], kT2[par], sg2[par]
                    pss = psum[1] if par else psum[5]
                    P.act(r0[:, :], pss[:, :], AF.Ln, scale=1.0 / 128.0, bias=epsv[:, 0:1])
                    P.act(r0[:, :], r0[:, :], AF.Exp, scale=-0.5)
                    P.stt(r1[:, :], oTt[:, hd, :], svv(o_onw + hd, 1), r0[:, :], ALU.mult, ALU.mult)
                    P.tt("dve", og[:, hd, slice(0, 512)], r1[:, :], gsp[:, :], ALU.mult)

                epA(0)
                for hd in range(8):
                    if hd + 1 < 8:
                        epA(hd + 1)
                    epB(hd)
                for oc in range(8):
                    pb = psum[2 + oc % 2]
                    for kc in range(8):
                        P.mm(pb[:, :], hwow[:, kc, oc * 128:(oc + 1) * 128], og[:, kc, slice(0, 512)], start=(kc == 0), stop=(kc == 7))
                    P.stt(Rst[:, oc, :], pb[:, :], mod(1, 2, 0)(oc), Rst[:, oc, :], ALU.mult, ALU.add)
                P.dma("sp", H3[:, :, self.src_col:self.src_col + 512], Rst[:, :, :])

        def run_chunks(job, nxt):
            job.alone = nxt is None
            n = job.nch
            job.pre(0)
            job.preU(0)
            for k in range(n):
                if k + 1 < n:
                    job.pre(k + 1)
                job.post(k)
                if nxt is not None and k < 8:
                    if k == 0:
                        nxt.prologue()
                    nxt.head(k)
                if k + 1 < n:
                    job.preU(k + 1)
            if nxt is not None:
                for hd in range(n, 8):
                    if n == 0:
                        nxt.prologue()
                    nxt.head(hd)

        load_hw([0, 2, 4])
        P.memset("dve", S32[:, :, :], 0.0)
        P.memset("dve", Sbf2[0][:, :, :], 0.0)
        jobs = [TileJob(HALF, CTX, 1, False, False, H2, 0)]
        for ti in range(HALF // 512):
            jobs.append(TileJob(ti * 512, 512, 0, False, True, H2, (ti + 1) % 2))
        jobs[0].prologue()
        for hd in range(8):
            jobs[0].head(hd, standalone=True)
        for ji, job in enumerate(jobs):
            nxt = jobs[ji + 1] if ji + 1 < len(jobs) else None
            run_chunks(job, nxt)
            if job.emit_out:
                P.dma("sp", OF[:, :, job.src_col:job.src_col + 512], oTt[:, :, :])
        P.dma("sp", SXs[:, :], S32[:, :, :].re("p h d -> p (h d)"))
        P.add("pool", lambda e: e.collective_compute("AllGather", ALU.bypass, ins=[SX_src.ap().opt()], outs=[SX_dst.ap().opt()],
                                                     replica_groups=[[2 * i, 2 * i + 1] for i in range(NB)]),
              reads=[SXs[:, :].key], writes=[SXd[:, :].key], cc=True)
        load_hw([0, 3, 4, 1])
        for kc in range(8):
            P.dma("sp", hwo[:, kc, :], hwo_b[kc * 128:(kc + 1) * 128, :])
        g0 = h[:, 0:2, :].re("p a b -> p (a b)")
        g1 = oTt[:, 0:2, :].re("p a b -> p (a b)")
        P.dma("sp", g0, SXd[0:128, :])
        P.dma("sp", g1, SXd[128:256, :])
        P.ts("dve", g0, g0, parw[:, 0:1], ALU.mult)
        P.stt(S32[:, :, :].re("p h d -> p (h d)"), g1, parw[:, 1:2], g0, ALU.mult, ALU.add)
        P.copy("act", Sbf2[sc[0] % 2][:, :, :], S32[:, :, :])
        tis = list(reversed(range(HALF // 512)))
        jobs2 = [TileJob(ti * 512, 512, 0, True, True, H2, 0, readout=True, of_col=ti * 512) for ti in tis]
        jobs2[0].load_h()
        for ji, job in enumerate(jobs2):
            job.prologue(load=False)
            P.dma("sp", oTt[:, :, :], OF[:, :, job.src_col:job.src_col + 512])
            if ji + 1 < len(jobs2):
                jobs2[ji + 1].load_h()
            for hd in range(8):
                job.head(hd, standalone=True)
            run_chunks(job, None)
            job.epilogue()
        A.release(mH)

    if "H" in PHASES:
        hgrn_layer()

    if "F1" in PHASES:
        ffn_phase(1, H3, yout, lat_tiles, final=True)

    if dbg_fn is not None:
        dbg_fn(locals())

    P.barrier()
    P.emit()
    return nc, P


_CACHE = {}


def _rope_tables(pos):
    inv = (np.float32(10000.0) ** (-(np.arange(16, dtype=np.float32) * np.float32(2.0) / np.float32(32.0)))).astype(np.float32)
    row = (pos // 64).astype(np.float32)
    col = (pos % 64).astype(np.float32)
    ar = row[:, None] * inv[None, :]
    ac = col[:, None] * inv[None, :]
    cr, sr, cc, sc = np.cos(ar), np.sin(ar), np.cos(ac), np.sin(ac)
    C = np.concatenate([cr, cr, cc, cc], axis=1).T.astype(np.float32)
    S = np.concatenate([-sr, sr, -sc, sc], axis=1).T.astype(np.float32)
    C = np.concatenate([C, C], axis=0)
    S = np.concatenate([S, S], axis=0)
    return np.ascontiguousarray(np.stack([C, S], axis=0))


def _fm(v):
    return np.ascontiguousarray(np.asarray(v).reshape(8, 128).T)


def kernel(x, c, ctx, c_ctx, ada_w, ada_b, norm_mix_w, norm_ffn_w, attn_w_qkv, attn_q_norm,
           attn_k_norm, attn_w_o, hgrn_w_in, hgrn_lb_logits, hgrn_out_norm, hgrn_w_o,
           ffn_w_in, ffn_w_out, final_norm_w):
    in_maps, gather = make_inputs(x, c, ctx, c_ctx, ada_w, ada_b, norm_mix_w, norm_ffn_w, attn_w_qkv, attn_q_norm,
                                  attn_k_norm, attn_w_o, hgrn_w_in, hgrn_lb_logits, hgrn_out_norm, hgrn_w_o,
                                  ffn_w_in, ffn_w_out, final_norm_w)
    if "nc" not in _CACHE:
        _CACHE["nc"] = build_program()[0]
    nc = _CACHE["nc"]
    res = run_bass_kernel_spmd(nc, in_maps, core_ids=list(range(2 * NB)))
    return gather([r["y"] for r in res.results])


def make_inputs(x, c, ctx, c_ctx, ada_w, ada_b, norm_mix_w, norm_ffn_w, attn_w_qkv, attn_q_norm,
                attn_k_norm, attn_w_o, hgrn_w_in, hgrn_lb_logits, hgrn_out_norm, hgrn_w_o,
                ffn_w_in, ffn_w_out, final_norm_w):
    f = lambda a: np.ascontiguousarray(np.asarray(a, dtype=np.float32))
    x, c, ctx, c_ctx = f(x), f(c), f(ctx), f(c_ctx)
    idx0 = np.arange(HALF)
    idx1 = SEQ - 1 - np.arange(HALF)
    rope = [_rope_tables(idx0), _rope_tables(idx1)]
    pim = np.zeros((128, 128), np.float32)
    for m in range(128):
        d = m % 64
        pi = d + 16 if (d % 32) < 16 else d - 16
        pim[(m // 64) * 64 + pi, m] = 1.0
    wqkv = f(attn_w_qkv)[0]
    qcols = np.concatenate([np.arange(h * 64, (h + 1) * 64) for h in HEAD_ORDER])
    wqkv_dev = np.ascontiguousarray(np.concatenate([wqkv[:, qcols], wqkv[:, 1024:]], axis=1))
    wo = f(attn_w_o)[0]
    wo_dev = np.ascontiguousarray(np.stack([wo[h * 64:(h + 1) * 64, :] for h in HEAD_ORDER], axis=1))
    qkn = np.ascontiguousarray(np.stack([np.tile(f(attn_q_norm)[0], 2), np.tile(f(attn_k_norm)[0], 2)], axis=1))
    hwin = f(hgrn_w_in)[0]
    hwin_sw = np.ascontiguousarray(np.concatenate([hwin[:, :2 * D], hwin[:, 3 * D:4 * D], hwin[:, 2 * D:3 * D], hwin[:, 4 * D:]], axis=1))
    adab = np.ascontiguousarray(np.stack([f(ada_b)[l].reshape(48, 128).T for l in range(2)], axis=1))
    nmw = np.ascontiguousarray(np.stack([_fm(f(norm_mix_w)[l]) for l in range(2)], axis=1))
    nfw = np.ascontiguousarray(np.stack([_fm(f(norm_ffn_w)[l]) for l in range(2)], axis=1))
    lbl = np.ascontiguousarray(np.stack([_fm(f(hgrn_lb_logits)[l]) for l in range(2)], axis=1))
    common = {
        "pimat": pim, "ada_w": f(ada_w), "ada_b": adab, "nmw": nmw, "nfw": nfw, "fnw": _fm(f(final_norm_w)),
        "wqkv": wqkv_dev, "qkn": qkn, "wo": wo_dev, "lbl": lbl, "onw": _fm(f(hgrn_out_norm)[0]),
        "hwo": f(hgrn_w_o)[0], "fwin": f(ffn_w_in), "fwout": f(ffn_w_out),
    }
    in_maps = []
    for core in range(2 * NB):
        b, s = core // 2, core % 2
        own = idx0 if s == 0 else idx1
        par = idx1 if s == 0 else idx0
        m = dict(common)
        m["xo"] = np.ascontiguousarray(x[b][own])
        m["cx"] = np.ascontiguousarray(ctx[b] if s == 0 else ctx[b][::-1])
        m["cvec"] = np.ascontiguousarray(np.stack([_fm(c[b]), _fm(c_ctx)], axis=2))
        m["rope_o"] = rope[s]
        m["hwin"] = hwin if s == 0 else hwin_sw
        m["parw"] = np.ascontiguousarray(np.tile(np.array([[0.0, 1.0]] if s == 0 else [[1.0, 0.0]], np.float32), (128, 1)))
        in_maps.append(m)

    def gather(ys):
        out = np.empty((NB, SEQ, D), np.float32)
        for core in range(2 * NB):
            b, s = core // 2, core % 2
            own = idx0 if s == 0 else idx1
            out[b][own] = ys[core]
        return out

    return in_maps, gather
```
